# Optimizing a Trainium2 kernel written in Bass

```python
import math
import jax, jax.numpy as jnp
from jax import lax
import numpy as np

D_MODEL = 1024
BATCH = 16
SEQ = 256
DEPTH = 4
DEC_BATCH = 4
DEC_SEQ = 1024
PAST_LEN = 256

GRID_W = 64
N_MIXERS = 3
N_GMLP_LAYERS = (DEPTH + 2) // 3
N_ATTN_LAYERS = (DEPTH + 1) // 3
N_SSM_LAYERS = DEPTH // 3
N_MOD = 9
D_FF = 2816
EPS = 1e-6
GMLP_HALF = 3 * D_MODEL
GMLP_GROUPS = 8
GMLP_GROUP_DIM = GMLP_HALF // GMLP_GROUPS
CHUNK = 128
HEAD_DIM = 64
N_Q_HEADS = D_MODEL // HEAD_DIM
N_KV_HEADS = 4
Q_PER_KV = N_Q_HEADS // N_KV_HEADS
Q_DIM = N_Q_HEADS * HEAD_DIM
KV_DIM = N_KV_HEADS * HEAD_DIM
WINDOW = 128
ATTN_BLOCK = 128
ATTN_SCALE = HEAD_DIM ** -0.5
ROPE_BASE = 10000.0
ROT_PAIRS = HEAD_DIM // 4
NEG_INF = -1e30
SSM_INNER = 2 * D_MODEL
SSM_HEAD_DIM = 64
SSM_HEADS = SSM_INNER // SSM_HEAD_DIM
SSM_GROUPS = 4
SSM_STATE = 128
SSM_CONV = 3
SSM_CHUNK = 128
SSM_GN = SSM_GROUPS * SSM_STATE
SSM_CONV_DIM = SSM_INNER + 2 * SSM_GN
SSM_IN_DIM = SSM_INNER + SSM_CONV_DIM + 2 * SSM_HEADS

kernel_name = 'hybrid_diffusion_prefix_trunk_step'


def rms_norm(x, g):
    xf = x.astype(jnp.float32)
    y = xf * lax.rsqrt(jnp.mean(xf * xf, axis=-1, keepdims=True) + EPS)
    return (y * g.astype(jnp.float32)).astype(x.dtype)


def layer_norm(x, g, b):
    xf = x.astype(jnp.float32)
    mu = jnp.mean(xf, axis=-1, keepdims=True)
    xc = xf - mu
    var = jnp.mean(xc * xc, axis=-1, keepdims=True)
    return (xc * lax.rsqrt(var + EPS) * g.astype(jnp.float32) + b.astype(jnp.float32)).astype(x.dtype)


def modulation(cond, w, b):
    return (jax.nn.silu(cond) @ w + b).reshape(cond.shape[0], N_MOD, D_MODEL)


def adaln(x, g, shift, scale):
    return rms_norm(x, g) * (1 + scale[:, None]) + shift[:, None]


def swiglu(h, w_in, w_out):
    gu = h @ w_in
    return (jax.nn.silu(gu[..., :D_FF]) * gu[..., D_FF:]) @ w_out


def chunk_mlp(h, w_in, ln_g, ln_b, w_s, b_s, w_out):
    b, n, _ = h.shape
    uv = jax.nn.gelu(h @ w_in, approximate=False)
    u, v = uv[..., :GMLP_HALF], uv[..., GMLP_HALF:]
    v = layer_norm(v, ln_g, ln_b).reshape(b, n // CHUNK, CHUNK, GMLP_GROUPS, GMLP_GROUP_DIM)
    v = jnp.einsum('gij,bcjgd->bcigd', w_s, v) + b_s.T[:, :, None]
    return (u * v.reshape(b, n, GMLP_HALF)) @ w_out


def axial_rope(t):
    n = t.shape[1]
    rows = n // GRID_W
    pos_r = jnp.repeat(jnp.arange(rows, dtype=jnp.float32), GRID_W)
    pos_c = (jnp.arange(n) % GRID_W).astype(jnp.float32)
    inv = ROPE_BASE ** (-jnp.arange(ROT_PAIRS, dtype=jnp.float32) / ROT_PAIRS)
    bshape = (1, n) + (1,) * (t.ndim - 3) + (ROT_PAIRS,)
    tf = t.astype(jnp.float32)

    def rot(x, pos):
        ang = (pos[:, None] * inv).reshape(bshape)
        cos, sin = jnp.cos(ang), jnp.sin(ang)
        x1, x2 = x[..., :ROT_PAIRS], x[..., ROT_PAIRS:]
        return jnp.concatenate([x1 * cos - x2 * sin, x2 * cos + x1 * sin], axis=-1)

    half = HEAD_DIM // 2
    return jnp.concatenate([rot(tf[..., :half], pos_r), rot(tf[..., half:], pos_c)], axis=-1).astype(t.dtype)


def project_qkv(h, w_qkv):
    b, n, _ = h.shape
    qkv = h @ w_qkv
    q = qkv[..., :Q_DIM].reshape(b, n, N_KV_HEADS, Q_PER_KV, HEAD_DIM)
    k = qkv[..., Q_DIM:Q_DIM + KV_DIM].reshape(b, n, N_KV_HEADS, HEAD_DIM)
    v = qkv[..., Q_DIM + KV_DIM:].reshape(b, n, N_KV_HEADS, HEAD_DIM)
    return q, k, v


def sink_probs(logits, sinks):
    sink = jnp.broadcast_to(sinks.astype(jnp.float32)[None, :, :, None, None], logits.shape[:-1] + (1,))
    return jax.nn.softmax(jnp.concatenate([logits, sink], axis=-1), axis=-1)[..., :-1]


def attn_context(h, w_qkv, sinks, w_out):
    b, s, _ = h.shape
    q, k, v = project_qkv(h, w_qkv)
    nb = s // ATTN_BLOCK
    qb = jnp.moveaxis(q.reshape(b, nb, ATTN_BLOCK, N_KV_HEADS, Q_PER_KV, HEAD_DIM), 1, 0)
    sk = sinks.reshape(N_KV_HEADS, Q_PER_KV)

    def block(qi):
        logits = jnp.einsum('bqhgd,bkhd->bhgqk', qi, k).astype(jnp.float32) * ATTN_SCALE
        p = sink_probs(logits, sk).astype(v.dtype)
        return jnp.einsum('bhgqk,bkhd->bqhgd', p, v)

    o = jnp.moveaxis(lax.map(block, qb), 0, 1).reshape(b, s, Q_DIM)
    return o @ w_out, k, v


def attn_latent(h, w_qkv, sinks, w_out, ck, cv):
    b, n, _ = h.shape
    q, k, v = project_qkv(h, w_qkv)
    q, k = axial_rope(q), axial_rope(k)
    nb = n // ATTN_BLOCK
    pad = ((0, 0), (ATTN_BLOCK, ATTN_BLOCK), (0, 0), (0, 0))
    kp, vp = jnp.pad(k, pad), jnp.pad(v, pad)
    qb = jnp.moveaxis(q.reshape(b, nb, ATTN_BLOCK, N_KV_HEADS, Q_PER_KV, HEAD_DIM), 1, 0)
    sk = sinks.reshape(N_KV_HEADS, Q_PER_KV)
    span = 3 * ATTN_BLOCK

    def block(args):
        i, qi = args
        start = i * ATTN_BLOCK
        kb = lax.dynamic_slice_in_dim(kp, start, span, axis=1)
        vb = lax.dynamic_slice_in_dim(vp, start, span, axis=1)
        qpos = start + jnp.arange(ATTN_BLOCK)
        kpos = start - ATTN_BLOCK + jnp.arange(span)
        valid = (kpos >= 0) & (kpos < n) & (jnp.abs(qpos[:, None] - kpos[None, :]) <= WINDOW)
        s_loc = jnp.einsum('bqhgd,bkhd->bhgqk', qi, kb).astype(jnp.float32) * ATTN_SCALE
        s_loc = jnp.where(valid, s_loc, NEG_INF)
        s_ctx = jnp.einsum('bqhgd,bkhd->bhgqk', qi, ck).astype(jnp.float32) * ATTN_SCALE
        p = sink_probs(jnp.concatenate([s_loc, s_ctx], axis=-1), sk).astype(vb.dtype)
        return (jnp.einsum('bhgqk,bkhd->bqhgd', p[..., :span], vb)
                + jnp.einsum('bhgqk,bkhd->bqhgd', p[..., span:], cv))

    o = lax.map(block, (jnp.arange(nb), qb))
    o = jnp.moveaxis(o, 0, 1).reshape(b, n, Q_DIM)
    return o @ w_out


def dwconv(x, w, b):
    ch = x.shape[-1]
    padw = SSM_CONV // 2
    y = lax.conv_general_dilated(x, w[:, None, :], window_strides=(1,), padding=[(padw, padw)],
                                 dimension_numbers=('NWC', 'WIO', 'NWC'), feature_group_count=ch)
    return y + b


def ssd_scan(x, dt, a, bm, cm, h0):
    b, n, h, p = x.shape
    g, s = bm.shape[2], bm.shape[3]
    hg, L = h // g, SSM_CHUNK
    c = n // L
    f32 = jnp.float32
    x = x.astype(f32).reshape(b, c, L, g, hg, p)
    dt = dt.reshape(b, c, L, g, hg)
    bm = bm.astype(f32).reshape(b, c, L, g, s)
    cm = cm.astype(f32).reshape(b, c, L, g, s)
    acum = jnp.cumsum(dt * a.reshape(g, hg), axis=2)
    xdt = x * dt[..., None]
    tril = jnp.tril(jnp.ones((L, L), dtype=bool))[:, :, None, None]
    seg = acum[:, :, :, None] - acum[:, :, None, :]
    decay = jnp.exp(jnp.where(tril, seg, -jnp.inf))
    w = jnp.einsum('bcign,bcjgn->bcijg', cm, bm)[..., None] * decay
    y_diag = jnp.einsum('bcijgh,bcjghp->bcighp', w, xdt)
    to_end = jnp.exp(acum[:, :, -1:] - acum)
    states = jnp.einsum('bcjgn,bcjghp->bcghpn', bm, xdt * to_end[..., None])
    chunk_decay = jnp.exp(acum[:, :, -1])

    def step(hs, inp):
        dec, st = inp
        return dec[..., None, None] * hs + st, hs

    h_final, h_prev = lax.scan(step, h0.astype(f32).reshape(b, g, hg, p, s),
                               (jnp.moveaxis(chunk_decay, 1, 0), jnp.moveaxis(states, 1, 0)))
    h_prev = jnp.moveaxis(h_prev, 0, 1)
    y_off = jnp.einsum('bcign,bcghpn->bcighp', cm, h_prev) * jnp.exp(acum)[..., None]
    return (y_diag + y_off).reshape(b, n, h, p), h_final.reshape(b, h, p, s)


def ssm_mixer(h, w_in, conv_w, conv_b, dt_bias, a_log, d_skip, norm_g, w_out, h0):
    b, n, _ = h.shape
    zxbcdt = h @ w_in
    z = zxbcdt[..., :SSM_INNER]
    xbc = jax.nn.silu(dwconv(zxbcdt[..., SSM_INNER:SSM_INNER + SSM_CONV_DIM], conv_w, conv_b))
    dt = zxbcdt[..., SSM_INNER + SSM_CONV_DIM:].reshape(b, n, 2, SSM_HEADS).astype(jnp.float32)
    dt = jax.nn.softplus(dt + dt_bias.astype(jnp.float32))
    a = -jnp.exp(a_log.astype(jnp.float32))
    xs = xbc[..., :SSM_INNER].reshape(b, n, SSM_HEADS, SSM_HEAD_DIM)
    bm = xbc[..., SSM_INNER:SSM_INNER + SSM_GN].reshape(b, n, SSM_GROUPS, SSM_STATE)
    cm = xbc[..., SSM_INNER + SSM_GN:].reshape(b, n, SSM_GROUPS, SSM_STATE)
    y_f, s_f = ssd_scan(xs, dt[:, :, 0], a[0], bm, cm, h0[:, 0])

    def rev(t):
        return jnp.flip(t, axis=1)

    y_b, s_b = ssd_scan(rev(xs), rev(dt[:, :, 1]), a[1], rev(bm), rev(cm), h0[:, 1])
    y = y_f + rev(y_b) + d_skip.astype(jnp.float32)[:, None] * xs.astype(jnp.float32)
    y = y.reshape(b, n, SSM_INNER) * jax.nn.silu(z.astype(jnp.float32))
    y = rms_norm(y, norm_g).astype(h.dtype)
    return y @ w_out, jnp.stack([s_f, s_b], axis=1)


def trunk(x, cond, W, ctx_k, ctx_v, ctx_state):
    latent = ctx_k is not None
    new_k, new_v, new_s = [], [], []
    for i in range(DEPTH):
        mod = modulation(cond, W['w_mod'][i], W['b_mod'][i])
        g = W['norm_g'][i]
        f = swiglu(adaln(x, g[0], mod[:, 0], mod[:, 1]), W['ffn_in'][i, 0], W['ffn_out'][i, 0])
        x = x + 0.5 * mod[:, 2][:, None] * rms_norm(f, g[1])
        h = adaln(x, g[2], mod[:, 3], mod[:, 4])
        kind, j = i % N_MIXERS, i // N_MIXERS
        if kind == 0:
            m = chunk_mlp(h, W['gmlp_in'][j], W['gmlp_ln_g'][j], W['gmlp_ln_b'][j],
                          W['gmlp_ws'][j], W['gmlp_bs'][j], W['gmlp_out'][j])
        elif kind == 1:
            if latent:
                m = attn_latent(h, W['attn_qkv'][j], W['attn_sink'][j], W['attn_out'][j],
                                ctx_k[:, j], ctx_v[:, j])
            else:
                m, k, v = attn_context(h, W['attn_qkv'][j], W['attn_sink'][j], W['attn_out'][j])
                new_k.append(k)
                new_v.append(v)
        else:
            if latent:
                h0 = ctx_state[:, j]
            else:
                h0 = jnp.zeros((x.shape[0], 2, SSM_HEADS, SSM_HEAD_DIM, SSM_STATE), jnp.float32)
            m, s = ssm_mixer(h, W['ssm_in'][j], W['ssm_conv_w'][j], W['ssm_conv_b'][j],
                             W['ssm_dt_bias'][j], W['ssm_a_log'][j], W['ssm_d'][j],
                             W['ssm_norm'][j], W['ssm_out'][j], h0)
            if not latent:
                new_s.append(s)
        x = x + mod[:, 5][:, None] * rms_norm(m, g[3])
        f = swiglu(adaln(x, g[4], mod[:, 6], mod[:, 7]), W['ffn_in'][i, 1], W['ffn_out'][i, 1])
        x = x + 0.5 * mod[:, 8][:, None] * rms_norm(f, g[5])
    return x, new_k, new_v, new_s


def setup_inputs(seed: int = 0) -> dict:
    key = jax.random.key(seed)
    ks = jax.random.split(key, 32)
    D = D_MODEL

    def nrm(i, shape, scale=1.0):
        return jax.random.normal(ks[i], shape, jnp.float32) * scale

    dt0 = jnp.exp(jax.random.uniform(ks[25], (N_SSM_LAYERS, 2, SSM_HEADS), jnp.float32,
                                     math.log(1e-3), math.log(1e-1)))
    return {
        'x_prompt': nrm(0, (BATCH, SEQ, D)),
        'x_sample': nrm(1, (DEC_BATCH, DEC_SEQ, D)),
        'cache_k': nrm(2, (DEC_BATCH, N_ATTN_LAYERS, PAST_LEN, N_KV_HEADS, HEAD_DIM)),
        'cache_v': nrm(3, (DEC_BATCH, N_ATTN_LAYERS, PAST_LEN, N_KV_HEADS, HEAD_DIM)),
        'state_ssm': nrm(4, (DEC_BATCH, N_SSM_LAYERS, 2, SSM_HEADS, SSM_HEAD_DIM, SSM_STATE), 0.1),
        'c': nrm(5, (DEC_BATCH, D)),
        'c_ctx': nrm(6, (D,)),
        'w_mod': nrm(7, (DEPTH, D, N_MOD * D), D ** -0.5),
        'b_mod': nrm(8, (DEPTH, N_MOD * D), 0.02),
        'norm_g': 1.0 + nrm(9, (DEPTH, 6, D), 0.02),
        'ffn_in': nrm(10, (DEPTH, 2, D, 2 * D_FF), D ** -0.5),
        'ffn_out': nrm(11, (DEPTH, 2, D_FF, D), D_FF ** -0.5),
        'gmlp_in': nrm(12, (N_GMLP_LAYERS, D, 2 * GMLP_HALF), D ** -0.5),
        'gmlp_ln_g': 1.0 + nrm(13, (N_GMLP_LAYERS, GMLP_HALF), 0.02),
        'gmlp_ln_b': nrm(14, (N_GMLP_LAYERS, GMLP_HALF), 0.02),
        'gmlp_ws': nrm(15, (N_GMLP_LAYERS, GMLP_GROUPS, CHUNK, CHUNK), CHUNK ** -0.5),
        'gmlp_bs': 1.0 + nrm(16, (N_GMLP_LAYERS, GMLP_GROUPS, CHUNK), 0.02),
        'gmlp_out': nrm(17, (N_GMLP_LAYERS, GMLP_HALF, D), GMLP_HALF ** -0.5),
        'attn_qkv': nrm(18, (N_ATTN_LAYERS, D, Q_DIM + 2 * KV_DIM), D ** -0.5),
        'attn_sink': nrm(19, (N_ATTN_LAYERS, N_Q_HEADS), 0.5),
        'attn_out': nrm(20, (N_ATTN_LAYERS, Q_DIM, D), Q_DIM ** -0.5),
        'ssm_in': nrm(21, (N_SSM_LAYERS, D, SSM_IN_DIM), D ** -0.5),
        'ssm_conv_w': nrm(22, (N_SSM_LAYERS, SSM_CONV, SSM_CONV_DIM), SSM_CONV ** -0.5),
        'ssm_conv_b': nrm(23, (N_SSM_LAYERS, SSM_CONV_DIM), 0.02),
        'ssm_dt_bias': dt0 + jnp.log(-jnp.expm1(-dt0)),
        'ssm_a_log': jnp.log(jax.random.uniform(ks[26], (N_SSM_LAYERS, 2, SSM_HEADS), jnp.float32, 1.0, 16.0)),
        'ssm_d': 1.0 + nrm(27, (N_SSM_LAYERS, SSM_HEADS), 0.02),
        'ssm_norm': 1.0 + nrm(28, (N_SSM_LAYERS, SSM_INNER), 0.02),
        'ssm_out': nrm(29, (N_SSM_LAYERS, SSM_INNER, D), SSM_INNER ** -0.5),
    }


def reference(x_prompt, x_sample, cache_k, cache_v, state_ssm, c, c_ctx,
              w_mod, b_mod, norm_g, ffn_in, ffn_out,
              gmlp_in, gmlp_ln_g, gmlp_ln_b, gmlp_ws, gmlp_bs, gmlp_out,
              attn_qkv, attn_sink, attn_out,
              ssm_in, ssm_conv_w, ssm_conv_b, ssm_dt_bias, ssm_a_log, ssm_d, ssm_norm, ssm_out):
    W = {
        'w_mod': w_mod, 'b_mod': b_mod, 'norm_g': norm_g, 'ffn_in': ffn_in, 'ffn_out': ffn_out,
        'gmlp_in': gmlp_in, 'gmlp_ln_g': gmlp_ln_g, 'gmlp_ln_b': gmlp_ln_b, 'gmlp_ws': gmlp_ws,
        'gmlp_bs': gmlp_bs, 'gmlp_out': gmlp_out,
        'attn_qkv': attn_qkv, 'attn_sink': attn_sink, 'attn_out': attn_out,
        'ssm_in': ssm_in, 'ssm_conv_w': ssm_conv_w, 'ssm_conv_b': ssm_conv_b,
        'ssm_dt_bias': ssm_dt_bias, 'ssm_a_log': ssm_a_log, 'ssm_d': ssm_d,
        'ssm_norm': ssm_norm, 'ssm_out': ssm_out,
    }
    y_prompt, ks_new, vs_new, ss_new = trunk(x_prompt, c_ctx[None, :], W, None, None, None)
    y_sample, _, _, _ = trunk(x_sample, c, W, cache_k, cache_v, state_ssm)
    new_cache_k = jnp.stack(ks_new, axis=1)
    new_cache_v = jnp.stack(vs_new, axis=1)
    new_state_ssm = jnp.stack(ss_new, axis=1)
    return (y_prompt, y_sample, new_cache_k, new_cache_v, new_state_ssm)
```

```python
import contextlib
import numpy as np
import concourse.bass as bass
import concourse.mybir as mybir

F32 = mybir.dt.float32
BF16 = mybir.dt.bfloat16
AF = mybir.ActivationFunctionType
ALU = mybir.AluOpType
AX = mybir.AxisListType

_ESZ = {F32: 4, BF16: 2, mybir.dt.int32: 4, mybir.dt.uint8: 1, mybir.dt.float16: 2}

RAW_ONLY_SAME_ENGINE = False
ENGS = ("PE", "ACT", "DVE", "POOL", "SP")
COMPUTE = ("PE", "ACT", "DVE", "POOL")


def _region(ap):
    sp = str(ap.space)
    if "SB" not in sp and "PSUM" not in sp:
        return None
    if "PSUM" in sp:
        return (ap.tensor.name, 0, 128, ((0, 2048),))
    esz = _ESZ[ap.dtype]
    apl = ap.ap
    pstride, pcount = apl[0]
    off = int(ap.offset)
    if pstride > 0:
        p0 = off // pstride
        f0 = off % pstride
    else:
        p0, f0 = 0, off
    ivs = [(0, 1)]
    for (stride, count) in reversed(apl[1:]):
        if count == 1 or stride == 0:
            continue
        if len(ivs) == 1 and stride <= (ivs[0][1] - ivs[0][0]):
            ivs = [(ivs[0][0], ivs[0][0] + stride * (count - 1) + (ivs[0][1] - ivs[0][0]))]
        elif len(ivs) * count <= 64:
            ivs = [(a + i * stride, b + i * stride) for i in range(count) for (a, b) in ivs]
        else:
            lo = min(a for a, _ in ivs)
            hi = max(b for _, b in ivs) + stride * (count - 1)
            ivs = [(lo, hi)]
    ivs = tuple(sorted(((f0 + a) * esz, (f0 + b) * esz) for a, b in ivs))
    return (ap.tensor.name, p0, p0 + pcount, ivs)


def _ovl(r1, r2):
    if r1[1] >= r2[2] or r2[1] >= r1[2]:
        return False
    a, b = r1[3], r2[3]
    if a[0][0] >= b[-1][1] or b[0][0] >= a[-1][1]:
        return False
    for (x0, x1) in a:
        for (y0, y1) in b:
            if x0 < y1 and y0 < x1:
                return True
    return False


def _covers(w, e):
    if w[1] > e[1] or w[2] < e[2]:
        return False
    for (y0, y1) in e[3]:
        ok = False
        for (x0, x1) in w[3]:
            if x0 <= y0 and y1 <= x1:
                ok = True
                break
        if not ok:
            return False
    return True


class Op:
    __slots__ = ("id", "eng", "fn", "deps", "raw", "idx", "is_dma", "waits", "inc", "vc", "dsem", "dval", "waited", "key")

    def __init__(self, id, eng, fn, is_dma):
        self.id = id
        self.eng = eng
        self.fn = fn
        self.deps = set()
        self.raw = set()
        self.is_dma = is_dma
        self.waits = []
        self.inc = None
        self.waited = False
        self.dsem = None
        self.dval = None
        self.key = float(id)


class MK:
    NDSEM = 10

    def __init__(self, nc, same_engine_sync=True):
        self.nc = nc
        self.stack = contextlib.ExitStack()
        self.ops = []
        self.hist = {}
        self.same_sync = same_engine_sync
        self.out_dmas = []
        self.psum_banks = []
        self.psum_i = 0
        self._names = set()

    def sb(self, name, shape, dtype):
        return self.stack.enter_context(self.nc.sbuf_tensor(name, list(shape), dtype))

    def alloc_psum(self):
        for i in range(8):
            self.psum_banks.append(self.stack.enter_context(self.nc.psum_tensor(f"psb{i}", [128, 512], F32)))

    def ps(self):
        b = self.psum_banks[self.psum_i % 8]
        self.psum_i += 1
        return b

    def op(self, eng, fn, reads, writes, is_dma=False):
        o = Op(len(self.ops), eng, fn, is_dma)
        self.ops.append(o)
        for ap in reads:
            if ap is None or isinstance(ap, (int, float)):
                continue
            r = _region(ap)
            if r is None:
                continue
            if r[0].startswith("psb"):
                h = self.hist.get(r[0], [])
                for e in h:
                    if e[2] != o.id:
                        o.deps.add(e[2])
                        o.raw.add(e[2])
                self.hist[r[0]] = [(r, True, o.id)]
                continue
            h = self.hist.setdefault(r[0], [])
            for e in h:
                if e[1] and _ovl(r, e[0]):
                    o.deps.add(e[2])
                    o.raw.add(e[2])
            if not is_dma:
                for k, e in enumerate(h):
                    if (not e[1]) and e[0] == r and self.ops[e[2]].eng == eng and not self.ops[e[2]].is_dma:
                        h[k] = (r, False, o.id)
                        break
                else:
                    h.append((r, False, o.id))
            else:
                h.append((r, False, o.id))
        for ap in writes:
            r = _region(ap)
            if r is None:
                continue
            h = self.hist.setdefault(r[0], [])
            keep = []
            for e in h:
                if _ovl(r, e[0]):
                    if e[2] != o.id:
                        o.deps.add(e[2])
                    if _covers(r, e[0]):
                        continue
                keep.append(e)
            keep.append((r, True, o.id))
            self.hist[r[0]] = keep
        o.deps.discard(o.id)
        return o

    def mm(self, out, lhsT, rhs, start=True, stop=True, **kw):
        return self.op("PE", lambda e: e.matmul(out, lhsT, rhs, start=start, stop=stop, **kw), [lhsT, rhs], [out])

    def transpose(self, out, in_, ident):
        return self.op("PE", lambda e: e.transpose(out, in_, ident), [in_, ident], [out])

    def act(self, out, in_, func, bias=0.0, scale=1.0, accum_out=None):
        rd = [in_]
        if not isinstance(bias, (int, float)):
            rd.append(bias)
        if not isinstance(scale, (int, float)):
            rd.append(scale)
        wr = [out] + ([accum_out] if accum_out is not None else [])
        kw = {}
        if accum_out is not None:
            kw["accum_out"] = accum_out
        return self.op("ACT", lambda e: e.activation(out, in_, func, bias=bias, scale=scale, **kw), rd, wr)

    def tt(self, eng, out, in0, in1, op):
        return self.op(eng, lambda e: e.tensor_tensor(out, in0, in1, op), [in0, in1], [out])

    def ts(self, eng, out, in0, s1, s2, op0, op1=None, accum_out=None):
        rd = [in0] + [s for s in (s1, s2) if s is not None and not isinstance(s, (int, float))]
        wr = [out] + ([accum_out] if accum_out is not None else [])
        if op1 is None:
            return self.op(eng, lambda e: e.tensor_scalar(out, in0, s1, None, op0), rd, wr)
        kw = {}
        if accum_out is not None:
            kw["accum_out"] = accum_out
        return self.op(eng, lambda e: e.tensor_scalar(out, in0, s1, s2, op0, op1, **kw), rd, wr)

    def stt(self, eng, out, in0, scalar, in1, op0, op1):
        rd = [in0, in1] + ([scalar] if not isinstance(scalar, (int, float)) else [])
        return self.op(eng, lambda e: e.scalar_tensor_tensor(out, in0, scalar, in1, op0, op1), rd, [out])

    def copy(self, eng, out, in_):
        if eng == "ACT":
            return self.op(eng, lambda e: e.copy(out, in_), [in_], [out])
        return self.op(eng, lambda e: e.tensor_copy(out, in_), [in_], [out])

    def memset(self, eng, out, val):
        return self.op(eng, lambda e: e.memset(out, val), [], [out])

    def recip(self, out, in_):
        return self.op("DVE", lambda e: e.reciprocal(out, in_), [in_], [out])

    def dma(self, q, out, in_, is_output=False, hoist=False, **kw):
        o = self.op(q, lambda e: e.dma_start(out, in_, **kw), [in_], [out], is_dma=True)
        if is_output:
            self.out_dmas.append(o)
        if hoist:
            o.key = (max(self.ops[d].key for d in o.deps) if o.deps else -1.0) + 0.5
        return o

    def finish(self):
        nc = self.nc
        byid = self.ops
        ops = sorted(self.ops, key=lambda o: (o.key, o.id))
        streams = {e: [] for e in ENGS}
        for o in ops:
            o.idx = len(streams[o.eng])
            streams[o.eng].append(o)
        dma_sem_last = {}
        dma_count = {}
        dq_n = {e: 0 for e in ENGS}
        for o in ops:
            if o.is_dma:
                k = (o.eng, dq_n[o.eng] % self.NDSEM)
                dq_n[o.eng] += 1
                if k in dma_sem_last:
                    o.deps.add(dma_sem_last[k].id)
                dma_sem_last[k] = o
                dma_count[k] = dma_count.get(k, 0) + 1
                o.dsem = k
                o.dval = 16 * dma_count[k]
        known = {e: {c: -1 for c in COMPUTE} for e in ENGS}
        known_dma = {e: set() for e in ENGS}
        for o in ops:
            kn = known[o.eng]
            kd = known_dma[o.eng]
            need = []
            for did in sorted(o.deps):
                d = byid[did]
                if d.is_dma:
                    if did in kd:
                        continue
                    need.append(d)
                    kd.add(did)
                    for c in COMPUTE:
                        if d.vc[c] > kn[c]:
                            kn[c] = d.vc[c]
                else:
                    if d.eng == o.eng:
                        if o.eng == "PE" or o.eng == "SP" or not self.same_sync:
                            continue
                        if RAW_ONLY_SAME_ENGINE and did not in o.raw:
                            continue
                    if kn[d.eng] >= d.idx and d.eng != o.eng:
                        continue
                    if d.eng == o.eng and kn.get("_self", -1) >= d.idx:
                        continue
                    need.append(d)
                    if d.eng == o.eng:
                        kn["_self"] = d.idx
                    for c in COMPUTE:
                        if d.vc[c] > kn[c]:
                            kn[c] = d.vc[c]
                    if d.eng != o.eng and d.idx > kn[d.eng]:
                        kn[d.eng] = d.idx
            o.waits = need
            for d in need:
                d.waited = True
            vc = {c: kn[c] for c in COMPUTE}
            if (not o.is_dma) and o.eng in COMPUTE:
                vc[o.eng] = o.idx
            o.vc = vc
        sems = {}
        for e in COMPUTE:
            sems[e] = self.stack.enter_context(nc.semaphore(f"s_{e}"))
        dsems = {}
        for k in dma_sem_last:
            dsems[k] = self.stack.enter_context(nc.semaphore(f"d_{k[0]}_{k[1]}"))
        cnt = {e: 0 for e in COMPUTE}
        for e in COMPUTE:
            for o in streams[e]:
                if o.is_dma:
                    continue
                if o.waited:
                    cnt[e] += 1
                    o.inc = cnt[e]
        self.stats = {e: len(streams[e]) for e in ENGS}
        self.stats["incs"] = dict(cnt)
        self.stats["waits"] = sum(len(o.waits) for o in ops)

        def emit_stream(eng_obj, ename):
            for o in streams[ename]:
                for d in o.waits:
                    if d.is_dma:
                        eng_obj.wait_ge(dsems[d.dsem], d.dval)
                    else:
                        eng_obj.wait_ge(sems[d.eng], d.inc)
                ins = o.fn(eng_obj)
                if o.is_dma:
                    ins.then_inc(dsems[o.dsem], 16)
                elif o.inc is not None:
                    ins.then_inc(sems[o.eng], 1)
            if ename == "SP":
                for k, last in dma_sem_last.items():
                    eng_obj.wait_ge(dsems[k], last.dval)
                for e in COMPUTE:
                    if cnt[e] > 0:
                        eng_obj.wait_ge(sems[e], cnt[e])

        with nc.Block() as block:
            @block.sync
            def _(e):
                emit_stream(e, "SP")

            @block.tensor
            def _(e):
                emit_stream(e, "PE")

            @block.scalar
            def _(e):
                emit_stream(e, "ACT")

            @block.vector
            def _(e):
                emit_stream(e, "DVE")

            @block.gpsimd
            def _(e):
                emit_stream(e, "POOL")
        self.stack.close()

from concourse.bass_utils import run_bass_kernel_spmd

DEPTH = 4
NTOK = 1024
D_FF = 2816
EPS_V = 1e-6


def _prod(s):
    r = 1
    for x in s:
        r *= int(x)
    return r


INPUT_SHAPES = {
    "wmod": (4, 18, 128, 8, 512),
    "ffin": (4, 2, 11, 128, 8, 512),
    "ffout": (4, 2, 8, 128, 22, 128),
    "gin_u": (2, 6, 128, 8, 512),
    "gin_v": (2, 6, 128, 8, 512),
    "gws": (2, 128, 8, 128),
    "gbs": (2, 128, 1024),
    "glng": (2, 128, 24),
    "glnb": (2, 128, 24),
    "gout": (2, 8, 128, 24, 128),
    "aqkv": (3, 128, 8, 512),
    "asink": (128, 16),
    "aout": (8, 64, 16, 128),
    "sin_x": (6, 128, 8, 512),
    "sin_z": (4, 128, 8, 512),
    "sin_dt": (128, 8, 64),
    "sconvw": (128, 24, 3),
    "sconvb": (128, 24),
    "sdtb": (128, 64),
    "salog": (128, 64),
    "sd": (128, 32),
    "snorm": (128, 16),
    "sout": (8, 128, 16, 128),
    "bmod": (128, 288),
    "normg": (128, 192),
    "ident": (128, 128),
    "perm": (128, 128),
    "tri": (128, 4, 128),
    "xT": (128, 8, 1024),
    "condT": (128, 8),
    "ckT": (128, 2, 256),
    "cvT": (128, 2, 256),
    "h0T": (128, 2, 2048),
    "ropeC": (128, 1024),
    "ropeS": (128, 1024),
    "amask": (128, 16, 128),
    "flags": (128, 8),
}
OUTPUT_SHAPES = {
    "yT": (128, 8, 1024),
    "kout": (1024, 256),
    "vout": (1024, 256),
    "stT": (128, 8, 2048),
}


def build_program(cfg=None):
    cfg = cfg or {}
    plan = cfg.get("plan") or [(L, w) for L in range(DEPTH) for w in range(3)]
    used_layers = sorted(set(L for L, _ in plan))
    LIDX = {L: k for k, L in enumerate(used_layers)}
    used = {"wmod", "bmod", "normg", "ident", "perm", "tri", "xT", "condT", "flags"}
    for (L, w) in plan:
        if w != 1:
            used |= {"ffin", "ffout"}
        elif L % 3 == 0:
            used |= {"gin_u", "gin_v", "gws", "gbs", "glng", "glnb", "gout"}
        elif L % 3 == 1:
            used |= {"aqkv", "asink", "aout", "ckT", "cvT", "ropeC", "ropeS", "amask"}
        else:
            used |= {"sin_x", "sin_z", "sin_dt", "sconvw", "sconvb", "sdtb", "salog", "sd", "snorm", "sout", "h0T"}
    nc = bass.Bass("TRN2", target_bir_lowering=False)
    D = {}
    for name, shape in INPUT_SHAPES.items():
        if name not in used:
            continue
        shape = list(shape)
        if name in ("wmod", "ffin", "ffout"):
            shape[0] = len(used_layers)
        D[name] = nc.dram_tensor(name, shape, F32, kind="ExternalInput").ap()
    O = {}
    for name, shape in OUTPUT_SHAPES.items():
        O[name] = nc.dram_tensor(name, list(shape), F32, kind="ExternalOutput").ap()
    mk = MK(nc, same_engine_sync=cfg.get("same_sync", True))
    mk.alloc_psum()
    MODBANK = mk.psum_banks.pop()
    _ps_n = [0]
    _ps_pool = [list(range(5))]
    SSB = [mk.psum_banks[5], mk.psum_banks[6]]
    xss = [False]

    def ps():
        pool = _ps_pool[0]
        b = mk.psum_banks[pool[_ps_n[0] % len(pool)]]
        _ps_n[0] += 1
        return b

    X = mk.sb("X", [128, 8, 1024], F32)
    H = mk.sb("H", [128, 8, 1024], BF16)
    ACTR = mk.sb("ACTR", [128, 24, 1024], BF16)
    WR = mk.sb("WR", [128, 4, 4096], BF16)
    FS = mk.sb("FS", [128, 12288], F32)
    SCR = mk.sb("SCR", [128, 6144], F32)
    ONES = mk.sb("ONES", [128, 128], BF16)
    IDB = mk.sb("IDB", [128, 128], BF16)
    MODT = mk.sb("MODT", [128, 2, 72], F32)
    COEFA = mk.sb("COEFA", [128, 2, 3, 8], F32)
    COEFG = mk.sb("COEFG", [128, 2, 3, 8], F32)
    SC = mk.sb("SC", [128, 8], BF16)
    CONDT = mk.sb("CONDT", [128, 8], F32)
    GT = mk.sb("GT", [128, 48], F32)
    DAB = mk.sb("DAB", [128, 8, 64], BF16)
    BMOD = mk.sb("BMOD", [128, 72], F32)
    FLAGS = mk.sb("FLAGS", [128, 8], F32)
    EPS = mk.sb("EPS", [128, 1], F32)
    SMALL = mk.sb("SMALL", [128, 384], F32)
    LW2 = mk.sb("LW2", [128, 1024], BF16)
    TRI = mk.sb("TRI", [128, 4, 128], BF16)
    ONE1 = mk.sb("ONE1", [128, 1], F32)

    F = FS[:, 0:8192].rearrange("p (c n) -> p c n", n=1024)
    FSb = FS[:].bitcast(BF16)
    SCRb = SCR[:].bitcast(BF16)
    RSTD = SCR[:, 0:1024]
    RSTD2 = SCR[:, 1024:2048]
    TMP = [SCR[:, 2048:3072], SCR[:, 3072:4096]]
    SQ = [SCRb[:, 8192:9216], SCRb[:, 9216:10240]]
    TG = [SCR[:, 5120:5632], SCR[:, 5632:6144]]

    def halves(ap2):
        return [ap2[:, 0:512], ap2[:, 512:1024]]

    class Ring:
        def __init__(self):
            self.n = 0

        def slot(self):
            s = WR[:, self.n % 4, :]
            self.n += 1
            return s

        def load(self, dram_ap):
            s = self.slot()
            shp = dram_ap.shape
            npart = shp[0]
            sz = _prod(shp[1:])
            v = s[0:npart, 0:sz].rearrange("p (k n) -> p k n", n=shp[-1])
            mk.dma("POOL", v, dram_ap, hoist=True)
            return v

    ring = Ring()

    for dc in range(8):
        mk.dma("SP", X[:, dc, :], D["xT"][:, dc, :])
    mk.memset("DVE", ONES[:], 1.0)
    mk.memset("DVE", EPS[:], EPS_V)
    mk.dma("POOL", IDB[:], D["ident"])
    mk.dma("POOL", TRI[:], D["tri"])
    mk.memset("DVE", ONE1[:], 1.0)
    mk.dma("SP", CONDT[:], D["condT"])
    mk.dma("SP", FLAGS[:], D["flags"])
    mk.act(SC[:], CONDT[:], AF.Silu)

    mod_done = set()

    def mod_steps(L):
        steps = []
        par = L % 2

        def fin(sub):
            def f():
                if sub == 0:
                    mk.dma("SP", BMOD[:], D["bmod"][:, L * 72:(L + 1) * 72])
                    mk.dma("SP", GT[:], D["normg"][:, L * 48:(L + 1) * 48])
                cs = slice(24 * sub, 24 * sub + 24)
                mk.tt("DVE", MODT[:, par, cs], MODBANK[:, cs], BMOD[:, cs], ALU.add)
                g_pre = GT[:, (2 * sub) * 8:(2 * sub) * 8 + 8]
                g_post = GT[:, (2 * sub + 1) * 8:(2 * sub + 1) * 8 + 8]
                sc_ = MODT[:, par, (3 * sub + 1) * 8:(3 * sub + 1) * 8 + 8]
                gate = MODT[:, par, (3 * sub + 2) * 8:(3 * sub + 2) * 8 + 8]
                mk.stt("DVE", COEFA[:, par, sub, :], sc_, 1.0, g_pre, ALU.add, ALU.mult)
                mk.stt("DVE", COEFG[:, par, sub, :], gate, (1.0 if sub == 1 else 0.5), g_post, ALU.mult, ALU.mult)
                mod_done.add((L, sub))
            return f

        for pn in range(18):
            def step(pn=pn):
                W = ring.load(D["wmod"][LIDX[L], pn])
                for j in range(4):
                    m = pn * 4 + j
                    for kc in range(8):
                        mk.mm(MODBANK[:, m:m + 1], W[:, kc, j * 128:(j + 1) * 128], SC[:, kc:kc + 1],
                              start=(kc == 0), stop=(kc == 7))
            steps.append(step)
            if pn % 6 == 5:
                steps.append(fin(pn // 6))
        return steps

    bg = []
    LB_ENG = cfg.get('lb_eng', 'POOL')

    def bg_step():
        if bg:
            bg.pop(0)()

    SQ4 = [SQ[0][:, 0:512], SQ[0][:, 512:1024], SQ[1][:, 0:512], SQ[1][:, 512:1024]]
    _sq_n = [0]

    _ss_pending = []

    def ss_flush():
        while _ss_pending:
            _ss_pending.pop(0)()

    def ss_add(src_half, th, first, last, defer=False):
        sq = SQ4[_sq_n[0] % 4]
        _sq_n[0] += 1
        mk.act(sq, src_half, AF.Square)

        def mmop():
            mk.mm(SSB[th][:], ONES[:], sq, start=first, stop=last)
        if defer:
            _ss_pending.append(mmop)
        else:
            mmop()

    def ss_finish(dst):
        for th in range(2):
            d = dst[:, th * 512:(th + 1) * 512]
            mk.act(d, SSB[th][:], AF.Sqrt, bias=EPS[:], scale=1.0 / 1024.0)
            mk.recip(d, d)

    def sumsq_rstd(src3, dst):
        for dc in range(8):
            for th in range(2):
                ss_add(src3[:, dc, th * 512:(th + 1) * 512], th, dc == 0, dc == 7)
        ss_finish(dst)

    def adaln(L, s):
        par = L % 2
        while (L, s) not in mod_done:
            bg.pop(0)()
        if xss[0]:
            ss_finish(RSTD)
            xss[0] = False
        else:
            sumsq_rstd(X, RSTD)
        for dc in range(8):
            t = TMP[dc % 2]
            mk.stt("DVE", t, X[:, dc, :], COEFA[:, par, s, dc:dc + 1], RSTD, ALU.mult, ALU.mult)
            mk.act(H[:, dc, :], t, AF.Identity, bias=MODT[:, par, 3 * s * 8 + dc:3 * s * 8 + dc + 1], scale=1.0)

    def evac_f(Fdst, dc, th, p):
        ths = slice(th * 512, (th + 1) * 512)
        ss_flush()
        mk.act(Fdst[:, dc, ths], p[:], AF.Copy)
        ss_add(p[:], th, dc == 0, dc == 7, defer=True)

    def postnorm(L, s, F=F):
        par = L % 2
        ss_flush()
        ss_finish(RSTD2)
        for dc in range(8):
            t = TMP[dc % 2]
            mk.stt("DVE", t, F[:, dc, :], COEFG[:, par, s, dc:dc + 1], RSTD2, ALU.mult, ALU.mult)
            mk.tt("DVE", X[:, dc, :], X[:, dc, :], t, ALU.add)
            for th in range(2):
                ss_add(X[:, dc, th * 512:(th + 1) * 512], th, dc == 0, dc == 7)
        xss[0] = True

    def ffn(L, s2):
        s = 0 if s2 == 0 else 2
        adaln(L, s)
        _ps_pool[0] = list(range(7))
        for pj in range(11):
            W = ring.load(D["ffin"][LIDX[L], s2, pj])
            for c in range(2):
                for th in range(2):
                    ths = slice(th * 512, (th + 1) * 512)
                    pg, pu = ps(), ps()
                    for kc in range(8):
                        mk.mm(pg[:], W[:, kc, c * 128:(c + 1) * 128], H[:, kc, ths], start=(kc == 0), stop=(kc == 7))
                    for kc in range(8):
                        mk.mm(pu[:], W[:, kc, 256 + c * 128:256 + (c + 1) * 128], H[:, kc, ths], start=(kc == 0), stop=(kc == 7))
                    mk.act(TG[th], pg[:], AF.Silu)
                    mk.tt("DVE", ACTR[:, 2 * pj + c, ths], TG[th], pu[:], ALU.mult)
            bg_step()
        _ps_pool[0] = list(range(5))
        for dc in range(8):
            W = ring.load(D["ffout"][LIDX[L], s2, dc])
            for th in range(2):
                ths = slice(th * 512, (th + 1) * 512)
                p = ps()
                for kc in range(22):
                    mk.mm(p[:], W[:, kc, :], ACTR[:, kc, ths], start=(kc == 0), stop=(kc == 21))
                evac_f(F, dc, th, p)
            bg_step()
        postnorm(L, s)

    def gmlp(L, j):
        adaln(L, 1)
        WV = FSb.rearrange("p (a k n) -> p a k n", a=6, k=8)
        for vp in range(6):
            mk.dma("POOL", WV[:, vp], D["gin_v"][j, vp], hoist=True)
        gstop = cfg.get("gstop", 9)
        if gstop <= 1:
            return
        for up in range(6):
            W = ring.load(D["gin_u"][j, up])
            for c in range(4):
                for th in range(2):
                    ths = slice(th * 512, (th + 1) * 512)
                    p = ps()
                    for kc in range(8):
                        mk.mm(p[:], W[:, kc, c * 128:(c + 1) * 128], H[:, kc, ths], start=(kc == 0), stop=(kc == 7))
                    mk.act(ACTR[:, up * 4 + c, ths], p[:], AF.Gelu)
            bg_step()
        if ring.n % 4 == 3:
            ring.slot()
        s0 = ring.slot()
        ring.slot()
        sl = (ring.n - 2) % 4
        T1 = WR[:, sl:sl + 2, :].rearrange("p s n -> p (s n)").bitcast(F32)[:, 0:3072].rearrange("p (c i) -> p c i", i=128)
        BSB = SCR[:, 0:1024].rearrange("p (g i) -> p g i", i=128)
        WST = SCRb[:, 11264:12288].rearrange("p (g i) -> p g i", i=128)
        LNG = SMALL[:, 0:24]
        LNB = SMALL[:, 24:48]
        mk.dma("POOL", WST, D["gws"][j])
        mk.dma("SP", BSB, D["gbs"][j].rearrange("p (g i) -> p g i", i=128))
        mk.dma("SP", LNG, D["glng"][j])
        mk.dma("SP", LNB, D["glnb"][j])
        for hh in range(2):
            p = ps()
            for g4 in range(4):
                g = hh * 4 + g4
                mk.mm(p[:, g4 * 128:(g4 + 1) * 128], ONES[:], WST[:, g, :], start=True, stop=True)
            for g4 in range(4):
                g = hh * 4 + g4
                for c3 in range(3):
                    fc = g * 3 + c3
                    mk.stt("DVE", T1[:, fc, :], p[:, g4 * 128:(g4 + 1) * 128], LNB[:, fc:fc + 1], BSB[:, g, :], ALU.mult, ALU.add)
        if gstop <= 2:
            return
        VT = SCR[:, 0:3072]
        VN = SCRb[:, 6144:9216]
        SVT = SCR[:, 4608:5120].rearrange("p (k i) -> p k i", i=128)
        ST = SMALL[:, 64:80]
        def v_mm(tc, vp, bank):
            tcs = slice(tc * 128, (tc + 1) * 128)
            for kc in range(8):
                mk.mm(bank[:], H[:, kc, tcs], WV[:, vp, kc, :], start=(kc == 0), stop=(kc == 7))

        def v_evac(vp, bank):
            mk.act(VT[:, vp * 512:(vp + 1) * 512], bank[:], AF.Gelu, accum_out=ST[:, 8 + vp:9 + vp])

        _rot = [0]

        def rot3():
            b = mk.psum_banks[4 + _rot[0] % 3]
            _rot[0] += 1
            return b

        SE = cfg.get("stat_eng", "POOL")

        def v_chunk_early(tc):
            for vp in range(4):
                v_mm(tc, vp, mk.psum_banks[vp])

        def v_chunk_late(tc):
            for vp in range(4):
                v_evac(vp, mk.psum_banks[vp])
            for vp in (4, 5):
                v_mm(tc, vp, mk.psum_banks[vp - 4])
                v_evac(vp, mk.psum_banks[vp - 4])

        def rot2():
            return rot3()

        v_chunk_early(0)
        v_chunk_late(0)
        for tc in range(8):
            tcs = slice(tc * 128, (tc + 1) * 128)
            STX = SMALL[:, 320:328]
            mk.act(STX[:, 0:6], ST[:, 8:14], AF.Copy, accum_out=ST[:, 0:1])
            mk.act(VN, VT, AF.Square, accum_out=ST[:, 2:3])
            mk.act(ST[:, 5:6], ST[:, 0:1], AF.Square, scale=1.0 / 3072.0)
            mk.act(ST[:, 5:6], ST[:, 5:6], AF.Copy, scale=-1.0)
            mk.act(ST[:, 6:7], ST[:, 2:3], AF.Identity, bias=ST[:, 5:6], scale=1.0 / 3072.0)
            mk.act(ST[:, 3:4], ST[:, 6:7], AF.Ln, bias=EPS[:], scale=1.0)
            mk.act(ST[:, 4:5], ST[:, 3:4], AF.Exp, scale=-0.5)
            mk.act(ST[:, 7:8], ST[:, 0:1], AF.Copy, scale=ST[:, 4:5])
            mk.act(ST[:, 7:8], ST[:, 7:8], AF.Copy, scale=-1.0 / 3072.0)
            mk.act(VN, VT, AF.Identity, bias=ST[:, 7:8], scale=ST[:, 4:5])
            if tc + 1 < 8:
                v_chunk_early(tc + 1)
            for f4 in range(6):
                p = rot2()
                for k in range(4):
                    fc = f4 * 4 + k
                    mk.mm(p[:, k * 128:(k + 1) * 128], VN[:, fc * 128:(fc + 1) * 128], WST[:, fc // 3, :], start=True, stop=True)
                for k in range(4):
                    fc = f4 * 4 + k
                    mk.stt("DVE", SVT[:, k, :], p[:, k * 128:(k + 1) * 128], LNG[:, fc:fc + 1], T1[:, fc, :], ALU.mult, ALU.add)
                u = ACTR[:, f4 * 4:(f4 + 1) * 4, tcs]
                mk.tt("DVE", u, u, SVT, ALU.mult)
            if tc + 1 < 8:
                v_chunk_late(tc + 1)
        if gstop <= 6:
            return
        for dc in range(8):
            W = ring.load(D["gout"][j, dc])
            for th in range(2):
                ths = slice(th * 512, (th + 1) * 512)
                p = ps()
                for kc in range(24):
                    mk.mm(p[:], W[:, kc, :], ACTR[:, kc, ths], start=(kc == 0), stop=(kc == 23))
                evac_f(F, dc, th, p)
            bg_step()
        postnorm(L, 1)

    def attn(L):
        adaln(L, 1)
        QT = ACTR[:, 0:8, :]
        KT = ACTR[:, 8:10, :]
        VTK = ACTR[:, 10:12, :].rearrange("p a (t n) -> p (a t) n", n=256)
        OT = FSb[0:64, 0:16384].rearrange("p (h n) -> p h n", n=1024)
        ROPEC = SCR[:, 0:1024]
        ROPES = SCR[:, 1024:2048]
        AMASK = SCRb[:, 4096:6144].rearrange("p (m n) -> p m n", n=128)
        CKT = SCRb[:, 6144:6656].rearrange("p (c n) -> p c n", n=256)
        CVT = SCRb[:, 6656:7168].rearrange("p (c n) -> p c n", n=256)
        QB = [SCRb[:, 7168:7680], SCRb[:, 7680:8192]]
        RT0 = [SCR[:, 4096:4608], SCR[:, 4608:5120]]
        RT1 = [SCR[:, 5120:5632], SCR[:, 5632:6144]]
        PERM = SMALL[:, 128:192].bitcast(BF16)
        ESB = SMALL[:, 192:208]
        mk.dma("SP", ROPEC, D["ropeC"])
        mk.dma("SP", ROPES, D["ropeS"])
        mk.dma("POOL", AMASK, D["amask"])
        mk.dma("POOL", CKT, D["ckT"])
        mk.dma("POOL", CVT, D["cvT"])
        mk.dma("POOL", PERM, D["perm"])
        mk.dma("SP", ESB, D["asink"])
        mk.act(ESB, ESB, AF.Exp)
        W2 = None

        _rp = []

        def rope_flush():
            while _rp:
                _rp.pop(0)()

        def rope_chunk(dst3, ci, W, c):
            for th in range(2):
                ths = slice(th * 512, (th + 1) * 512)
                p = ps()
                for kc in range(8):
                    mk.mm(p[:], W[:, kc, c * 128:(c + 1) * 128], H[:, kc, ths], start=(kc == 0), stop=(kc == 7))
                rope_flush()
                mk.act(QB[th], p[:], AF.Copy)
                mk.tt("DVE", RT0[th], p[:], ROPEC[:, ths], ALU.mult)

                def later(th=th, ths=ths, ci=ci, dst3=dst3):
                    pr = ps()
                    mk.mm(pr[:], PERM, QB[th], start=True, stop=True)
                    mk.tt("DVE", RT1[th], pr[:], ROPES[:, ths], ALU.mult)
                    mk.tt("DVE", dst3[:, ci, ths], RT0[th], RT1[th], ALU.add)
                _rp.append(later)

        for qp in range(2):
            W = ring.load(D["aqkv"][qp])
            for c in range(4):
                rope_chunk(QT, qp * 4 + c, W, c)
        W2 = ring.load(D["aqkv"][2])
        for c in range(2):
            rope_chunk(KT, c, W2, c)
        rope_flush()
        astop = cfg.get("astop", 9)
        if astop <= 1:
            return
        KVS = [FS[:, 8192:8704], FS[:, 8704:9216]]
        for tc in range(8):
            tcs = slice(tc * 128, (tc + 1) * 128)
            p = ps()
            for kc in range(8):
                mk.mm(p[:], H[:, kc, tcs], W2[:, kc, :], start=(kc == 0), stop=(kc == 7))
            kv = KVS[tc % 2]
            mk.act(kv, p[:], AF.Copy)
            mk.copy("DVE", VTK[:, tc, :], p[:, 256:512])
            mk.dma("SP", O["kout"][tc * 128:(tc + 1) * 128, :], kv[:, 0:256], is_output=True)
            mk.dma("SP", O["vout"][tc * 128:(tc + 1) * 128, :], kv[:, 256:512], is_output=True)
        if astop <= 2:
            return
        EST = FS[0:64, 9216:11264].rearrange("p (h n) -> p h n", n=128)
        mk.copy("DVE", EST, ESB[0:64, :].unsqueeze(2).to_broadcast([64, 16, 128]))
        PTS = [[SCRb[:, 8192 + s * 512:8192 + (s + 1) * 512] for s in range(5)],
               [SCRb[:, s * 512:(s + 1) * 512] for s in range(5)]]
        DEN = FS[0:64, 11264:11776]
        its = [(i, hk) for i in range(8) for hk in range(4)]
        vls = {}

        def stage_s(n):
            i, hk = its[n]
            PT = PTS[n % 2]
            half = hk % 2
            hs = slice(half * 64, half * 64 + 64)
            qc0 = 4 * (hk // 2)
            qmov = QT[hs, qc0:qc0 + 4, i * 128:(i + 1) * 128]
            vlist = []
            for s_ in range(5):
                if s_ < 3:
                    kb = min(max(i - 1 + s_, 0), 7)
                    keysT = KT[hs, hk // 2, kb * 128:(kb + 1) * 128]
                    vlist.append(VTK[:, kb, hk * 64:(hk + 1) * 64])
                else:
                    keysT = CKT[hs, hk // 2, (s_ - 3) * 128:(s_ - 2) * 128]
                    vlist.append(CVT[:, s_ - 3, hk * 64:(hk + 1) * 64])
                p = mk.psum_banks[s_]
                mk.mm(p[:].rearrange("p (h n) -> p h n", n=128), keysT, qmov, start=True, stop=True)
                if s_ < 3:
                    mk.act(PT[s_], p[:], AF.Exp, scale=0.125)
                else:
                    mk.act(PT[s_], p[:], AF.Exp, bias=FLAGS[:, 0:1], scale=0.125)
                if s_ in (0, 2):
                    m = AMASK[:, i * 2 + (0 if s_ == 0 else 1), :]
                    pv = PT[s_].rearrange("p (h n) -> p h n", n=128)
                    mk.tt("DVE", pv, pv, m.unsqueeze(1).to_broadcast([128, 4, 128]), ALU.mult)
            vls[n] = vlist

        def stage_o(n):
            i, hk = its[n]
            PT = PTS[n % 2]
            vlist = vls.pop(n)
            po, pd = mk.psum_banks[5], mk.psum_banks[6]
            for s_ in range(5):
                mk.mm(po[0:64, :], vlist[s_], PT[s_], start=(s_ == 0), stop=(s_ == 4))
            for s_ in range(5):
                mk.mm(pd[0:64, :], ONES[:, 0:64], PT[s_], start=(s_ == 0), stop=(s_ == 4))
            mk.tt("DVE", DEN, pd[0:64, :], EST[:, hk * 4:(hk + 1) * 4, :], ALU.add)
            mk.recip(DEN, DEN)
            mk.tt("DVE", OT[:, hk * 4:(hk + 1) * 4, i * 128:(i + 1) * 128], po[0:64, :].rearrange("p (h n) -> p h n", n=128),
                  DEN.rearrange("p (h n) -> p h n", n=128), ALU.mult)

        stage_s(0)
        for n in range(len(its)):
            if n + 1 < len(its):
                stage_s(n + 1)
            stage_o(n)
        FA = ACTR[:].rearrange("p c n -> p (c n)").bitcast(F32)[:, 0:8192].rearrange("p (c n) -> p c n", n=1024)
        for dc in range(8):
            W = ring.load(D["aout"][dc])
            for th in range(2):
                ths = slice(th * 512, (th + 1) * 512)
                p = ps()
                for h in range(16):
                    mk.mm(p[:], W[:, h, :], OT[:, h, ths], start=(h == 0), stop=(h == 15))
                evac_f(FA, dc, th, p)
            bg_step()
        postnorm(L, 1, FA)

    def ssm(L):
        adaln(L, 1)
        XCT = ACTR
        HB = FSb[:, 0:16384].rearrange("p (t n) -> p t n", n=2048)
        BTK = FSb[:, 16384:20480].rearrange("p (t n) -> p t n", n=512)
        HST = FS[:, 10240:12288]
        CW = SMALL[:, 0:72].rearrange("p (c k) -> p c k", k=3)
        CBI = SMALL[:, 72:96]
        W0P = SMALL[:, 96:120]
        W2P = SMALL[:, 120:144]
        DTB = SMALL[:, 144:208]
        ABC = SMALL[:, 208:272]
        SDB = SMALL[:, 272:304]
        SNORM = SMALL[:, 304:320]
        SSQ = SMALL[:, 320:328]
        mk.dma("SP", CW, D["sconvw"])
        mk.dma("SP", CBI, D["sconvb"])
        mk.dma("SP", DTB, D["sdtb"])
        mk.dma("SP", ABC, D["salog"])
        mk.dma("SP", SDB, D["sd"])
        mk.dma("SP", SNORM, D["snorm"])
        mk.act(ABC, ABC, AF.Exp)
        mk.ts("DVE", ABC, ABC, -1.0, None, ALU.mult)
        mk.ts("DVE", W0P, CW[:, :, 0], FLAGS[:, 3:4], None, ALU.mult)
        mk.ts("DVE", W2P, CW[:, :, 2], FLAGS[:, 3:4], None, ALU.mult)
        RAW = [SCR[:, 0:1026], SCR[:, 1026:2052]]
        ACC = [SCR[:, 2052:3076], SCR[:, 3076:4100]]
        for r in RAW:
            mk.memset("DVE", r[:, 0:1], 0.0)
            mk.memset("DVE", r[:, 1025:1026], 0.0)
        conv_pending = []
        for xp in range(6):
            W = ring.load(D["sin_x"][xp])
            for c in range(4):
                ch = xp * 4 + c
                raw = RAW[ch % 2]
                acc = ACC[ch % 2]
                if len(conv_pending) > 1:
                    pch, pacc = conv_pending.pop(0)
                    mk.act(XCT[:, pch, :], pacc, AF.Silu)
                for th in range(2):
                    ths = slice(th * 512, (th + 1) * 512)
                    p = ps()
                    for kc in range(8):
                        mk.mm(p[:], W[:, kc, c * 128:(c + 1) * 128], H[:, kc, ths], start=(kc == 0), stop=(kc == 7))
                    mk.act(raw[:, 1 + th * 512:1 + (th + 1) * 512], p[:], AF.Copy)
                mk.act(acc, raw[:, 1:1025], AF.Identity, bias=CBI[:, ch:ch + 1], scale=CW[:, ch, 1:2])
                mk.stt("DVE", acc, raw[:, 0:1024], CW[:, ch, 0:1], acc, ALU.mult, ALU.add)
                mk.stt("DVE", acc, raw[:, 2:1026], CW[:, ch, 2:3], acc, ALU.mult, ALU.add)
                a0 = acc[:, 256:1024:256]
                mk.stt("DVE", a0, raw[:, 256:1024:256], W0P[:, ch:ch + 1], a0, ALU.mult, ALU.add)
                a1 = acc[:, 255:1023:256]
                mk.stt("DVE", a1, raw[:, 257:1025:256], W2P[:, ch:ch + 1], a1, ALU.mult, ALU.add)
                conv_pending.append((ch, acc))
            bg_step()
        while conv_pending:
            pch, pacc = conv_pending.pop(0)
            mk.act(XCT[:, pch, :], pacc, AF.Silu)
        Wdt = ring.load(D["sin_dt"])
        DT = SCR[:, 0:512].rearrange("p (t n) -> p t n", n=64)
        DTR = SCR[:, 4100:4612].rearrange("p (t n) -> p t n", n=64)
        ABt = SCR[:, 4612:5124].rearrange("p (t n) -> p t n", n=64)
        DAH = SCRb[:, 10248:10760].rearrange("p (t n) -> p t n", n=64)
        DAL = SCRb[:, 10760:11272].rearrange("p (t n) -> p t n", n=64)
        pdt = ps()
        for tc in range(8):
            tcs = slice(tc * 128, (tc + 1) * 128)
            for kc in range(8):
                mk.mm(pdt[:, tc * 64:(tc + 1) * 64], H[:, kc, tcs], Wdt[:, kc, :], start=(kc == 0), stop=(kc == 7))
        mk.tt("DVE", DTR, pdt[:].rearrange("p (t n) -> p t n", n=64), DTB.unsqueeze(1).to_broadcast([128, 8, 64]), ALU.add)
        mk.act(ABt, DTR, AF.Abs)
        mk.act(ABt, ABt, AF.Exp, scale=-1.0)
        mk.act(ABt, ABt, AF.Ln, bias=ONE1[:], scale=1.0)
        mk.act(DTR, DTR, AF.Relu)
        mk.tt("DVE", DT, DTR, ABt, ALU.add)
        DA = DTR
        mk.tt("DVE", DA, DT, ABC.unsqueeze(1).to_broadcast([128, 8, 64]), ALU.mult)
        mk.copy("DVE", DAH, DA)
        mk.copy("DVE", DAB[:], DA)
        mk.tt("DVE", ABt, DA, DAH, ALU.subtract)
        mk.copy("DVE", DAL, ABt)
        pac, ptot = ps(), ps()
        for tc in range(8):
            for d_ in range(2):
                o = pac[:, tc * 64 + d_ * 32:tc * 64 + (d_ + 1) * 32]
                mk.mm(o, TRI[:, d_, :], DAH[:, tc, d_ * 32:(d_ + 1) * 32], start=True, stop=False)
                mk.mm(o, TRI[:, d_, :], DAL[:, tc, d_ * 32:(d_ + 1) * 32], start=False, stop=True)
            o = ptot[:, tc * 64:(tc + 1) * 64]
            mk.mm(o, ONES[:], DAH[:, tc, :], start=True, stop=False)
            mk.mm(o, ONES[:], DAL[:, tc, :], start=False, stop=True)
        EA = SCR[:, 512:1024].rearrange("p (t n) -> p t n", n=64)
        CD = SCR[:, 1024:1536].rearrange("p (t n) -> p t n", n=64)
        CO = SCR[:, 1536:2048].rearrange("p (t n) -> p t n", n=64)
        mk.copy("DVE", EA, pac[:].rearrange("p (t n) -> p t n", n=64))
        mk.copy("DVE", CD, ptot[:].rearrange("p (t n) -> p t n", n=64))
        mk.tt("DVE", CO, CD, EA, ALU.subtract)
        mk.act(CO, CO, AF.Exp)
        mk.tt("DVE", CO, CO, DT, ALU.mult)
        mk.act(EA, EA, AF.Exp)
        mk.act(CD, CD, AF.Exp)
        XTK = SCRb[:, 4096:6144]
        XDW = SCRb[:, 6144:8192]
        CBM = SCRb[:, 8192:9216].rearrange("p (d g i) -> p d g i", d=2, g=4)
        YA = SCR[:, 4608:5120]
        YB = SCR[:, 5120:5632]
        WTS = [SCRb[:, 11264:11776].rearrange("p (k i) -> p k i", i=128), LW2[:, 512:1024].rearrange("p (k i) -> p k i", i=128)]
        LBS = [SCRb[:, 11776:12288].rearrange("p (k i) -> p k i", i=128), LW2[:, 0:512].rearrange("p (k i) -> p k i", i=128)]
        Wz = [ring.load(D["sin_z"][g]) for g in range(4)]

        def bc_hp(ap2, n=8):
            return ap2.unsqueeze(2).to_broadcast([128, n, 64])

        def make_tok(tc, with_b):
            tcs = slice(tc * 128, (tc + 1) * 128)
            chans = list(range(16)) + (list(range(16, 20)) if with_b else [])
            for q in range(0, len(chans), 4):
                pb = ps()[:].bitcast(BF16)
                for k in range(4):
                    mk.transpose(pb[:, k * 128:(k + 1) * 128], XCT[:, chans[q + k], tcs], IDB[:])
                if q < 16:
                    mk.copy("ACT", XTK[:, q * 128:(q + 4) * 128], pb[:, 0:512])
                else:
                    mk.copy("ACT", BTK[:, tc, :], pb[:, 0:512])

        def state_update(tc, d_):
            mk.tt("DVE", XDW.rearrange("p (h q) -> p h q", q=64), XTK.rearrange("p (h q) -> p h q", q=64),
                  bc_hp(CO[:, tc, d_ * 32:(d_ + 1) * 32], 32), ALU.mult)
            for g in range(4):
                p = ps()
                mk.mm(p[:], BTK[:, tc, g * 128:(g + 1) * 128], XDW[:, g * 512:(g + 1) * 512], start=True, stop=True)
                hs = HST[:, g * 512:(g + 1) * 512].rearrange("p (h q) -> p h q", q=64)
                mk.tt("DVE", hs, hs, bc_hp(CD[:, tc, d_ * 32 + g * 8:d_ * 32 + (g + 1) * 8]), ALU.mult)
                mk.tt("DVE", HST[:, g * 512:(g + 1) * 512], HST[:, g * 512:(g + 1) * 512], p[:], ALU.add)

        mk.dma("SP", HST, D["h0T"][:, 1, :])
        for tc in range(7, -1, -1):
            if tc % 2 == 1 and tc < 7:
                mk.ts("DVE", HST, HST, FLAGS[:, 1:2], None, ALU.mult)
            mk.copy("ACT", HB[:, tc, :], HST)
            make_tok(tc, True)
            state_update(tc, 1)
            if tc % 2 == 0:
                mk.dma("SP", O["stT"][:, (tc // 2) * 2 + 1, :], HST, is_output=True)
        mk.dma("SP", HST, D["h0T"][:, 0, :])
        HFB = XDW
        GY = XDW
        mk.copy("ACT", HFB, HST)
        for tc in range(8):
            tcs = slice(tc * 128, (tc + 1) * 128)
            make_tok(tc, False)
            pcb = ps()
            for g in range(4):
                mk.mm(pcb[:, g * 128:(g + 1) * 128], XCT[:, 16 + g, tcs], XCT[:, 20 + g, tcs], start=True, stop=True)
            for d_ in range(2):
                mk.tt("DVE", CBM[:, d_], pcb[:].rearrange("p (g i) -> p g i", i=128),
                      TRI[:, d_, :].unsqueeze(1).to_broadcast([128, 4, 128]), ALU.mult)
            hf_cur = HFB if tc == 0 else HB[:, tc - 1, :]
            units = [(g, d_, h4) for g in range(4) for d_ in range(2) for h4 in range(2)]
            segs = {}
            pyb = {}

            def stage_a(ui):
                g, d_, h4 = units[ui]
                lbs = LBS[ui % 2]
                pseg = mk.psum_banks[2 + ui % 2]
                segs[ui] = pseg
                dh0 = d_ * 32 + g * 8 + h4 * 4
                mk.tt(LB_ENG, lbs, TRI[:, 2 + d_, :].unsqueeze(1).to_broadcast([128, 4, 128]),
                      DAB[:, tc, dh0:dh0 + 4].unsqueeze(2).to_broadcast([128, 4, 128]), ALU.mult)
                for k in range(4):
                    mk.mm(pseg[:, k * 128:(k + 1) * 128], lbs[:, k, :], TRI[:, d_, :], start=True, stop=True)

            def stage_b(ui):
                pseg = segs[ui]
                mk.act(pseg[:], pseg[:], AF.Exp)

            def stage_c(ui):
                g, d_, h4 = units[ui]
                wts = WTS[ui % 2]
                pseg = segs[ui]
                if g not in pyb:
                    pyb[g] = mk.psum_banks[g % 2]
                py = pyb[g]
                dh0 = d_ * 32 + g * 8 + h4 * 4
                p3 = pseg[:].rearrange("p (k i) -> p k i", i=128)
                mk.tt("DVE", p3, p3, CBM[:, d_, g, :].unsqueeze(1).to_broadcast([128, 4, 128]), ALU.mult)
                mk.tt("DVE", wts, p3, DT[:, tc, dh0:dh0 + 4].unsqueeze(2).to_broadcast([128, 4, 128]), ALU.mult)
                for k in range(4):
                    hh = h4 * 4 + k
                    h = g * 8 + hh
                    mk.mm(py[:, hh * 64:(hh + 1) * 64], wts[:, k, :], XTK[:, h * 64:(h + 1) * 64],
                          start=(d_ == 0 and hh == 0), stop=(d_ == 1 and hh == 7))

            pzb = {}

            pofb, pobb = {}, {}

            def z_early(g):
                pz = mk.psum_banks[5 + g % 2]
                pzb[g] = pz
                for kc in range(8):
                    mk.mm(pz[:], H[:, kc, tcs], Wz[g][:, kc, :], start=(kc == 0), stop=(kc == 7))
                mk.act(pz[:], pz[:], AF.Silu)
                pofb[g] = mk.psum_banks[4]
                pobb[g] = mk.psum_banks[6 - g % 2]
                mk.mm(pofb[g][:], XCT[:, 20 + g, tcs], hf_cur[:, g * 512:(g + 1) * 512], start=True, stop=True)
                mk.mm(pobb[g][:], XCT[:, 20 + g, tcs], HB[:, tc, g * 512:(g + 1) * 512], start=True, stop=True)

            def ycomb(g):
                py = pyb[g]
                ya3 = YA.rearrange("p (h q) -> p h q", q=64)
                yb3 = YB.rearrange("p (h q) -> p h q", q=64)
                pof, pob = pofb[g], pobb[g]
                mk.tt("DVE", ya3, pof[:].rearrange("p (h q) -> p h q", q=64), bc_hp(EA[:, tc, g * 8:(g + 1) * 8]), ALU.mult)
                mk.tt("DVE", yb3, pob[:].rearrange("p (h q) -> p h q", q=64), bc_hp(EA[:, tc, 32 + g * 8:32 + (g + 1) * 8]), ALU.mult)
                mk.tt("DVE", YA, YA, YB, ALU.add)
                mk.tt("DVE", YA, YA, py[:], ALU.add)
                mk.tt("DVE", yb3, XTK[:, g * 512:(g + 1) * 512].rearrange("p (h q) -> p h q", q=64), bc_hp(SDB[:, g * 8:(g + 1) * 8]), ALU.mult)
                mk.tt("DVE", YA, YA, YB, ALU.add)
                mk.tt("DVE", YA, YA, pzb[g][:], ALU.mult)

                def act_part(g=g):
                    mk.act(YB, YA, AF.Square, accum_out=SSQ[:, g:g + 1])
                    mk.copy("ACT", GY[:, g * 512:(g + 1) * 512], YA)
                yc_pending.append(act_part)

            _ps_pool[0] = [5, 6]
            yc_pending = []
            stage_a(0)
            stage_b(0)
            for ui in range(16):
                if ui % 4 == 0:
                    z_early(ui // 4)
                if ui + 1 < 16:
                    stage_a(ui + 1)
                    stage_b(ui + 1)
                while yc_pending:
                    yc_pending.pop(0)()
                stage_c(ui)
                if ui % 4 == 3:
                    ycomb(ui // 4)
            while yc_pending:
                yc_pending.pop(0)()
            _ps_pool[0] = list(range(5))
            mk.op("DVE", lambda e: e.reduce_sum(SSQ[:, 4:5], SSQ[:, 0:4], AX.X), [SSQ[:, 0:4]], [SSQ[:, 4:5]])
            mk.act(SSQ[:, 5:6], SSQ[:, 4:5], AF.Sqrt, bias=EPS[:], scale=1.0 / 2048.0)
            mk.recip(SSQ[:, 6:7], SSQ[:, 5:6])
            mk.ts("DVE", GY, GY, SSQ[:, 6:7], None, ALU.mult)
            for q in range(0, 16, 4):
                pb = ps()[:].bitcast(BF16)
                for k in range(4):
                    mk.transpose(pb[:, k * 128:(k + 1) * 128], GY[:, (q + k) * 128:(q + k + 1) * 128], IDB[:])
                for k in range(4):
                    mk.act(XCT[:, q + k, tcs], pb[:, k * 128:(k + 1) * 128], AF.Identity, scale=SNORM[:, q + k:q + k + 1])
            state_update(tc, 0)
            if tc % 2 == 1:
                mk.dma("SP", O["stT"][:, (tc // 2) * 2, :], HST, is_output=True)
                if tc < 7:
                    mk.ts("DVE", HST, HST, FLAGS[:, 1:2], None, ALU.mult)
            if tc < 7:
                mk.copy("ACT", HB[:, tc, :], HST)
        for dc in range(8):
            W = ring.load(D["sout"][dc])
            for th in range(2):
                ths = slice(th * 512, (th + 1) * 512)
                p = ps()
                for kc in range(16):
                    mk.mm(p[:], W[:, kc, :], XCT[:, kc, ths], start=(kc == 0), stop=(kc == 15))
                evac_f(F, dc, th, p)
            bg_step()
        postnorm(L, 1)

    bg.extend(mod_steps(plan[0][0]))
    for k, (L, which) in enumerate(plan):
        nxt = plan[k + 1][0] if k + 1 < len(plan) else None
        if nxt is not None and nxt != L and not bg and which == (0 if cfg.get("plan") else 0):
            pass
        if which == 0 or (k == 0) or plan[k - 1][0] != L:
            nl = None
            for (L2, _) in plan[k:]:
                if L2 != L:
                    nl = L2
                    break
            if nl is not None:
                bg.extend(mod_steps(nl))
        if which == 0:
            ffn(L, 0)
        elif which == 2:
            ffn(L, 1)
        else:
            kind = L % 3
            if kind == 0:
                gmlp(L, L // 3)
            elif kind == 1:
                attn(L)
            else:
                ssm(L)
        if nxt is None or nxt != L:
            while bg:
                bg_step()
    for dc in range(8):
        mk.dma("SP", O["yT"][:, dc, :], X[:, dc, :], is_output=True)
    mk.finish()
    mk.input_names = list(D.keys())
    return nc, mk

def _panelize(W, NW):
    K, N = W.shape
    return np.ascontiguousarray(W.reshape(K // 128, 128, N // NW, NW).transpose(2, 1, 0, 3))


def _fm(v):
    v = np.asarray(v)
    lead = v.shape[:-1]
    n = v.shape[-1] // 128
    r = v.reshape(lead + (n, 128))
    r = np.moveaxis(r, -1, 0)
    return np.ascontiguousarray(r.reshape(128, -1))


def _rep(v):
    v = np.asarray(v, np.float32).reshape(1, -1)
    return np.ascontiguousarray(np.broadcast_to(v, (128, v.shape[1])))


def prep_weights(inp):
    f32 = np.float32
    W = {}
    W["wmod"] = np.stack([_panelize(np.asarray(inp["w_mod"][i], f32), 512) for i in range(4)])
    fi = np.asarray(inp["ffn_in"], f32)
    perm = []
    for pj in range(11):
        for c in range(2):
            perm.extend(range((2 * pj + c) * 128, (2 * pj + c + 1) * 128))
        for c in range(2):
            perm.extend(range(D_FF + (2 * pj + c) * 128, D_FF + (2 * pj + c + 1) * 128))
    perm = np.array(perm)
    W["ffin"] = np.stack([np.stack([_panelize(fi[i, s][:, perm], 512) for s in range(2)]) for i in range(4)])
    fo = np.asarray(inp["ffn_out"], f32)
    W["ffout"] = np.stack([np.stack([_panelize(fo[i, s], 128) for s in range(2)]) for i in range(4)])
    gi = np.asarray(inp["gmlp_in"], f32)
    W["gin_u"] = np.stack([_panelize(gi[j][:, :3072], 512) for j in range(2)])
    W["gin_v"] = np.stack([_panelize(gi[j][:, 3072:], 512) for j in range(2)])
    ws = np.asarray(inp["gmlp_ws"], f32)
    W["gws"] = np.ascontiguousarray(ws.transpose(0, 3, 1, 2))
    W["gbs"] = np.stack([_rep(np.asarray(inp["gmlp_bs"], f32)[j].reshape(-1)) for j in range(2)])
    W["glng"] = np.stack([_fm(np.asarray(inp["gmlp_ln_g"], f32)[j]) for j in range(2)])
    W["glnb"] = np.stack([_fm(np.asarray(inp["gmlp_ln_b"], f32)[j]) for j in range(2)])
    W["gout"] = np.stack([_panelize(np.asarray(inp["gmlp_out"], f32)[j], 128) for j in range(2)])
    qkv = np.asarray(inp["attn_qkv"], f32)[0]
    qperm = []
    for c in range(8):
        a = 8 * (c // 4) + c % 4
        qperm.extend(range(a * 64, a * 64 + 64))
        qperm.extend(range((a + 4) * 64, (a + 4) * 64 + 64))
    qcols = qkv[:, :1024][:, np.array(qperm)]
    W["aqkv"] = np.concatenate([_panelize(qcols, 512), _panelize(qkv[:, 1024:1536], 512)], axis=0)
    W["asink"] = _rep(np.asarray(inp["attn_sink"], f32)[0])
    ao = np.asarray(inp["attn_out"], f32)[0]
    W["aout"] = np.ascontiguousarray(ao.reshape(16, 64, 8, 128).transpose(2, 1, 0, 3))
    si = np.asarray(inp["ssm_in"], f32)[0]
    W["sin_z"] = _panelize(si[:, :2048], 512)
    W["sin_x"] = _panelize(si[:, 2048:5120], 512)
    W["sin_dt"] = _panelize(si[:, 5120:5184], 64)[0]
    cw = np.asarray(inp["ssm_conv_w"], f32)[0]
    W["sconvw"] = np.ascontiguousarray(cw.reshape(3, 24, 128).transpose(2, 1, 0))
    W["sconvb"] = _fm(np.asarray(inp["ssm_conv_b"], f32)[0])
    W["sdtb"] = _rep(np.asarray(inp["ssm_dt_bias"], f32)[0].reshape(-1))
    W["salog"] = _rep(np.asarray(inp["ssm_a_log"], f32)[0].reshape(-1))
    W["sd"] = _rep(np.asarray(inp["ssm_d"], f32)[0])
    W["snorm"] = _fm(np.asarray(inp["ssm_norm"], f32)[0])
    W["sout"] = _panelize(np.asarray(inp["ssm_out"], f32)[0], 128)
    W["bmod"] = _fm(np.asarray(inp["b_mod"], f32))
    W["normg"] = _fm(np.asarray(inp["norm_g"], f32))
    W["ident"] = np.eye(128, dtype=f32)
    pm = np.zeros((128, 128), f32)
    for m in range(128):
        sub = (m % 64) % 32
        partner = m + 16 if sub < 16 else m - 16
        pm[partner, m] = 1.0
    W["perm"] = pm
    tri = np.zeros((128, 4, 128), f32)
    jj, ii = np.meshgrid(np.arange(128), np.arange(128), indexing="ij")
    tri[:, 0, :] = (jj <= ii)
    tri[:, 1, :] = (jj >= ii)
    tri[:, 2, :] = (jj > ii)
    tri[:, 3, :] = (jj < ii)
    W["tri"] = tri
    return W


def rope_tables():
    t = np.arange(1024)
    pos_r = (t // 64).astype(np.float64)
    pos_c = (t % 64).astype(np.float64)
    inv = 10000.0 ** (-np.arange(16, dtype=np.float64) / 16)
    C = np.zeros((128, 1024), np.float32)
    S = np.zeros((128, 1024), np.float32)
    for p in range(128):
        d = p % 64
        sub = d % 32
        i = sub % 16
        pos = pos_r if d < 32 else pos_c
        ang = (pos.astype(np.float32) * np.float32(inv[i])).astype(np.float32)
        C[p] = np.cos(ang)
        S[p] = (-np.sin(ang)) if sub < 16 else np.sin(ang)
    return C, S


def prep_core(inp, core):
    f32 = np.float32
    d = {}
    is_s = core >= 4
    if not is_s:
        x = np.asarray(inp["x_prompt"], f32)[4 * core:4 * core + 4].reshape(1024, 1024)
        cond = np.asarray(inp["c_ctx"], f32)
    else:
        b = core - 4
        x = np.asarray(inp["x_sample"], f32)[b]
        cond = np.asarray(inp["c"], f32)[b]
    d["xT"] = np.ascontiguousarray(x.T.reshape(8, 128, 1024).transpose(1, 0, 2))
    d["condT"] = _fm(cond)
    jj, ii = np.meshgrid(np.arange(128), np.arange(128), indexing="ij")
    am = np.zeros((128, 16, 128), f32)
    if is_s:
        b = core - 4
        ck = np.asarray(inp["cache_k"], f32)[b, 0]
        cv = np.asarray(inp["cache_v"], f32)[b, 0]
        d["ckT"] = np.ascontiguousarray(ck.reshape(256, 2, 128).transpose(2, 1, 0))
        d["cvT"] = np.ascontiguousarray(cv.reshape(2, 128, 256).transpose(1, 0, 2))
        st = np.asarray(inp["state_ssm"], f32)[b, 0]
        d["h0T"] = np.ascontiguousarray(st.reshape(2, 2048, 128).transpose(2, 0, 1))
        C, S = rope_tables()
        d["ropeC"], d["ropeS"] = C, S
        for i in range(8):
            if i >= 1:
                am[:, 2 * i, :] = (jj >= ii)
            if i <= 6:
                am[:, 2 * i + 1, :] = (jj <= ii)
        fl = np.zeros((128, 8), f32)
        fl[:, 0] = 0.0
        fl[:, 1] = 1.0
        fl[:, 2] = 0.0
        fl[:, 3] = 0.0
    else:
        d["ckT"] = np.zeros((128, 2, 256), f32)
        d["cvT"] = np.zeros((128, 2, 256), f32)
        d["h0T"] = np.zeros((128, 2, 2048), f32)
        d["ropeC"] = np.ones((128, 1024), f32)
        d["ropeS"] = np.zeros((128, 1024), f32)
        for i in range(8):
            if i % 2 == 1:
                am[:, 2 * i, :] = 1.0
            else:
                am[:, 2 * i + 1, :] = 1.0
        fl = np.zeros((128, 8), f32)
        fl[:, 0] = -30000.0
        fl[:, 1] = 0.0
        fl[:, 2] = 1.0
        fl[:, 3] = -1.0
    d["amask"] = am
    d["flags"] = fl
    return d


_CACHE = {}


def run_cores(inp, cfg=None, trace=False):
    key = repr(sorted((cfg or {}).items()))
    if key not in _CACHE:
        _CACHE[key] = build_program(cfg)
    nc, mk = _CACHE[key]
    W = prep_weights(inp)
    plan = (cfg or {}).get("plan")
    if plan:
        ul = sorted(set(L for L, _ in plan))
        for nm in ("wmod", "ffin", "ffout"):
            W[nm] = np.ascontiguousarray(W[nm][ul])
    names = set(mk.input_names)
    in_maps = []
    for core in range(8):
        m = dict(W)
        m.update(prep_core(inp, core))
        in_maps.append({k: v for k, v in m.items() if k in names})
    res = run_bass_kernel_spmd(nc, in_maps, core_ids=list(range(8)), trace=trace)
    return res


def assemble(res):
    f32 = np.float32
    yp = np.zeros((16, 256, 1024), f32)
    ys = np.zeros((4, 1024, 1024), f32)
    nk = np.zeros((16, 1, 256, 4, 64), f32)
    nv = np.zeros((16, 1, 256, 4, 64), f32)
    ns = np.zeros((16, 1, 2, 32, 64, 128), f32)
    for core in range(8):
        r = res.results[core]
        y = np.asarray(r["yT"]).transpose(1, 0, 2).reshape(1024, 1024).T
        if core < 4:
            yp[4 * core:4 * core + 4] = y.reshape(4, 256, 1024)
            nk[4 * core:4 * core + 4, 0] = np.asarray(r["kout"]).reshape(4, 256, 4, 64)
            nv[4 * core:4 * core + 4, 0] = np.asarray(r["vout"]).reshape(4, 256, 4, 64)
            st = np.asarray(r["stT"]).reshape(128, 4, 2, 2048)
            ns[4 * core:4 * core + 4, 0] = st.transpose(1, 2, 3, 0).reshape(4, 2, 32, 64, 128)
        else:
            ys[core - 4] = y
    return yp, ys, nk, nv, ns


def kernel(**inputs):
    res = run_cores(inputs, None)
    return assemble(res)
```

```python
import contextlib
import numpy as np
import concourse.bass as bass
import concourse.mybir as mybir

F32 = mybir.dt.float32
BF16 = mybir.dt.bfloat16
AF = mybir.ActivationFunctionType
ALU = mybir.AluOpType
AX = mybir.AxisListType

_ESZ = {F32: 4, BF16: 2, mybir.dt.int32: 4, mybir.dt.uint8: 1, mybir.dt.float16: 2}

RAW_ONLY_SAME_ENGINE = False
ENGS = ("PE", "ACT", "DVE", "POOL", "SP")
COMPUTE = ("PE", "ACT", "DVE", "POOL")


def _region(ap):
    sp = str(ap.space)
    if "SB" not in sp and "PSUM" not in sp:
        return None
    if "PSUM" in sp:
        return (ap.tensor.name, 0, 128, ((0, 2048),))
    esz = _ESZ[ap.dtype]
    apl = ap.ap
    pstride, pcount = apl[0]
    off = int(ap.offset)
    if pstride > 0:
        p0 = off // pstride
        f0 = off % pstride
    else:
        p0, f0 = 0, off
    ivs = [(0, 1)]
    for (stride, count) in reversed(apl[1:]):
        if count == 1 or stride == 0:
            continue
        if len(ivs) == 1 and stride <= (ivs[0][1] - ivs[0][0]):
            ivs = [(ivs[0][0], ivs[0][0] + stride * (count - 1) + (ivs[0][1] - ivs[0][0]))]
        elif len(ivs) * count <= 64:
            ivs = [(a + i * stride, b + i * stride) for i in range(count) for (a, b) in ivs]
        else:
            lo = min(a for a, _ in ivs)
            hi = max(b for _, b in ivs) + stride * (count - 1)
            ivs = [(lo, hi)]
    ivs = tuple(sorted(((f0 + a) * esz, (f0 + b) * esz) for a, b in ivs))
    return (ap.tensor.name, p0, p0 + pcount, ivs)


def _ovl(r1, r2):
    if r1[1] >= r2[2] or r2[1] >= r1[2]:
        return False
    a, b = r1[3], r2[3]
    if a[0][0] >= b[-1][1] or b[0][0] >= a[-1][1]:
        return False
    for (x0, x1) in a:
        for (y0, y1) in b:
            if x0 < y1 and y0 < x1:
                return True
    return False


def _covers(w, e):
    if w[1] > e[1] or w[2] < e[2]:
        return False
    for (y0, y1) in e[3]:
        ok = False
        for (x0, x1) in w[3]:
            if x0 <= y0 and y1 <= x1:
                ok = True
                break
        if not ok:
            return False
    return True


class Op:
    __slots__ = ("id", "eng", "fn", "deps", "raw", "idx", "is_dma", "waits", "inc", "vc", "dsem", "dval", "waited", "key")

    def __init__(self, id, eng, fn, is_dma):
        self.id = id
        self.eng = eng
        self.fn = fn
        self.deps = set()
        self.raw = set()
        self.is_dma = is_dma
        self.waits = []
        self.inc = None
        self.waited = False
        self.dsem = None
        self.dval = None
        self.key = float(id)


class MK:
    NDSEM = 10

    def __init__(self, nc, same_engine_sync=True):
        self.nc = nc
        self.stack = contextlib.ExitStack()
        self.ops = []
        self.hist = {}
        self.same_sync = same_engine_sync
        self.out_dmas = []
        self.psum_banks = []
        self.psum_i = 0
        self._names = set()

    def sb(self, name, shape, dtype):
        return self.stack.enter_context(self.nc.sbuf_tensor(name, list(shape), dtype))

    def alloc_psum(self):
        for i in range(8):
            self.psum_banks.append(self.stack.enter_context(self.nc.psum_tensor(f"psb{i}", [128, 512], F32)))

    def ps(self):
        b = self.psum_banks[self.psum_i % 8]
        self.psum_i += 1
        return b

    def op(self, eng, fn, reads, writes, is_dma=False):
        o = Op(len(self.ops), eng, fn, is_dma)
        self.ops.append(o)
        for ap in reads:
            if ap is None or isinstance(ap, (int, float)):
                continue
            r = _region(ap)
            if r is None:
                continue
            if r[0].startswith("psb"):
                h = self.hist.get(r[0], [])
                for e in h:
                    if e[2] != o.id:
                        o.deps.add(e[2])
                        o.raw.add(e[2])
                self.hist[r[0]] = [(r, True, o.id)]
                continue
            h = self.hist.setdefault(r[0], [])
            for e in h:
                if e[1] and _ovl(r, e[0]):
                    o.deps.add(e[2])
                    o.raw.add(e[2])
            if not is_dma:
                for k, e in enumerate(h):
                    if (not e[1]) and e[0] == r and self.ops[e[2]].eng == eng and not self.ops[e[2]].is_dma:
                        h[k] = (r, False, o.id)
                        break
                else:
                    h.append((r, False, o.id))
            else:
                h.append((r, False, o.id))
        for ap in writes:
            r = _region(ap)
            if r is None:
                continue
            h = self.hist.setdefault(r[0], [])
            keep = []
            for e in h:
                if _ovl(r, e[0]):
                    if e[2] != o.id:
                        o.deps.add(e[2])
                    if _covers(r, e[0]):
                        continue
                keep.append(e)
            keep.append((r, True, o.id))
            self.hist[r[0]] = keep
        o.deps.discard(o.id)
        return o

    def mm(self, out, lhsT, rhs, start=True, stop=True, **kw):
        return self.op("PE", lambda e: e.matmul(out, lhsT, rhs, start=start, stop=stop, **kw), [lhsT, rhs], [out])

    def transpose(self, out, in_, ident):
        return self.op("PE", lambda e: e.transpose(out, in_, ident), [in_, ident], [out])

    def act(self, out, in_, func, bias=0.0, scale=1.0, accum_out=None):
        rd = [in_]
        if not isinstance(bias, (int, float)):
            rd.append(bias)
        if not isinstance(scale, (int, float)):
            rd.append(scale)
        wr = [out] + ([accum_out] if accum_out is not None else [])
        kw = {}
        if accum_out is not None:
            kw["accum_out"] = accum_out
        return self.op("ACT", lambda e: e.activation(out, in_, func, bias=bias, scale=scale, **kw), rd, wr)

    def tt(self, eng, out, in0, in1, op):
        return self.op(eng, lambda e: e.tensor_tensor(out, in0, in1, op), [in0, in1], [out])

    def ts(self, eng, out, in0, s1, s2, op0, op1=None, accum_out=None):
        rd = [in0] + [s for s in (s1, s2) if s is not None and not isinstance(s, (int, float))]
        wr = [out] + ([accum_out] if accum_out is not None else [])
        if op1 is None:
            return self.op(eng, lambda e: e.tensor_scalar(out, in0, s1, None, op0), rd, wr)
        kw = {}
        if accum_out is not None:
            kw["accum_out"] = accum_out
        return self.op(eng, lambda e: e.tensor_scalar(out, in0, s1, s2, op0, op1, **kw), rd, wr)

    def stt(self, eng, out, in0, scalar, in1, op0, op1):
        rd = [in0, in1] + ([scalar] if not isinstance(scalar, (int, float)) else [])
        return self.op(eng, lambda e: e.scalar_tensor_tensor(out, in0, scalar, in1, op0, op1), rd, [out])

    def copy(self, eng, out, in_):
        if eng == "ACT":
            return self.op(eng, lambda e: e.copy(out, in_), [in_], [out])
        return self.op(eng, lambda e: e.tensor_copy(out, in_), [in_], [out])

    def memset(self, eng, out, val):
        return self.op(eng, lambda e: e.memset(out, val), [], [out])

    def recip(self, out, in_):
        return self.op("DVE", lambda e: e.reciprocal(out, in_), [in_], [out])

    def dma(self, q, out, in_, is_output=False, hoist=False, **kw):
        o = self.op(q, lambda e: e.dma_start(out, in_, **kw), [in_], [out], is_dma=True)
        if is_output:
            self.out_dmas.append(o)
        if hoist:
            o.key = (max(self.ops[d].key for d in o.deps) if o.deps else -1.0) + 0.5
        return o

    def finish(self):
        nc = self.nc
        byid = self.ops
        ops = sorted(self.ops, key=lambda o: (o.key, o.id))
        streams = {e: [] for e in ENGS}
        for o in ops:
            o.idx = len(streams[o.eng])
            streams[o.eng].append(o)
        dma_sem_last = {}
        dma_count = {}
        dq_n = {e: 0 for e in ENGS}
        for o in ops:
            if o.is_dma:
                k = (o.eng, dq_n[o.eng] % self.NDSEM)
                dq_n[o.eng] += 1
                if k in dma_sem_last:
                    o.deps.add(dma_sem_last[k].id)
                dma_sem_last[k] = o
                dma_count[k] = dma_count.get(k, 0) + 1
                o.dsem = k
                o.dval = 16 * dma_count[k]
        known = {e: {c: -1 for c in COMPUTE} for e in ENGS}
        known_dma = {e: set() for e in ENGS}
        for o in ops:
            kn = known[o.eng]
            kd = known_dma[o.eng]
            need = []
            for did in sorted(o.deps):
                d = byid[did]
                if d.is_dma:
                    if did in kd:
                        continue
                    need.append(d)
                    kd.add(did)
                    for c in COMPUTE:
                        if d.vc[c] > kn[c]:
                            kn[c] = d.vc[c]
                else:
                    if d.eng == o.eng:
                        if o.eng == "PE" or o.eng == "SP" or not self.same_sync:
                            continue
                        if RAW_ONLY_SAME_ENGINE and did not in o.raw:
                            continue
                    if kn[d.eng] >= d.idx and d.eng != o.eng:
                        continue
                    if d.eng == o.eng and kn.get("_self", -1) >= d.idx:
                        continue
                    need.append(d)
                    if d.eng == o.eng:
                        kn["_self"] = d.idx
                    for c in COMPUTE:
                        if d.vc[c] > kn[c]:
                            kn[c] = d.vc[c]
                    if d.eng != o.eng and d.idx > kn[d.eng]:
                        kn[d.eng] = d.idx
            o.waits = need
            for d in need:
                d.waited = True
            vc = {c: kn[c] for c in COMPUTE}
            if (not o.is_dma) and o.eng in COMPUTE:
                vc[o.eng] = o.idx
            o.vc = vc
        sems = {}
        for e in COMPUTE:
            sems[e] = self.stack.enter_context(nc.semaphore(f"s_{e}"))
        dsems = {}
        for k in dma_sem_last:
            dsems[k] = self.stack.enter_context(nc.semaphore(f"d_{k[0]}_{k[1]}"))
        cnt = {e: 0 for e in COMPUTE}
        for e in COMPUTE:
            for o in streams[e]:
                if o.is_dma:
                    continue
                if o.waited:
                    cnt[e] += 1
                    o.inc = cnt[e]
        self.stats = {e: len(streams[e]) for e in ENGS}
        self.stats["incs"] = dict(cnt)
        self.stats["waits"] = sum(len(o.waits) for o in ops)

        def emit_stream(eng_obj, ename):
            for o in streams[ename]:
                for d in o.waits:
                    if d.is_dma:
                        eng_obj.wait_ge(dsems[d.dsem], d.dval)
                    else:
                        eng_obj.wait_ge(sems[d.eng], d.inc)
                ins = o.fn(eng_obj)
                if o.is_dma:
                    ins.then_inc(dsems[o.dsem], 16)
                elif o.inc is not None:
                    ins.then_inc(sems[o.eng], 1)
            if ename == "SP":
                for k, last in dma_sem_last.items():
                    eng_obj.wait_ge(dsems[k], last.dval)
                for e in COMPUTE:
                    if cnt[e] > 0:
                        eng_obj.wait_ge(sems[e], cnt[e])

        with nc.Block() as block:
            @block.sync
            def _(e):
                emit_stream(e, "SP")

            @block.tensor
            def _(e):
                emit_stream(e, "PE")

            @block.scalar
            def _(e):
                emit_stream(e, "ACT")

            @block.vector
            def _(e):
                emit_stream(e, "DVE")

            @block.gpsimd
            def _(e):
                emit_stream(e, "POOL")
        self.stack.close()

from concourse.bass_utils import run_bass_kernel_spmd

DEPTH = 4
NTOK = 1024
D_FF = 2816
EPS_V = 1e-6


def _prod(s):
    r = 1
    for x in s:
        r *= int(x)
    return r


INPUT_SHAPES = {
    "wmod": (4, 18, 128, 8, 512),
    "ffin": (4, 2, 11, 128, 8, 512),
    "ffout": (4, 2, 8, 128, 22, 128),
    "gin_u": (2, 6, 128, 8, 512),
    "gin_v": (2, 6, 128, 8, 512),
    "gws": (2, 128, 8, 128),
    "gbs": (2, 128, 1024),
    "glng": (2, 128, 24),
    "glnb": (2, 128, 24),
    "gout": (2, 8, 128, 24, 128),
    "aqkv": (3, 128, 8, 512),
    "asink": (128, 16),
    "aout": (8, 64, 16, 128),
    "sin_x": (6, 128, 8, 512),
    "sin_z": (4, 128, 8, 512),
    "sin_dt": (128, 8, 64),
    "sconvw": (128, 24, 3),
    "sconvb": (128, 24),
    "sdtb": (128, 64),
    "salog": (128, 64),
    "sd": (128, 32),
    "snorm": (128, 16),
    "sout": (8, 128, 16, 128),
    "bmod": (128, 288),
    "normg": (128, 192),
    "ident": (128, 128),
    "perm": (128, 128),
    "tri": (128, 4, 128),
    "xT": (128, 8, 1024),
    "condT": (128, 8),
    "ckT": (128, 2, 256),
    "cvT": (128, 2, 256),
    "h0T": (128, 2, 2048),
    "ropeC": (128, 1024),
    "ropeS": (128, 1024),
    "amask": (128, 16, 128),
    "flags": (128, 8),
}
OUTPUT_SHAPES = {
    "yT": (128, 8, 1024),
    "kout": (1024, 256),
    "vout": (1024, 256),
    "stT": (128, 8, 2048),
}


def build_program(cfg=None):
    cfg = cfg or {}
    plan = cfg.get("plan") or [(L, w) for L in range(DEPTH) for w in range(3)]
    used_layers = sorted(set(L for L, _ in plan))
    LIDX = {L: k for k, L in enumerate(used_layers)}
    used = {"wmod", "bmod", "normg", "ident", "perm", "tri", "xT", "condT", "flags"}
    for (L, w) in plan:
        if w != 1:
            used |= {"ffin", "ffout"}
        elif L % 3 == 0:
            used |= {"gin_u", "gin_v", "gws", "gbs", "glng", "glnb", "gout"}
        elif L % 3 == 1:
            used |= {"aqkv", "asink", "aout", "ckT", "cvT", "ropeC", "ropeS", "amask"}
        else:
            used |= {"sin_x", "sin_z", "sin_dt", "sconvw", "sconvb", "sdtb", "salog", "sd", "snorm", "sout", "h0T"}
    nc = bass.Bass("TRN2", target_bir_lowering=False)
    D = {}
    for name, shape in INPUT_SHAPES.items():
        if name not in used:
            continue
        shape = list(shape)
        if name in ("wmod", "ffin", "ffout"):
            shape[0] = len(used_layers)
        D[name] = nc.dram_tensor(name, shape, F32, kind="ExternalInput").ap()
    O = {}
    for name, shape in OUTPUT_SHAPES.items():
        O[name] = nc.dram_tensor(name, list(shape), F32, kind="ExternalOutput").ap()
    mk = MK(nc, same_engine_sync=cfg.get("same_sync", True))
    mk.alloc_psum()
    MODBANK = mk.psum_banks.pop()
    _ps_n = [0]
    _ps_pool = [list(range(5))]
    SSB = [mk.psum_banks[5], mk.psum_banks[6]]
    xss = [False]

    def ps():
        pool = _ps_pool[0]
        b = mk.psum_banks[pool[_ps_n[0] % len(pool)]]
        _ps_n[0] += 1
        return b

    X = mk.sb("X", [128, 8, 1024], F32)
    H = mk.sb("H", [128, 8, 1024], BF16)
    ACTR = mk.sb("ACTR", [128, 24, 1024], BF16)
    WR = mk.sb("WR", [128, 4, 4096], BF16)
    FS = mk.sb("FS", [128, 12288], F32)
    SCR = mk.sb("SCR", [128, 6144], F32)
    ONES = mk.sb("ONES", [128, 128], BF16)
    IDB = mk.sb("IDB", [128, 128], BF16)
    MODT = mk.sb("MODT", [128, 2, 72], F32)
    COEFA = mk.sb("COEFA", [128, 2, 3, 8], F32)
    COEFG = mk.sb("COEFG", [128, 2, 3, 8], F32)
    SC = mk.sb("SC", [128, 8], BF16)
    CONDT = mk.sb("CONDT", [128, 8], F32)
    GT = mk.sb("GT", [128, 48], F32)
    DAB = mk.sb("DAB", [128, 8, 64], BF16)
    BMOD = mk.sb("BMOD", [128, 72], F32)
    FLAGS = mk.sb("FLAGS", [128, 8], F32)
    EPS = mk.sb("EPS", [128, 1], F32)
    SMALL = mk.sb("SMALL", [128, 384], F32)
    LW2 = mk.sb("LW2", [128, 1024], BF16)
    TRI = mk.sb("TRI", [128, 4, 128], BF16)
    ONE1 = mk.sb("ONE1", [128, 1], F32)

    F = FS[:, 0:8192].rearrange("p (c n) -> p c n", n=1024)
    FSb = FS[:].bitcast(BF16)
    SCRb = SCR[:].bitcast(BF16)
    RSTD = SCR[:, 0:1024]
    RSTD2 = SCR[:, 1024:2048]
    TMP = [SCR[:, 2048:3072], SCR[:, 3072:4096]]
    SQ = [SCRb[:, 8192:9216], SCRb[:, 9216:10240]]
    TG = [SCR[:, 5120:5632], SCR[:, 5632:6144]]

    def halves(ap2):
        return [ap2[:, 0:512], ap2[:, 512:1024]]

    class Ring:
        def __init__(self):
            self.n = 0

        def slot(self):
            s = WR[:, self.n % 4, :]
            self.n += 1
            return s

        def load(self, dram_ap):
            s = self.slot()
            shp = dram_ap.shape
            npart = shp[0]
            sz = _prod(shp[1:])
            v = s[0:npart, 0:sz].rearrange("p (k n) -> p k n", n=shp[-1])
            mk.dma("POOL", v, dram_ap, hoist=True)
            return v

    ring = Ring()

    for dc in range(8):
        mk.dma("SP", X[:, dc, :], D["xT"][:, dc, :])
    mk.memset("DVE", ONES[:], 1.0)
    mk.memset("DVE", EPS[:], EPS_V)
    mk.dma("POOL", IDB[:], D["ident"])
    mk.dma("POOL", TRI[:], D["tri"])
    mk.memset("DVE", ONE1[:], 1.0)
    mk.dma("SP", CONDT[:], D["condT"])
    mk.dma("SP", FLAGS[:], D["flags"])
    mk.act(SC[:], CONDT[:], AF.Silu)

    mod_done = set()

    def mod_steps(L):
        steps = []
        par = L % 2

        def fin(sub):
            def f():
                if sub == 0:
                    mk.dma("SP", BMOD[:], D["bmod"][:, L * 72:(L + 1) * 72])
                    mk.dma("SP", GT[:], D["normg"][:, L * 48:(L + 1) * 48])
                cs = slice(24 * sub, 24 * sub + 24)
                mk.tt("DVE", MODT[:, par, cs], MODBANK[:, cs], BMOD[:, cs], ALU.add)
                g_pre = GT[:, (2 * sub) * 8:(2 * sub) * 8 + 8]
                g_post = GT[:, (2 * sub + 1) * 8:(2 * sub + 1) * 8 + 8]
                sc_ = MODT[:, par, (3 * sub + 1) * 8:(3 * sub + 1) * 8 + 8]
                gate = MODT[:, par, (3 * sub + 2) * 8:(3 * sub + 2) * 8 + 8]
                mk.stt("DVE", COEFA[:, par, sub, :], sc_, 1.0, g_pre, ALU.add, ALU.mult)
                mk.stt("DVE", COEFG[:, par, sub, :], gate, (1.0 if sub == 1 else 0.5), g_post, ALU.mult, ALU.mult)
                mod_done.add((L, sub))
            return f

        for pn in range(18):
            def step(pn=pn):
                W = ring.load(D["wmod"][LIDX[L], pn])
                for j in range(4):
                    m = pn * 4 + j
                    for kc in range(8):
                        mk.mm(MODBANK[:, m:m + 1], W[:, kc, j * 128:(j + 1) * 128], SC[:, kc:kc + 1],
                              start=(kc == 0), stop=(kc == 7))
            steps.append(step)
            if pn % 6 == 5:
                steps.append(fin(pn // 6))
        return steps

    bg = []
    LB_ENG = cfg.get('lb_eng', 'POOL')

    def bg_step():
        if bg:
            bg.pop(0)()

    SQ4 = [SQ[0][:, 0:512], SQ[0][:, 512:1024], SQ[1][:, 0:512], SQ[1][:, 512:1024]]
    _sq_n = [0]

    _ss_pending = []

    def ss_flush():
        while _ss_pending:
            _ss_pending.pop(0)()

    def ss_add(src_half, th, first, last, defer=False):
        sq = SQ4[_sq_n[0] % 4]
        _sq_n[0] += 1
        mk.act(sq, src_half, AF.Square)

        def mmop():
            mk.mm(SSB[th][:], ONES[:], sq, start=first, stop=last)
        if defer:
            _ss_pending.append(mmop)
        else:
            mmop()

    def ss_finish(dst):
        for th in range(2):
            d = dst[:, th * 512:(th + 1) * 512]
            mk.act(d, SSB[th][:], AF.Sqrt, bias=EPS[:], scale=1.0 / 1024.0)
            mk.recip(d, d)

    def sumsq_rstd(src3, dst):
        for dc in range(8):
            for th in range(2):
                ss_add(src3[:, dc, th * 512:(th + 1) * 512], th, dc == 0, dc == 7)
        ss_finish(dst)

    def adaln(L, s):
        par = L % 2
        while (L, s) not in mod_done:
            bg.pop(0)()
        if xss[0]:
            ss_finish(RSTD)
            xss[0] = False
        else:
            sumsq_rstd(X, RSTD)
        for dc in range(8):
            t = TMP[dc % 2]
            mk.stt("DVE", t, X[:, dc, :], COEFA[:, par, s, dc:dc + 1], RSTD, ALU.mult, ALU.mult)
            mk.act(H[:, dc, :], t, AF.Identity, bias=MODT[:, par, 3 * s * 8 + dc:3 * s * 8 + dc + 1], scale=1.0)

    def evac_f(Fdst, dc, th, p):
        ths = slice(th * 512, (th + 1) * 512)
        ss_flush()
        mk.act(Fdst[:, dc, ths], p[:], AF.Copy)
        ss_add(p[:], th, dc == 0, dc == 7, defer=True)

    def postnorm(L, s, F=F):
        par = L % 2
        ss_flush()
        ss_finish(RSTD2)
        for dc in range(8):
            t = TMP[dc % 2]
            mk.stt("DVE", t, F[:, dc, :], COEFG[:, par, s, dc:dc + 1], RSTD2, ALU.mult, ALU.mult)
            mk.tt("DVE", X[:, dc, :], X[:, dc, :], t, ALU.add)
            for th in range(2):
                ss_add(X[:, dc, th * 512:(th + 1) * 512], th, dc == 0, dc == 7)
        xss[0] = True

    def ffn(L, s2):
        s = 0 if s2 == 0 else 2
        adaln(L, s)
        _ps_pool[0] = list(range(7))
        for pj in range(11):
            W = ring.load(D["ffin"][LIDX[L], s2, pj])
            for c in range(2):
                for th in range(2):
                    ths = slice(th * 512, (th + 1) * 512)
                    pg, pu = ps(), ps()
                    for kc in range(8):
                        mk.mm(pg[:], W[:, kc, c * 128:(c + 1) * 128], H[:, kc, ths], start=(kc == 0), stop=(kc == 7))
                    for kc in range(8):
                        mk.mm(pu[:], W[:, kc, 256 + c * 128:256 + (c + 1) * 128], H[:, kc, ths], start=(kc == 0), stop=(kc == 7))
                    mk.act(TG[th], pg[:], AF.Silu)
                    mk.tt("DVE", ACTR[:, 2 * pj + c, ths], TG[th], pu[:], ALU.mult)
            bg_step()
        _ps_pool[0] = list(range(5))
        for dc in range(8):
            W = ring.load(D["ffout"][LIDX[L], s2, dc])
            for th in range(2):
                ths = slice(th * 512, (th + 1) * 512)
                p = ps()
                for kc in range(22):
                    mk.mm(p[:], W[:, kc, :], ACTR[:, kc, ths], start=(kc == 0), stop=(kc == 21))
                evac_f(F, dc, th, p)
            bg_step()
        postnorm(L, s)

    def gmlp(L, j):
        adaln(L, 1)
        WV = FSb.rearrange("p (a k n) -> p a k n", a=6, k=8)
        for vp in range(6):
            mk.dma("POOL", WV[:, vp], D["gin_v"][j, vp], hoist=True)
        gstop = cfg.get("gstop", 9)
        if gstop <= 1:
            return
        for up in range(6):
            W = ring.load(D["gin_u"][j, up])
            for c in range(4):
                for th in range(2):
                    ths = slice(th * 512, (th + 1) * 512)
                    p = ps()
                    for kc in range(8):
                        mk.mm(p[:], W[:, kc, c * 128:(c + 1) * 128], H[:, kc, ths], start=(kc == 0), stop=(kc == 7))
                    mk.act(ACTR[:, up * 4 + c, ths], p[:], AF.Gelu)
            bg_step()
        if ring.n % 4 == 3:
            ring.slot()
        s0 = ring.slot()
        ring.slot()
        sl = (ring.n - 2) % 4
        T1 = WR[:, sl:sl + 2, :].rearrange("p s n -> p (s n)").bitcast(F32)[:, 0:3072].rearrange("p (c i) -> p c i", i=128)
        BSB = SCR[:, 0:1024].rearrange("p (g i) -> p g i", i=128)
        WST = SCRb[:, 11264:12288].rearrange("p (g i) -> p g i", i=128)
        LNG = SMALL[:, 0:24]
        LNB = SMALL[:, 24:48]
        mk.dma("POOL", WST, D["gws"][j])
        mk.dma("SP", BSB, D["gbs"][j].rearrange("p (g i) -> p g i", i=128))
        mk.dma("SP", LNG, D["glng"][j])
        mk.dma("SP", LNB, D["glnb"][j])
        for hh in range(2):
            p = ps()
            for g4 in range(4):
                g = hh * 4 + g4
                mk.mm(p[:, g4 * 128:(g4 + 1) * 128], ONES[:], WST[:, g, :], start=True, stop=True)
            for g4 in range(4):
                g = hh * 4 + g4
                for c3 in range(3):
                    fc = g * 3 + c3
                    mk.stt("DVE", T1[:, fc, :], p[:, g4 * 128:(g4 + 1) * 128], LNB[:, fc:fc + 1], BSB[:, g, :], ALU.mult, ALU.add)
        if gstop <= 2:
            return
        VT = SCR[:, 0:3072]
        VN = SCRb[:, 6144:9216]
        SVT = SCR[:, 4608:5120].rearrange("p (k i) -> p k i", i=128)
        ST = SMALL[:, 64:80]
        def v_mm(tc, vp, bank):
            tcs = slice(tc * 128, (tc + 1) * 128)
            for kc in range(8):
                mk.mm(bank[:], H[:, kc, tcs], WV[:, vp, kc, :], start=(kc == 0), stop=(kc == 7))

        def v_evac(vp, bank):
            mk.act(VT[:, vp * 512:(vp + 1) * 512], bank[:], AF.Gelu, accum_out=ST[:, 8 + vp:9 + vp])

        _rot = [0]

        def rot3():
            b = mk.psum_banks[4 + _rot[0] % 3]
            _rot[0] += 1
            return b

        SE = cfg.get("stat_eng", "POOL")

        def v_chunk_early(tc):
            for vp in range(4):
                v_mm(tc, vp, mk.psum_banks[vp])

        def v_chunk_late(tc):
            for vp in range(4):
                v_evac(vp, mk.psum_banks[vp])
            for vp in (4, 5):
                v_mm(tc, vp, mk.psum_banks[vp - 4])
                v_evac(vp, mk.psum_banks[vp - 4])

        def rot2():
            return rot3()

        v_chunk_early(0)
        v_chunk_late(0)
        for tc in range(8):
            tcs = slice(tc * 128, (tc + 1) * 128)
            STX = SMALL[:, 320:328]
            mk.act(STX[:, 0:6], ST[:, 8:14], AF.Copy, accum_out=ST[:, 0:1])
            mk.act(VN, VT, AF.Square, accum_out=ST[:, 2:3])
            mk.act(ST[:, 5:6], ST[:, 0:1], AF.Square, scale=1.0 / 3072.0)
            mk.act(ST[:, 5:6], ST[:, 5:6], AF.Copy, scale=-1.0)
            mk.act(ST[:, 6:7], ST[:, 2:3], AF.Identity, bias=ST[:, 5:6], scale=1.0 / 3072.0)
            mk.act(ST[:, 3:4], ST[:, 6:7], AF.Ln, bias=EPS[:], scale=1.0)
            mk.act(ST[:, 4:5], ST[:, 3:4], AF.Exp, scale=-0.5)
            mk.act(ST[:, 7:8], ST[:, 0:1], AF.Copy, scale=ST[:, 4:5])
            mk.act(ST[:, 7:8], ST[:, 7:8], AF.Copy, scale=-1.0 / 3072.0)
            mk.act(VN, VT, AF.Identity, bias=ST[:, 7:8], scale=ST[:, 4:5])
            if tc + 1 < 8:
                v_chunk_early(tc + 1)
            for f4 in range(6):
                p = rot2()
                for k in range(4):
                    fc = f4 * 4 + k
                    mk.mm(p[:, k * 128:(k + 1) * 128], VN[:, fc * 128:(fc + 1) * 128], WST[:, fc // 3, :], start=True, stop=True)
                for k in range(4):
                    fc = f4 * 4 + k
                    mk.stt("DVE", SVT[:, k, :], p[:, k * 128:(k + 1) * 128], LNG[:, fc:fc + 1], T1[:, fc, :], ALU.mult, ALU.add)
                u = ACTR[:, f4 * 4:(f4 + 1) * 4, tcs]
                mk.tt("DVE", u, u, SVT, ALU.mult)
            if tc + 1 < 8:
                v_chunk_late(tc + 1)
        if gstop <= 6:
            return
        for dc in range(8):
            W = ring.load(D["gout"][j, dc])
            for th in range(2):
                ths = slice(th * 512, (th + 1) * 512)
                p = ps()
                for kc in range(24):
                    mk.mm(p[:], W[:, kc, :], ACTR[:, kc, ths], start=(kc == 0), stop=(kc == 23))
                evac_f(F, dc, th, p)
            bg_step()
        postnorm(L, 1)

    def attn(L):
        adaln(L, 1)
        QT = ACTR[:, 0:8, :]
        KT = ACTR[:, 8:10, :]
        VTK = ACTR[:, 10:12, :].rearrange("p a (t n) -> p (a t) n", n=256)
        OT = FSb[0:64, 0:16384].rearrange("p (h n) -> p h n", n=1024)
        ROPEC = SCR[:, 0:1024]
        ROPES = SCR[:, 1024:2048]
        AMASK = SCRb[:, 4096:6144].rearrange("p (m n) -> p m n", n=128)
        CKT = SCRb[:, 6144:6656].rearrange("p (c n) -> p c n", n=256)
        CVT = SCRb[:, 6656:7168].rearrange("p (c n) -> p c n", n=256)
        QB = [SCRb[:, 7168:7680], SCRb[:, 7680:8192]]
        RT0 = [SCR[:, 4096:4608], SCR[:, 4608:5120]]
        RT1 = [SCR[:, 5120:5632], SCR[:, 5632:6144]]
        PERM = SMALL[:, 128:192].bitcast(BF16)
        ESB = SMALL[:, 192:208]
        mk.dma("SP", ROPEC, D["ropeC"])
        mk.dma("SP", ROPES, D["ropeS"])
        mk.dma("POOL", AMASK, D["amask"])
        mk.dma("POOL", CKT, D["ckT"])
        mk.dma("POOL", CVT, D["cvT"])
        mk.dma("POOL", PERM, D["perm"])
        mk.dma("SP", ESB, D["asink"])
        mk.act(ESB, ESB, AF.Exp)
        W2 = None

        _rp = []

        def rope_flush():
            while _rp:
                _rp.pop(0)()

        def rope_chunk(dst3, ci, W, c):
            for th in range(2):
                ths = slice(th * 512, (th + 1) * 512)
                p = ps()
                for kc in range(8):
                    mk.mm(p[:], W[:, kc, c * 128:(c + 1) * 128], H[:, kc, ths], start=(kc == 0), stop=(kc == 7))
                rope_flush()
                mk.act(QB[th], p[:], AF.Copy)
                mk.tt("DVE", RT0[th], p[:], ROPEC[:, ths], ALU.mult)

                def later(th=th, ths=ths, ci=ci, dst3=dst3):
                    pr = ps()
                    mk.mm(pr[:], PERM, QB[th], start=True, stop=True)
                    mk.tt("DVE", RT1[th], pr[:], ROPES[:, ths], ALU.mult)
                    mk.tt("DVE", dst3[:, ci, ths], RT0[th], RT1[th], ALU.add)
                _rp.append(later)

        for qp in range(2):
            W = ring.load(D["aqkv"][qp])
            for c in range(4):
                rope_chunk(QT, qp * 4 + c, W, c)
        W2 = ring.load(D["aqkv"][2])
        for c in range(2):
            rope_chunk(KT, c, W2, c)
        rope_flush()
        astop = cfg.get("astop", 9)
        if astop <= 1:
            return
        KVS = [FS[:, 8192:8704], FS[:, 8704:9216]]
        for tc in range(8):
            tcs = slice(tc * 128, (tc + 1) * 128)
            p = ps()
            for kc in range(8):
                mk.mm(p[:], H[:, kc, tcs], W2[:, kc, :], start=(kc == 0), stop=(kc == 7))
            kv = KVS[tc % 2]
            mk.act(kv, p[:], AF.Copy)
            mk.copy("DVE", VTK[:, tc, :], p[:, 256:512])
            mk.dma("SP", O["kout"][tc * 128:(tc + 1) * 128, :], kv[:, 0:256], is_output=True)
            mk.dma("SP", O["vout"][tc * 128:(tc + 1) * 128, :], kv[:, 256:512], is_output=True)
        if astop <= 2:
            return
        EST = FS[0:64, 9216:11264].rearrange("p (h n) -> p h n", n=128)
        mk.copy("DVE", EST, ESB[0:64, :].unsqueeze(2).to_broadcast([64, 16, 128]))
        PTS = [[SCRb[:, 8192 + s * 512:8192 + (s + 1) * 512] for s in range(5)],
               [SCRb[:, s * 512:(s + 1) * 512] for s in range(5)]]
        DEN = FS[0:64, 11264:11776]
        its = [(i, hk) for i in range(8) for hk in range(4)]
        vls = {}

        def stage_s(n):
            i, hk = its[n]
            PT = PTS[n % 2]
            half = hk % 2
            hs = slice(half * 64, half * 64 + 64)
            qc0 = 4 * (hk // 2)
            qmov = QT[hs, qc0:qc0 + 4, i * 128:(i + 1) * 128]
            vlist = []
            for s_ in range(5):
                if s_ < 3:
                    kb = min(max(i - 1 + s_, 0), 7)
                    keysT = KT[hs, hk // 2, kb * 128:(kb + 1) * 128]
                    vlist.append(VTK[:, kb, hk * 64:(hk + 1) * 64])
                else:
                    keysT = CKT[hs, hk // 2, (s_ - 3) * 128:(s_ - 2) * 128]
                    vlist.append(CVT[:, s_ - 3, hk * 64:(hk + 1) * 64])
                p = mk.psum_banks[s_]
                mk.mm(p[:].rearrange("p (h n) -> p h n", n=128), keysT, qmov, start=True, stop=True)
                if s_ < 3:
                    mk.act(PT[s_], p[:], AF.Exp, scale=0.125)
                else:
                    mk.act(PT[s_], p[:], AF.Exp, bias=FLAGS[:, 0:1], scale=0.125)
                if s_ in (0, 2):
                    m = AMASK[:, i * 2 + (0 if s_ == 0 else 1), :]
                    pv = PT[s_].rearrange("p (h n) -> p h n", n=128)
                    mk.tt("POOL", pv, pv, m.unsqueeze(1).to_broadcast([128, 4, 128]), ALU.mult)
            vls[n] = vlist

        def stage_o(n):
            i, hk = its[n]
            PT = PTS[n % 2]
            vlist = vls.pop(n)
            po, pd = mk.psum_banks[5], mk.psum_banks[6]
            for s_ in range(5):
                mk.mm(po[0:64, :], vlist[s_], PT[s_], start=(s_ == 0), stop=(s_ == 4))
            for s_ in range(5):
                mk.mm(pd[0:64, :], ONES[:, 0:64], PT[s_], start=(s_ == 0), stop=(s_ == 4))
            mk.tt("DVE", DEN, pd[0:64, :], EST[:, hk * 4:(hk + 1) * 4, :], ALU.add)
            mk.recip(DEN, DEN)
            mk.tt("DVE", OT[:, hk * 4:(hk + 1) * 4, i * 128:(i + 1) * 128], po[0:64, :].rearrange("p (h n) -> p h n", n=128),
                  DEN.rearrange("p (h n) -> p h n", n=128), ALU.mult)

        stage_s(0)
        for n in range(len(its)):
            if n + 1 < len(its):
                stage_s(n + 1)
            stage_o(n)
        FA = ACTR[:].rearrange("p c n -> p (c n)").bitcast(F32)[:, 0:8192].rearrange("p (c n) -> p c n", n=1024)
        for dc in range(8):
            W = ring.load(D["aout"][dc])
            for th in range(2):
                ths = slice(th * 512, (th + 1) * 512)
                p = ps()
                for h in range(16):
                    mk.mm(p[:], W[:, h, :], OT[:, h, ths], start=(h == 0), stop=(h == 15))
                evac_f(FA, dc, th, p)
            bg_step()
        postnorm(L, 1, FA)

    def ssm(L):
        adaln(L, 1)
        XCT = ACTR
        HB = FSb[:, 0:16384].rearrange("p (t n) -> p t n", n=2048)
        BTK = FSb[:, 16384:20480].rearrange("p (t n) -> p t n", n=512)
        HST = FS[:, 10240:12288]
        CW = SMALL[:, 0:72].rearrange("p (c k) -> p c k", k=3)
        CBI = SMALL[:, 72:96]
        W0P = SMALL[:, 96:120]
        W2P = SMALL[:, 120:144]
        DTB = SMALL[:, 144:208]
        ABC = SMALL[:, 208:272]
        SDB = SMALL[:, 272:304]
        SNORM = SMALL[:, 304:320]
        SSQ = SMALL[:, 320:328]
        mk.dma("SP", CW, D["sconvw"])
        mk.dma("SP", CBI, D["sconvb"])
        mk.dma("SP", DTB, D["sdtb"])
        mk.dma("SP", ABC, D["salog"])
        mk.dma("SP", SDB, D["sd"])
        mk.dma("SP", SNORM, D["snorm"])
        mk.act(ABC, ABC, AF.Exp)
        mk.ts("DVE", ABC, ABC, -1.0, None, ALU.mult)
        mk.ts("DVE", W0P, CW[:, :, 0], FLAGS[:, 3:4], None, ALU.mult)
        mk.ts("DVE", W2P, CW[:, :, 2], FLAGS[:, 3:4], None, ALU.mult)
        RAW = [SCR[:, 0:1026], SCR[:, 1026:2052]]
        ACC = [SCR[:, 2052:3076], SCR[:, 3076:4100]]
        for r in RAW:
            mk.memset("DVE", r[:, 0:1], 0.0)
            mk.memset("DVE", r[:, 1025:1026], 0.0)
        conv_pending = []
        for xp in range(6):
            W = ring.load(D["sin_x"][xp])
            for c in range(4):
                ch = xp * 4 + c
                raw = RAW[ch % 2]
                acc = ACC[ch % 2]
                if len(conv_pending) > 1:
                    pch, pacc = conv_pending.pop(0)
                    mk.act(XCT[:, pch, :], pacc, AF.Silu)
                for th in range(2):
                    ths = slice(th * 512, (th + 1) * 512)
                    p = ps()
                    for kc in range(8):
                        mk.mm(p[:], W[:, kc, c * 128:(c + 1) * 128], H[:, kc, ths], start=(kc == 0), stop=(kc == 7))
                    mk.act(raw[:, 1 + th * 512:1 + (th + 1) * 512], p[:], AF.Copy)
                mk.act(acc, raw[:, 1:1025], AF.Identity, bias=CBI[:, ch:ch + 1], scale=CW[:, ch, 1:2])
                mk.stt("DVE", acc, raw[:, 0:1024], CW[:, ch, 0:1], acc, ALU.mult, ALU.add)
                mk.stt("DVE", acc, raw[:, 2:1026], CW[:, ch, 2:3], acc, ALU.mult, ALU.add)
                a0 = acc[:, 256:1024:256]
                mk.stt("DVE", a0, raw[:, 256:1024:256], W0P[:, ch:ch + 1], a0, ALU.mult, ALU.add)
                a1 = acc[:, 255:1023:256]
                mk.stt("DVE", a1, raw[:, 257:1025:256], W2P[:, ch:ch + 1], a1, ALU.mult, ALU.add)
                conv_pending.append((ch, acc))
            bg_step()
        while conv_pending:
            pch, pacc = conv_pending.pop(0)
            mk.act(XCT[:, pch, :], pacc, AF.Silu)
        Wdt = ring.load(D["sin_dt"])
        DT = SCR[:, 0:512].rearrange("p (t n) -> p t n", n=64)
        DTR = SCR[:, 4100:4612].rearrange("p (t n) -> p t n", n=64)
        ABt = SCR[:, 4612:5124].rearrange("p (t n) -> p t n", n=64)
        DAH = SCRb[:, 10248:10760].rearrange("p (t n) -> p t n", n=64)
        DAL = SCRb[:, 10760:11272].rearrange("p (t n) -> p t n", n=64)
        pdt = ps()
        for tc in range(8):
            tcs = slice(tc * 128, (tc + 1) * 128)
            for kc in range(8):
                mk.mm(pdt[:, tc * 64:(tc + 1) * 64], H[:, kc, tcs], Wdt[:, kc, :], start=(kc == 0), stop=(kc == 7))
        mk.tt("DVE", DTR, pdt[:].rearrange("p (t n) -> p t n", n=64), DTB.unsqueeze(1).to_broadcast([128, 8, 64]), ALU.add)
        mk.act(ABt, DTR, AF.Abs)
        mk.act(ABt, ABt, AF.Exp, scale=-1.0)
        mk.act(ABt, ABt, AF.Ln, bias=ONE1[:], scale=1.0)
        mk.act(DTR, DTR, AF.Relu)
        mk.tt("DVE", DT, DTR, ABt, ALU.add)
        DA = DTR
        mk.tt("DVE", DA, DT, ABC.unsqueeze(1).to_broadcast([128, 8, 64]), ALU.mult)
        mk.copy("DVE", DAH, DA)
        mk.copy("DVE", DAB[:], DA)
        mk.tt("DVE", ABt, DA, DAH, ALU.subtract)
        mk.copy("DVE", DAL, ABt)
        pac, ptot = ps(), ps()
        for tc in range(8):
            for d_ in range(2):
                o = pac[:, tc * 64 + d_ * 32:tc * 64 + (d_ + 1) * 32]
                mk.mm(o, TRI[:, d_, :], DAH[:, tc, d_ * 32:(d_ + 1) * 32], start=True, stop=False)
                mk.mm(o, TRI[:, d_, :], DAL[:, tc, d_ * 32:(d_ + 1) * 32], start=False, stop=True)
            o = ptot[:, tc * 64:(tc + 1) * 64]
            mk.mm(o, ONES[:], DAH[:, tc, :], start=True, stop=False)
            mk.mm(o, ONES[:], DAL[:, tc, :], start=False, stop=True)
        EA = SCR[:, 512:1024].rearrange("p (t n) -> p t n", n=64)
        CD = SCR[:, 1024:1536].rearrange("p (t n) -> p t n", n=64)
        CO = SCR[:, 1536:2048].rearrange("p (t n) -> p t n", n=64)
        mk.copy("DVE", EA, pac[:].rearrange("p (t n) -> p t n", n=64))
        mk.copy("DVE", CD, ptot[:].rearrange("p (t n) -> p t n", n=64))
        mk.tt("DVE", CO, CD, EA, ALU.subtract)
        mk.act(CO, CO, AF.Exp)
        mk.tt("DVE", CO, CO, DT, ALU.mult)
        mk.act(EA, EA, AF.Exp)
        mk.act(CD, CD, AF.Exp)
        XTK = SCRb[:, 4096:6144]
        XDW = SCRb[:, 6144:8192]
        CBM = SCRb[:, 8192:9216].rearrange("p (d g i) -> p d g i", d=2, g=4)
        YA = SCR[:, 4608:5120]
        YB = SCR[:, 5120:5632]
        WTS = [SCRb[:, 11264:11776].rearrange("p (k i) -> p k i", i=128), LW2[:, 512:1024].rearrange("p (k i) -> p k i", i=128)]
        LBS = [SCRb[:, 11776:12288].rearrange("p (k i) -> p k i", i=128), LW2[:, 0:512].rearrange("p (k i) -> p k i", i=128)]
        Wz = [ring.load(D["sin_z"][g]) for g in range(4)]

        def bc_hp(ap2, n=8):
            return ap2.unsqueeze(2).to_broadcast([128, n, 64])

        def make_tok(tc, with_b):
            tcs = slice(tc * 128, (tc + 1) * 128)
            chans = list(range(16)) + (list(range(16, 20)) if with_b else [])
            for q in range(0, len(chans), 4):
                pb = ps()[:].bitcast(BF16)
                for k in range(4):
                    mk.transpose(pb[:, k * 128:(k + 1) * 128], XCT[:, chans[q + k], tcs], IDB[:])
                if q < 16:
                    mk.copy("ACT", XTK[:, q * 128:(q + 4) * 128], pb[:, 0:512])
                else:
                    mk.copy("ACT", BTK[:, tc, :], pb[:, 0:512])

        def state_update(tc, d_):
            mk.tt("DVE", XDW.rearrange("p (h q) -> p h q", q=64), XTK.rearrange("p (h q) -> p h q", q=64),
                  bc_hp(CO[:, tc, d_ * 32:(d_ + 1) * 32], 32), ALU.mult)
            for g in range(4):
                p = ps()
                mk.mm(p[:], BTK[:, tc, g * 128:(g + 1) * 128], XDW[:, g * 512:(g + 1) * 512], start=True, stop=True)
                hs = HST[:, g * 512:(g + 1) * 512].rearrange("p (h q) -> p h q", q=64)
                mk.tt("DVE", hs, hs, bc_hp(CD[:, tc, d_ * 32 + g * 8:d_ * 32 + (g + 1) * 8]), ALU.mult)
                mk.tt("DVE", HST[:, g * 512:(g + 1) * 512], HST[:, g * 512:(g + 1) * 512], p[:], ALU.add)

        mk.dma("SP", HST, D["h0T"][:, 1, :])
        for tc in range(7, -1, -1):
            if tc % 2 == 1 and tc < 7:
                mk.ts("DVE", HST, HST, FLAGS[:, 1:2], None, ALU.mult)
            mk.copy("ACT", HB[:, tc, :], HST)
            make_tok(tc, True)
            state_update(tc, 1)
            if tc % 2 == 0:
                mk.dma("SP", O["stT"][:, (tc // 2) * 2 + 1, :], HST, is_output=True)
        mk.dma("SP", HST, D["h0T"][:, 0, :])
        HFB = XDW
        GY = XDW
        mk.copy("ACT", HFB, HST)
        for tc in range(8):
            tcs = slice(tc * 128, (tc + 1) * 128)
            make_tok(tc, False)
            pcb = ps()
            for g in range(4):
                mk.mm(pcb[:, g * 128:(g + 1) * 128], XCT[:, 16 + g, tcs], XCT[:, 20 + g, tcs], start=True, stop=True)
            for d_ in range(2):
                mk.tt("DVE", CBM[:, d_], pcb[:].rearrange("p (g i) -> p g i", i=128),
                      TRI[:, d_, :].unsqueeze(1).to_broadcast([128, 4, 128]), ALU.mult)
            hf_cur = HFB if tc == 0 else HB[:, tc - 1, :]
            units = [(g, d_, h4) for g in range(4) for d_ in range(2) for h4 in range(2)]
            segs = {}
            pyb = {}

            def stage_a(ui):
                g, d_, h4 = units[ui]
                lbs = LBS[ui % 2]
                pseg = mk.psum_banks[2 + ui % 2]
                segs[ui] = pseg
                dh0 = d_ * 32 + g * 8 + h4 * 4
                mk.tt(LB_ENG, lbs, TRI[:, 2 + d_, :].unsqueeze(1).to_broadcast([128, 4, 128]),
                      DAB[:, tc, dh0:dh0 + 4].unsqueeze(2).to_broadcast([128, 4, 128]), ALU.mult)
                for k in range(4):
                    mk.mm(pseg[:, k * 128:(k + 1) * 128], lbs[:, k, :], TRI[:, d_, :], start=True, stop=True)

            def stage_b(ui):
                pseg = segs[ui]
                mk.act(pseg[:], pseg[:], AF.Exp)

            def stage_c(ui):
                g, d_, h4 = units[ui]
                wts = WTS[ui % 2]
                pseg = segs[ui]
                if g not in pyb:
                    pyb[g] = mk.psum_banks[g % 2]
                py = pyb[g]
                dh0 = d_ * 32 + g * 8 + h4 * 4
                p3 = pseg[:].rearrange("p (k i) -> p k i", i=128)
                mk.tt("DVE", p3, p3, CBM[:, d_, g, :].unsqueeze(1).to_broadcast([128, 4, 128]), ALU.mult)
                mk.tt("DVE", wts, p3, DT[:, tc, dh0:dh0 + 4].unsqueeze(2).to_broadcast([128, 4, 128]), ALU.mult)
                for k in range(4):
                    hh = h4 * 4 + k
                    h = g * 8 + hh
                    mk.mm(py[:, hh * 64:(hh + 1) * 64], wts[:, k, :], XTK[:, h * 64:(h + 1) * 64],
                          start=(d_ == 0 and hh == 0), stop=(d_ == 1 and hh == 7))

            pzb = {}

            pofb, pobb = {}, {}

            def z_early(g):
                pz = mk.psum_banks[5 + g % 2]
                pzb[g] = pz
                for kc in range(8):
                    mk.mm(pz[:], H[:, kc, tcs], Wz[g][:, kc, :], start=(kc == 0), stop=(kc == 7))
                mk.act(pz[:], pz[:], AF.Silu)
                pofb[g] = mk.psum_banks[4]
                pobb[g] = mk.psum_banks[6 - g % 2]
                mk.mm(pofb[g][:], XCT[:, 20 + g, tcs], hf_cur[:, g * 512:(g + 1) * 512], start=True, stop=True)
                mk.mm(pobb[g][:], XCT[:, 20 + g, tcs], HB[:, tc, g * 512:(g + 1) * 512], start=True, stop=True)

            def ycomb(g):
                py = pyb[g]
                ya3 = YA.rearrange("p (h q) -> p h q", q=64)
                yb3 = YB.rearrange("p (h q) -> p h q", q=64)
                pof, pob = pofb[g], pobb[g]
                mk.tt("DVE", ya3, pof[:].rearrange("p (h q) -> p h q", q=64), bc_hp(EA[:, tc, g * 8:(g + 1) * 8]), ALU.mult)
                mk.tt("DVE", yb3, pob[:].rearrange("p (h q) -> p h q", q=64), bc_hp(EA[:, tc, 32 + g * 8:32 + (g + 1) * 8]), ALU.mult)
                mk.tt("DVE", YA, YA, YB, ALU.add)
                mk.tt("DVE", YA, YA, py[:], ALU.add)
                mk.tt("DVE", yb3, XTK[:, g * 512:(g + 1) * 512].rearrange("p (h q) -> p h q", q=64), bc_hp(SDB[:, g * 8:(g + 1) * 8]), ALU.mult)
                mk.tt("DVE", YA, YA, YB, ALU.add)
                mk.tt("DVE", YA, YA, pzb[g][:], ALU.mult)

                def act_part(g=g):
                    mk.act(YB, YA, AF.Square, accum_out=SSQ[:, g:g + 1])
                    mk.copy("ACT", GY[:, g * 512:(g + 1) * 512], YA)
                yc_pending.append(act_part)

            _ps_pool[0] = [5, 6]
            yc_pending = []
            stage_a(0)
            stage_b(0)
            for ui in range(16):
                if ui % 4 == 0:
                    z_early(ui // 4)
                if ui + 1 < 16:
                    stage_a(ui + 1)
                    stage_b(ui + 1)
                while yc_pending:
                    yc_pending.pop(0)()
                stage_c(ui)
                if ui % 4 == 3:
                    ycomb(ui // 4)
            while yc_pending:
                yc_pending.pop(0)()
            _ps_pool[0] = list(range(5))
            mk.op("DVE", lambda e: e.reduce_sum(SSQ[:, 4:5], SSQ[:, 0:4], AX.X), [SSQ[:, 0:4]], [SSQ[:, 4:5]])
            mk.act(SSQ[:, 5:6], SSQ[:, 4:5], AF.Sqrt, bias=EPS[:], scale=1.0 / 2048.0)
            mk.recip(SSQ[:, 6:7], SSQ[:, 5:6])
            mk.ts("DVE", GY, GY, SSQ[:, 6:7], None, ALU.mult)
            for q in range(0, 16, 4):
                pb = ps()[:].bitcast(BF16)
                for k in range(4):
                    mk.transpose(pb[:, k * 128:(k + 1) * 128], GY[:, (q + k) * 128:(q + k + 1) * 128], IDB[:])
                for k in range(4):
                    mk.act(XCT[:, q + k, tcs], pb[:, k * 128:(k + 1) * 128], AF.Identity, scale=SNORM[:, q + k:q + k + 1])
            state_update(tc, 0)
            if tc % 2 == 1:
                mk.dma("SP", O["stT"][:, (tc // 2) * 2, :], HST, is_output=True)
                if tc < 7:
                    mk.ts("DVE", HST, HST, FLAGS[:, 1:2], None, ALU.mult)
            if tc < 7:
                mk.copy("ACT", HB[:, tc, :], HST)
        for dc in range(8):
            W = ring.load(D["sout"][dc])
            for th in range(2):
                ths = slice(th * 512, (th + 1) * 512)
                p = ps()
                for kc in range(16):
                    mk.mm(p[:], W[:, kc, :], XCT[:, kc, ths], start=(kc == 0), stop=(kc == 15))
                evac_f(F, dc, th, p)
            bg_step()
        postnorm(L, 1)

    bg.extend(mod_steps(plan[0][0]))
    for k, (L, which) in enumerate(plan):
        nxt = plan[k + 1][0] if k + 1 < len(plan) else None
        if nxt is not None and nxt != L and not bg and which == (0 if cfg.get("plan") else 0):
            pass
        if which == 0 or (k == 0) or plan[k - 1][0] != L:
            nl = None
            for (L2, _) in plan[k:]:
                if L2 != L:
                    nl = L2
                    break
            if nl is not None:
                bg.extend(mod_steps(nl))
        if which == 0:
            ffn(L, 0)
        elif which == 2:
            ffn(L, 1)
        else:
            kind = L % 3
            if kind == 0:
                gmlp(L, L // 3)
            elif kind == 1:
                attn(L)
            else:
                ssm(L)
        if nxt is None or nxt != L:
            while bg:
                bg_step()
    for dc in range(8):
        mk.dma("SP", O["yT"][:, dc, :], X[:, dc, :], is_output=True)
    mk.finish()
    mk.input_names = list(D.keys())
    return nc, mk

def _panelize(W, NW):
    K, N = W.shape
    return np.ascontiguousarray(W.reshape(K // 128, 128, N // NW, NW).transpose(2, 1, 0, 3))


def _fm(v):
    v = np.asarray(v)
    lead = v.shape[:-1]
    n = v.shape[-1] // 128
    r = v.reshape(lead + (n, 128))
    r = np.moveaxis(r, -1, 0)
    return np.ascontiguousarray(r.reshape(128, -1))


def _rep(v):
    v = np.asarray(v, np.float32).reshape(1, -1)
    return np.ascontiguousarray(np.broadcast_to(v, (128, v.shape[1])))


def prep_weights(inp):
    f32 = np.float32
    W = {}
    W["wmod"] = np.stack([_panelize(np.asarray(inp["w_mod"][i], f32), 512) for i in range(4)])
    fi = np.asarray(inp["ffn_in"], f32)
    perm = []
    for pj in range(11):
        for c in range(2):
            perm.extend(range((2 * pj + c) * 128, (2 * pj + c + 1) * 128))
        for c in range(2):
            perm.extend(range(D_FF + (2 * pj + c) * 128, D_FF + (2 * pj + c + 1) * 128))
    perm = np.array(perm)
    W["ffin"] = np.stack([np.stack([_panelize(fi[i, s][:, perm], 512) for s in range(2)]) for i in range(4)])
    fo = np.asarray(inp["ffn_out"], f32)
    W["ffout"] = np.stack([np.stack([_panelize(fo[i, s], 128) for s in range(2)]) for i in range(4)])
    gi = np.asarray(inp["gmlp_in"], f32)
    W["gin_u"] = np.stack([_panelize(gi[j][:, :3072], 512) for j in range(2)])
    W["gin_v"] = np.stack([_panelize(gi[j][:, 3072:], 512) for j in range(2)])
    ws = np.asarray(inp["gmlp_ws"], f32)
    W["gws"] = np.ascontiguousarray(ws.transpose(0, 3, 1, 2))
    W["gbs"] = np.stack([_rep(np.asarray(inp["gmlp_bs"], f32)[j].reshape(-1)) for j in range(2)])
    W["glng"] = np.stack([_fm(np.asarray(inp["gmlp_ln_g"], f32)[j]) for j in range(2)])
    W["glnb"] = np.stack([_fm(np.asarray(inp["gmlp_ln_b"], f32)[j]) for j in range(2)])
    W["gout"] = np.stack([_panelize(np.asarray(inp["gmlp_out"], f32)[j], 128) for j in range(2)])
    qkv = np.asarray(inp["attn_qkv"], f32)[0]
    qperm = []
    for c in range(8):
        a = 8 * (c // 4) + c % 4
        qperm.extend(range(a * 64, a * 64 + 64))
        qperm.extend(range((a + 4) * 64, (a + 4) * 64 + 64))
    qcols = qkv[:, :1024][:, np.array(qperm)]
    W["aqkv"] = np.concatenate([_panelize(qcols, 512), _panelize(qkv[:, 1024:1536], 512)], axis=0)
    W["asink"] = _rep(np.asarray(inp["attn_sink"], f32)[0])
    ao = np.asarray(inp["attn_out"], f32)[0]
    W["aout"] = np.ascontiguousarray(ao.reshape(16, 64, 8, 128).transpose(2, 1, 0, 3))
    si = np.asarray(inp["ssm_in"], f32)[0]
    W["sin_z"] = _panelize(si[:, :2048], 512)
    W["sin_x"] = _panelize(si[:, 2048:5120], 512)
    W["sin_dt"] = _panelize(si[:, 5120:5184], 64)[0]
    cw = np.asarray(inp["ssm_conv_w"], f32)[0]
    W["sconvw"] = np.ascontiguousarray(cw.reshape(3, 24, 128).transpose(2, 1, 0))
    W["sconvb"] = _fm(np.asarray(inp["ssm_conv_b"], f32)[0])
    W["sdtb"] = _rep(np.asarray(inp["ssm_dt_bias"], f32)[0].reshape(-1))
    W["salog"] = _rep(np.asarray(inp["ssm_a_log"], f32)[0].reshape(-1))
    W["sd"] = _rep(np.asarray(inp["ssm_d"], f32)[0])
    W["snorm"] = _fm(np.asarray(inp["ssm_norm"], f32)[0])
    W["sout"] = _panelize(np.asarray(inp["ssm_out"], f32)[0], 128)
    W["bmod"] = _fm(np.asarray(inp["b_mod"], f32))
    W["normg"] = _fm(np.asarray(inp["norm_g"], f32))
    W["ident"] = np.eye(128, dtype=f32)
    pm = np.zeros((128, 128), f32)
    for m in range(128):
        sub = (m % 64) % 32
        partner = m + 16 if sub < 16 else m - 16
        pm[partner, m] = 1.0
    W["perm"] = pm
    tri = np.zeros((128, 4, 128), f32)
    jj, ii = np.meshgrid(np.arange(128), np.arange(128), indexing="ij")
    tri[:, 0, :] = (jj <= ii)
    tri[:, 1, :] = (jj >= ii)
    tri[:, 2, :] = (jj > ii)
    tri[:, 3, :] = (jj < ii)
    W["tri"] = tri
    return W


def rope_tables():
    t = np.arange(1024)
    pos_r = (t // 64).astype(np.float64)
    pos_c = (t % 64).astype(np.float64)
    inv = 10000.0 ** (-np.arange(16, dtype=np.float64) / 16)
    C = np.zeros((128, 1024), np.float32)
    S = np.zeros((128, 1024), np.float32)
    for p in range(128):
        d = p % 64
        sub = d % 32
        i = sub % 16
        pos = pos_r if d < 32 else pos_c
        ang = (pos.astype(np.float32) * np.float32(inv[i])).astype(np.float32)
        C[p] = np.cos(ang)
        S[p] = (-np.sin(ang)) if sub < 16 else np.sin(ang)
    return C, S


def prep_core(inp, core):
    f32 = np.float32
    d = {}
    is_s = core >= 4
    if not is_s:
        x = np.asarray(inp["x_prompt"], f32)[4 * core:4 * core + 4].reshape(1024, 1024)
        cond = np.asarray(inp["c_ctx"], f32)
    else:
        b = core - 4
        x = np.asarray(inp["x_sample"], f32)[b]
        cond = np.asarray(inp["c"], f32)[b]
    d["xT"] = np.ascontiguousarray(x.T.reshape(8, 128, 1024).transpose(1, 0, 2))
    d["condT"] = _fm(cond)
    jj, ii = np.meshgrid(np.arange(128), np.arange(128), indexing="ij")
    am = np.zeros((128, 16, 128), f32)
    if is_s:
        b = core - 4
        ck = np.asarray(inp["cache_k"], f32)[b, 0]
        cv = np.asarray(inp["cache_v"], f32)[b, 0]
        d["ckT"] = np.ascontiguousarray(ck.reshape(256, 2, 128).transpose(2, 1, 0))
        d["cvT"] = np.ascontiguousarray(cv.reshape(2, 128, 256).transpose(1, 0, 2))
        st = np.asarray(inp["state_ssm"], f32)[b, 0]
        d["h0T"] = np.ascontiguousarray(st.reshape(2, 2048, 128).transpose(2, 0, 1))
        C, S = rope_tables()
        d["ropeC"], d["ropeS"] = C, S
        for i in range(8):
            if i >= 1:
                am[:, 2 * i, :] = (jj >= ii)
            if i <= 6:
                am[:, 2 * i + 1, :] = (jj <= ii)
        fl = np.zeros((128, 8), f32)
        fl[:, 0] = 0.0
        fl[:, 1] = 1.0
        fl[:, 2] = 0.0
        fl[:, 3] = 0.0
    else:
        d["ckT"] = np.zeros((128, 2, 256), f32)
        d["cvT"] = np.zeros((128, 2, 256), f32)
        d["h0T"] = np.zeros((128, 2, 2048), f32)
        d["ropeC"] = np.ones((128, 1024), f32)
        d["ropeS"] = np.zeros((128, 1024), f32)
        for i in range(8):
            if i % 2 == 1:
                am[:, 2 * i, :] = 1.0
            else:
                am[:, 2 * i + 1, :] = 1.0
        fl = np.zeros((128, 8), f32)
        fl[:, 0] = -30000.0
        fl[:, 1] = 0.0
        fl[:, 2] = 1.0
        fl[:, 3] = -1.0
    d["amask"] = am
    d["flags"] = fl
    return d


_CACHE = {}


def run_cores(inp, cfg=None, trace=False):
    key = repr(sorted((cfg or {}).items()))
    if key not in _CACHE:
        _CACHE[key] = build_program(cfg)
    nc, mk = _CACHE[key]
    W = prep_weights(inp)
    plan = (cfg or {}).get("plan")
    if plan:
        ul = sorted(set(L for L, _ in plan))
        for nm in ("wmod", "ffin", "ffout"):
            W[nm] = np.ascontiguousarray(W[nm][ul])
    names = set(mk.input_names)
    in_maps = []
    for core in range(8):
        m = dict(W)
        m.update(prep_core(inp, core))
        in_maps.append({k: v for k, v in m.items() if k in names})
    res = run_bass_kernel_spmd(nc, in_maps, core_ids=list(range(8)), trace=trace)
    return res


def assemble(res):
    f32 = np.float32
    yp = np.zeros((16, 256, 1024), f32)
    ys = np.zeros((4, 1024, 1024), f32)
    nk = np.zeros((16, 1, 256, 4, 64), f32)
    nv = np.zeros((16, 1, 256, 4, 64), f32)
    ns = np.zeros((16, 1, 2, 32, 64, 128), f32)
    for core in range(8):
        r = res.results[core]
        y = np.asarray(r["yT"]).transpose(1, 0, 2).reshape(1024, 1024).T
        if core < 4:
            yp[4 * core:4 * core + 4] = y.reshape(4, 256, 1024)
            nk[4 * core:4 * core + 4, 0] = np.asarray(r["kout"]).reshape(4, 256, 4, 64)
            nv[4 * core:4 * core + 4, 0] = np.asarray(r["vout"]).reshape(4, 256, 4, 64)
            st = np.asarray(r["stT"]).reshape(128, 4, 2, 2048)
            ns[4 * core:4 * core + 4, 0] = st.transpose(1, 2, 3, 0).reshape(4, 2, 32, 64, 128)
        else:
            ys[core - 4] = y
    return yp, ys, nk, nv, ns


def kernel(**inputs):
    res = run_cores(inputs, None)
    return assemble(res)
```

```python
import contextlib
import numpy as np
import concourse.bass as bass
import concourse.mybir as mybir

F32 = mybir.dt.float32
BF16 = mybir.dt.bfloat16
AF = mybir.ActivationFunctionType
ALU = mybir.AluOpType
AX = mybir.AxisListType

_ESZ = {F32: 4, BF16: 2, mybir.dt.int32: 4, mybir.dt.uint8: 1, mybir.dt.float16: 2}

RAW_ONLY_SAME_ENGINE = False
ENGS = ("PE", "ACT", "DVE", "POOL", "SP")
COMPUTE = ("PE", "ACT", "DVE", "POOL")


def _region(ap):
    sp = str(ap.space)
    if "SB" not in sp and "PSUM" not in sp:
        return None
    if "PSUM" in sp:
        return (ap.tensor.name, 0, 128, ((0, 2048),))
    esz = _ESZ[ap.dtype]
    apl = ap.ap
    pstride, pcount = apl[0]
    off = int(ap.offset)
    if pstride > 0:
        p0 = off // pstride
        f0 = off % pstride
    else:
        p0, f0 = 0, off
    ivs = [(0, 1)]
    for (stride, count) in reversed(apl[1:]):
        if count == 1 or stride == 0:
            continue
        if len(ivs) == 1 and stride <= (ivs[0][1] - ivs[0][0]):
            ivs = [(ivs[0][0], ivs[0][0] + stride * (count - 1) + (ivs[0][1] - ivs[0][0]))]
        elif len(ivs) * count <= 64:
            ivs = [(a + i * stride, b + i * stride) for i in range(count) for (a, b) in ivs]
        else:
            lo = min(a for a, _ in ivs)
            hi = max(b for _, b in ivs) + stride * (count - 1)
            ivs = [(lo, hi)]
    ivs = tuple(sorted(((f0 + a) * esz, (f0 + b) * esz) for a, b in ivs))
    return (ap.tensor.name, p0, p0 + pcount, ivs)


def _ovl(r1, r2):
    if r1[1] >= r2[2] or r2[1] >= r1[2]:
        return False
    a, b = r1[3], r2[3]
    if a[0][0] >= b[-1][1] or b[0][0] >= a[-1][1]:
        return False
    for (x0, x1) in a:
        for (y0, y1) in b:
            if x0 < y1 and y0 < x1:
                return True
    return False


def _covers(w, e):
    if w[1] > e[1] or w[2] < e[2]:
        return False
    for (y0, y1) in e[3]:
        ok = False
        for (x0, x1) in w[3]:
            if x0 <= y0 and y1 <= x1:
                ok = True
                break
        if not ok:
            return False
    return True


class Op:
    __slots__ = ("id", "eng", "fn", "deps", "raw", "idx", "is_dma", "waits", "inc", "vc", "dsem", "dval", "waited", "key")

    def __init__(self, id, eng, fn, is_dma):
        self.id = id
        self.eng = eng
        self.fn = fn
        self.deps = set()
        self.raw = set()
        self.is_dma = is_dma
        self.waits = []
        self.inc = None
        self.waited = False
        self.dsem = None
        self.dval = None
        self.key = float(id)


class MK:
    NDSEM = 10

    def __init__(self, nc, same_engine_sync=True):
        self.nc = nc
        self.stack = contextlib.ExitStack()
        self.ops = []
        self.hist = {}
        self.same_sync = same_engine_sync
        self.out_dmas = []
        self.psum_banks = []
        self.psum_i = 0
        self._names = set()

    def sb(self, name, shape, dtype):
        return self.stack.enter_context(self.nc.sbuf_tensor(name, list(shape), dtype))

    def alloc_psum(self):
        for i in range(8):
            self.psum_banks.append(self.stack.enter_context(self.nc.psum_tensor(f"psb{i}", [128, 512], F32)))

    def ps(self):
        b = self.psum_banks[self.psum_i % 8]
        self.psum_i += 1
        return b

    def op(self, eng, fn, reads, writes, is_dma=False):
        o = Op(len(self.ops), eng, fn, is_dma)
        self.ops.append(o)
        for ap in reads:
            if ap is None or isinstance(ap, (int, float)):
                continue
            r = _region(ap)
            if r is None:
                continue
            if r[0].startswith("psb"):
                h = self.hist.get(r[0], [])
                for e in h:
                    if e[2] != o.id:
                        o.deps.add(e[2])
                        o.raw.add(e[2])
                self.hist[r[0]] = [(r, True, o.id)]
                continue
            h = self.hist.setdefault(r[0], [])
            for e in h:
                if e[1] and _ovl(r, e[0]):
                    o.deps.add(e[2])
                    o.raw.add(e[2])
            if not is_dma:
                for k, e in enumerate(h):
                    if (not e[1]) and e[0] == r and self.ops[e[2]].eng == eng and not self.ops[e[2]].is_dma:
                        h[k] = (r, False, o.id)
                        break
                else:
                    h.append((r, False, o.id))
            else:
                h.append((r, False, o.id))
        for ap in writes:
            r = _region(ap)
            if r is None:
                continue
            h = self.hist.setdefault(r[0], [])
            keep = []
            for e in h:
                if _ovl(r, e[0]):
                    if e[2] != o.id:
                        o.deps.add(e[2])
                    if _covers(r, e[0]):
                        continue
                keep.append(e)
            keep.append((r, True, o.id))
            self.hist[r[0]] = keep
        o.deps.discard(o.id)
        return o

    def mm(self, out, lhsT, rhs, start=True, stop=True, **kw):
        return self.op("PE", lambda e: e.matmul(out, lhsT, rhs, start=start, stop=stop, **kw), [lhsT, rhs], [out])

    def transpose(self, out, in_, ident):
        return self.op("PE", lambda e: e.transpose(out, in_, ident), [in_, ident], [out])

    def act(self, out, in_, func, bias=0.0, scale=1.0, accum_out=None):
        rd = [in_]
        if not isinstance(bias, (int, float)):
            rd.append(bias)
        if not isinstance(scale, (int, float)):
            rd.append(scale)
        wr = [out] + ([accum_out] if accum_out is not None else [])
        kw = {}
        if accum_out is not None:
            kw["accum_out"] = accum_out
        return self.op("ACT", lambda e: e.activation(out, in_, func, bias=bias, scale=scale, **kw), rd, wr)

    def tt(self, eng, out, in0, in1, op):
        return self.op(eng, lambda e: e.tensor_tensor(out, in0, in1, op), [in0, in1], [out])

    def ts(self, eng, out, in0, s1, s2, op0, op1=None, accum_out=None):
        rd = [in0] + [s for s in (s1, s2) if s is not None and not isinstance(s, (int, float))]
        wr = [out] + ([accum_out] if accum_out is not None else [])
        if op1 is None:
            return self.op(eng, lambda e: e.tensor_scalar(out, in0, s1, None, op0), rd, wr)
        kw = {}
        if accum_out is not None:
            kw["accum_out"] = accum_out
        return self.op(eng, lambda e: e.tensor_scalar(out, in0, s1, s2, op0, op1, **kw), rd, wr)

    def stt(self, eng, out, in0, scalar, in1, op0, op1):
        rd = [in0, in1] + ([scalar] if not isinstance(scalar, (int, float)) else [])
        return self.op(eng, lambda e: e.scalar_tensor_tensor(out, in0, scalar, in1, op0, op1), rd, [out])

    def copy(self, eng, out, in_):
        if eng == "ACT":
            return self.op(eng, lambda e: e.copy(out, in_), [in_], [out])
        return self.op(eng, lambda e: e.tensor_copy(out, in_), [in_], [out])

    def memset(self, eng, out, val):
        return self.op(eng, lambda e: e.memset(out, val), [], [out])

    def recip(self, out, in_):
        return self.op("DVE", lambda e: e.reciprocal(out, in_), [in_], [out])

    def dma(self, q, out, in_, is_output=False, hoist=False, **kw):
        o = self.op(q, lambda e: e.dma_start(out, in_, **kw), [in_], [out], is_dma=True)
        if is_output:
            self.out_dmas.append(o)
        if hoist:
            o.key = (max(self.ops[d].key for d in o.deps) if o.deps else -1.0) + 0.5
        return o

    def finish(self):
        nc = self.nc
        byid = self.ops
        ops = sorted(self.ops, key=lambda o: (o.key, o.id))
        streams = {e: [] for e in ENGS}
        for o in ops:
            o.idx = len(streams[o.eng])
            streams[o.eng].append(o)
        dma_sem_last = {}
        dma_count = {}
        dq_n = {e: 0 for e in ENGS}
        for o in ops:
            if o.is_dma:
                k = (o.eng, dq_n[o.eng] % self.NDSEM)
                dq_n[o.eng] += 1
                if k in dma_sem_last:
                    o.deps.add(dma_sem_last[k].id)
                dma_sem_last[k] = o
                dma_count[k] = dma_count.get(k, 0) + 1
                o.dsem = k
                o.dval = 16 * dma_count[k]
        known = {e: {c: -1 for c in COMPUTE} for e in ENGS}
        known_dma = {e: set() for e in ENGS}
        for o in ops:
            kn = known[o.eng]
            kd = known_dma[o.eng]
            need = []
            for did in sorted(o.deps):
                d = byid[did]
                if d.is_dma:
                    if did in kd:
                        continue
                    need.append(d)
                    kd.add(did)
                    for c in COMPUTE:
                        if d.vc[c] > kn[c]:
                            kn[c] = d.vc[c]
                else:
                    if d.eng == o.eng:
                        if o.eng == "PE" or o.eng == "SP" or not self.same_sync:
                            continue
                        if RAW_ONLY_SAME_ENGINE and did not in o.raw:
                            continue
                    if kn[d.eng] >= d.idx and d.eng != o.eng:
                        continue
                    if d.eng == o.eng and kn.get("_self", -1) >= d.idx:
                        continue
                    need.append(d)
                    if d.eng == o.eng:
                        kn["_self"] = d.idx
                    for c in COMPUTE:
                        if d.vc[c] > kn[c]:
                            kn[c] = d.vc[c]
                    if d.eng != o.eng and d.idx > kn[d.eng]:
                        kn[d.eng] = d.idx
            o.waits = need
            for d in need:
                d.waited = True
            vc = {c: kn[c] for c in COMPUTE}
            if (not o.is_dma) and o.eng in COMPUTE:
                vc[o.eng] = o.idx
            o.vc = vc
        sems = {}
        for e in COMPUTE:
            sems[e] = self.stack.enter_context(nc.semaphore(f"s_{e}"))
        dsems = {}
        for k in dma_sem_last:
            dsems[k] = self.stack.enter_context(nc.semaphore(f"d_{k[0]}_{k[1]}"))
        cnt = {e: 0 for e in COMPUTE}
        for e in COMPUTE:
            for o in streams[e]:
                if o.is_dma:
                    continue
                if o.waited:
                    cnt[e] += 1
                    o.inc = cnt[e]
        self.stats = {e: len(streams[e]) for e in ENGS}
        self.stats["incs"] = dict(cnt)
        self.stats["waits"] = sum(len(o.waits) for o in ops)

        def emit_stream(eng_obj, ename):
            for o in streams[ename]:
                for d in o.waits:
                    if d.is_dma:
                        eng_obj.wait_ge(dsems[d.dsem], d.dval)
                    else:
                        eng_obj.wait_ge(sems[d.eng], d.inc)
                ins = o.fn(eng_obj)
                if o.is_dma:
                    ins.then_inc(dsems[o.dsem], 16)
                elif o.inc is not None:
                    ins.then_inc(sems[o.eng], 1)
            if ename == "SP":
                for k, last in dma_sem_last.items():
                    eng_obj.wait_ge(dsems[k], last.dval)
                for e in COMPUTE:
                    if cnt[e] > 0:
                        eng_obj.wait_ge(sems[e], cnt[e])

        with nc.Block() as block:
            @block.sync
            def _(e):
                emit_stream(e, "SP")

            @block.tensor
            def _(e):
                emit_stream(e, "PE")

            @block.scalar
            def _(e):
                emit_stream(e, "ACT")

            @block.vector
            def _(e):
                emit_stream(e, "DVE")

            @block.gpsimd
            def _(e):
                emit_stream(e, "POOL")
        self.stack.close()

from concourse.bass_utils import run_bass_kernel_spmd

DEPTH = 4
NTOK = 1024
D_FF = 2816
EPS_V = 1e-6


def _prod(s):
    r = 1
    for x in s:
        r *= int(x)
    return r


INPUT_SHAPES = {
    "wmod": (4, 18, 128, 8, 512),
    "ffin": (4, 2, 11, 128, 8, 512),
    "ffout": (4, 2, 8, 128, 22, 128),
    "gin_u": (2, 6, 128, 8, 512),
    "gin_v": (2, 6, 128, 8, 512),
    "gws": (2, 128, 8, 128),
    "gbs": (2, 128, 1024),
    "glng": (2, 128, 24),
    "glnb": (2, 128, 24),
    "gout": (2, 8, 128, 24, 128),
    "aqkv": (3, 128, 8, 512),
    "asink": (128, 16),
    "aout": (8, 64, 16, 128),
    "sin_x": (6, 128, 8, 512),
    "sin_z": (4, 128, 8, 512),
    "sin_dt": (128, 8, 64),
    "sconvw": (128, 24, 3),
    "sconvb": (128, 24),
    "sdtb": (128, 64),
    "salog": (128, 64),
    "sd": (128, 32),
    "snorm": (128, 16),
    "sout": (8, 128, 16, 128),
    "bmod": (128, 288),
    "normg": (128, 192),
    "ident": (128, 128),
    "perm": (128, 128),
    "tri": (128, 4, 128),
    "xT": (128, 8, 1024),
    "condT": (128, 8),
    "ckT": (128, 2, 256),
    "cvT": (128, 2, 256),
    "h0T": (128, 2, 2048),
    "ropeC": (128, 1024),
    "ropeS": (128, 1024),
    "amask": (128, 16, 128),
    "flags": (128, 8),
}
OUTPUT_SHAPES = {
    "yT": (128, 8, 1024),
    "kout": (1024, 256),
    "vout": (1024, 256),
    "stT": (128, 8, 2048),
}


def build_program(cfg=None):
    cfg = cfg or {}
    plan = cfg.get("plan") or [(L, w) for L in range(DEPTH) for w in range(3)]
    used_layers = sorted(set(L for L, _ in plan))
    LIDX = {L: k for k, L in enumerate(used_layers)}
    used = {"wmod", "bmod", "normg", "ident", "perm", "tri", "xT", "condT", "flags"}
    for (L, w) in plan:
        if w != 1:
            used |= {"ffin", "ffout"}
        elif L % 3 == 0:
            used |= {"gin_u", "gin_v", "gws", "gbs", "glng", "glnb", "gout"}
        elif L % 3 == 1:
            used |= {"aqkv", "asink", "aout", "ckT", "cvT", "ropeC", "ropeS", "amask"}
        else:
            used |= {"sin_x", "sin_z", "sin_dt", "sconvw", "sconvb", "sdtb", "salog", "sd", "snorm", "sout", "h0T"}
    nc = bass.Bass("TRN2", target_bir_lowering=False)
    D = {}
    for name, shape in INPUT_SHAPES.items():
        if name not in used:
            continue
        shape = list(shape)
        if name in ("wmod", "ffin", "ffout"):
            shape[0] = len(used_layers)
        D[name] = nc.dram_tensor(name, shape, F32, kind="ExternalInput").ap()
    O = {}
    for name, shape in OUTPUT_SHAPES.items():
        O[name] = nc.dram_tensor(name, list(shape), F32, kind="ExternalOutput").ap()
    mk = MK(nc, same_engine_sync=cfg.get("same_sync", True))
    mk.alloc_psum()
    MODBANK = mk.psum_banks.pop()
    _ps_n = [0]
    _ps_pool = [list(range(5))]
    SSB = [mk.psum_banks[5], mk.psum_banks[6]]
    xss = [False]

    def ps():
        pool = _ps_pool[0]
        b = mk.psum_banks[pool[_ps_n[0] % len(pool)]]
        _ps_n[0] += 1
        return b

    X = mk.sb("X", [128, 8, 1024], F32)
    H = mk.sb("H", [128, 8, 1024], BF16)
    ACTR = mk.sb("ACTR", [128, 24, 1024], BF16)
    WR = mk.sb("WR", [128, 4, 4096], BF16)
    FS = mk.sb("FS", [128, 12288], F32)
    SCR = mk.sb("SCR", [128, 6144], F32)
    ONES = mk.sb("ONES", [128, 128], BF16)
    IDB = mk.sb("IDB", [128, 128], BF16)
    MODT = mk.sb("MODT", [128, 2, 72], F32)
    COEFA = mk.sb("COEFA", [128, 2, 3, 8], F32)
    COEFG = mk.sb("COEFG", [128, 2, 3, 8], F32)
    SC = mk.sb("SC", [128, 8], BF16)
    CONDT = mk.sb("CONDT", [128, 8], F32)
    GT = mk.sb("GT", [128, 48], F32)
    DAB = mk.sb("DAB", [128, 8, 64], BF16)
    BMOD = mk.sb("BMOD", [128, 72], F32)
    FLAGS = mk.sb("FLAGS", [128, 8], F32)
    EPS = mk.sb("EPS", [128, 1], F32)
    SMALL = mk.sb("SMALL", [128, 384], F32)
    LW2 = mk.sb("LW2", [128, 1024], BF16)
    TRI = mk.sb("TRI", [128, 4, 128], BF16)
    ONE1 = mk.sb("ONE1", [128, 1], F32)

    F = FS[:, 0:8192].rearrange("p (c n) -> p c n", n=1024)
    FSb = FS[:].bitcast(BF16)
    SCRb = SCR[:].bitcast(BF16)
    RSTD = SCR[:, 0:1024]
    RSTD2 = SCR[:, 1024:2048]
    TMP = [SCR[:, 2048:3072], SCR[:, 3072:4096]]
    SQ = [SCRb[:, 8192:9216], SCRb[:, 9216:10240]]
    TG = [SCR[:, 5120:5632], SCR[:, 5632:6144]]

    def halves(ap2):
        return [ap2[:, 0:512], ap2[:, 512:1024]]

    class Ring:
        def __init__(self):
            self.n = 0

        def slot(self):
            s = WR[:, self.n % 4, :]
            self.n += 1
            return s

        def load(self, dram_ap):
            s = self.slot()
            shp = dram_ap.shape
            npart = shp[0]
            sz = _prod(shp[1:])
            v = s[0:npart, 0:sz].rearrange("p (k n) -> p k n", n=shp[-1])
            mk.dma("POOL", v, dram_ap, hoist=True)
            return v

    ring = Ring()

    for dc in range(8):
        mk.dma("SP", X[:, dc, :], D["xT"][:, dc, :])
    mk.memset("DVE", ONES[:], 1.0)
    mk.memset("DVE", EPS[:], EPS_V)
    mk.dma("POOL", IDB[:], D["ident"])
    mk.dma("POOL", TRI[:], D["tri"])
    mk.memset("DVE", ONE1[:], 1.0)
    mk.dma("SP", CONDT[:], D["condT"])
    mk.dma("SP", FLAGS[:], D["flags"])
    mk.act(SC[:], CONDT[:], AF.Silu)

    mod_done = set()

    def mod_steps(L):
        steps = []
        par = L % 2

        def fin(sub):
            def f():
                if sub == 0:
                    mk.dma("SP", BMOD[:], D["bmod"][:, L * 72:(L + 1) * 72])
                    mk.dma("SP", GT[:], D["normg"][:, L * 48:(L + 1) * 48])
                cs = slice(24 * sub, 24 * sub + 24)
                mk.tt("DVE", MODT[:, par, cs], MODBANK[:, cs], BMOD[:, cs], ALU.add)
                g_pre = GT[:, (2 * sub) * 8:(2 * sub) * 8 + 8]
                g_post = GT[:, (2 * sub + 1) * 8:(2 * sub + 1) * 8 + 8]
                sc_ = MODT[:, par, (3 * sub + 1) * 8:(3 * sub + 1) * 8 + 8]
                gate = MODT[:, par, (3 * sub + 2) * 8:(3 * sub + 2) * 8 + 8]
                mk.stt("DVE", COEFA[:, par, sub, :], sc_, 1.0, g_pre, ALU.add, ALU.mult)
                mk.stt("DVE", COEFG[:, par, sub, :], gate, (1.0 if sub == 1 else 0.5), g_post, ALU.mult, ALU.mult)
                mod_done.add((L, sub))
            return f

        for pn in range(18):
            def step(pn=pn):
                W = ring.load(D["wmod"][LIDX[L], pn])
                for j in range(4):
                    m = pn * 4 + j
                    for kc in range(8):
                        mk.mm(MODBANK[:, m:m + 1], W[:, kc, j * 128:(j + 1) * 128], SC[:, kc:kc + 1],
                              start=(kc == 0), stop=(kc == 7))
            steps.append(step)
            if pn % 6 == 5:
                steps.append(fin(pn // 6))
        return steps

    bg = []
    LB_ENG = cfg.get('lb_eng', 'POOL')

    def bg_step():
        if bg:
            bg.pop(0)()

    SQ4 = [SQ[0][:, 0:512], SQ[0][:, 512:1024], SQ[1][:, 0:512], SQ[1][:, 512:1024]]
    _sq_n = [0]

    _ss_pending = []

    def ss_flush():
        while _ss_pending:
            _ss_pending.pop(0)()

    def ss_add(src_half, th, first, last, defer=False):
        sq = SQ4[_sq_n[0] % 4]
        _sq_n[0] += 1
        mk.act(sq, src_half, AF.Square)

        def mmop():
            mk.mm(SSB[th][:], ONES[:], sq, start=first, stop=last)
        if defer:
            _ss_pending.append(mmop)
        else:
            mmop()

    def ss_finish(dst):
        for th in range(2):
            d = dst[:, th * 512:(th + 1) * 512]
            mk.act(d, SSB[th][:], AF.Sqrt, bias=EPS[:], scale=1.0 / 1024.0)
            mk.recip(d, d)

    def sumsq_rstd(src3, dst):
        for dc in range(8):
            for th in range(2):
                ss_add(src3[:, dc, th * 512:(th + 1) * 512], th, dc == 0, dc == 7)
        ss_finish(dst)

    def adaln(L, s):
        par = L % 2
        while (L, s) not in mod_done:
            bg.pop(0)()
        if xss[0]:
            ss_finish(RSTD)
            xss[0] = False
        else:
            sumsq_rstd(X, RSTD)
        for dc in range(8):
            t = TMP[dc % 2]
            mk.stt("DVE", t, X[:, dc, :], COEFA[:, par, s, dc:dc + 1], RSTD, ALU.mult, ALU.mult)
            mk.act(H[:, dc, :], t, AF.Identity, bias=MODT[:, par, 3 * s * 8 + dc:3 * s * 8 + dc + 1], scale=1.0)

    def evac_f(Fdst, dc, th, p):
        ths = slice(th * 512, (th + 1) * 512)
        ss_flush()
        mk.act(Fdst[:, dc, ths], p[:], AF.Copy)
        ss_add(p[:], th, dc == 0, dc == 7, defer=True)

    def postnorm(L, s, F=F):
        par = L % 2
        ss_flush()
        ss_finish(RSTD2)
        for dc in range(8):
            t = TMP[dc % 2]
            mk.stt("DVE", t, F[:, dc, :], COEFG[:, par, s, dc:dc + 1], RSTD2, ALU.mult, ALU.mult)
            mk.tt("DVE", X[:, dc, :], X[:, dc, :], t, ALU.add)
            for th in range(2):
                ss_add(X[:, dc, th * 512:(th + 1) * 512], th, dc == 0, dc == 7)
        xss[0] = True

    def ffn(L, s2):
        s = 0 if s2 == 0 else 2
        adaln(L, s)
        _ps_pool[0] = list(range(7))
        for pj in range(11):
            W = ring.load(D["ffin"][LIDX[L], s2, pj])
            for c in range(2):
                for th in range(2):
                    ths = slice(th * 512, (th + 1) * 512)
                    pg, pu = ps(), ps()
                    for kc in range(8):
                        mk.mm(pg[:], W[:, kc, c * 128:(c + 1) * 128], H[:, kc, ths], start=(kc == 0), stop=(kc == 7))
                    for kc in range(8):
                        mk.mm(pu[:], W[:, kc, 256 + c * 128:256 + (c + 1) * 128], H[:, kc, ths], start=(kc == 0), stop=(kc == 7))
                    mk.act(TG[th], pg[:], AF.Silu)
                    mk.tt("DVE", ACTR[:, 2 * pj + c, ths], TG[th], pu[:], ALU.mult)
            bg_step()
        _ps_pool[0] = list(range(5))
        for dc in range(8):
            W = ring.load(D["ffout"][LIDX[L], s2, dc])
            for th in range(2):
                ths = slice(th * 512, (th + 1) * 512)
                p = ps()
                for kc in range(22):
                    mk.mm(p[:], W[:, kc, :], ACTR[:, kc, ths], start=(kc == 0), stop=(kc == 21))
                evac_f(F, dc, th, p)
            bg_step()
        postnorm(L, s)

    def gmlp(L, j):
        adaln(L, 1)
        WV = FSb.rearrange("p (a k n) -> p a k n", a=6, k=8)
        for vp in range(6):
            mk.dma("POOL", WV[:, vp], D["gin_v"][j, vp], hoist=True)
        gstop = cfg.get("gstop", 9)
        if gstop <= 1:
            return
        for up in range(6):
            W = ring.load(D["gin_u"][j, up])
            for c in range(4):
                for th in range(2):
                    ths = slice(th * 512, (th + 1) * 512)
                    p = ps()
                    for kc in range(8):
                        mk.mm(p[:], W[:, kc, c * 128:(c + 1) * 128], H[:, kc, ths], start=(kc == 0), stop=(kc == 7))
                    mk.act(ACTR[:, up * 4 + c, ths], p[:], AF.Gelu)
            bg_step()
        if ring.n % 4 == 3:
            ring.slot()
        s0 = ring.slot()
        ring.slot()
        sl = (ring.n - 2) % 4
        T1 = WR[:, sl:sl + 2, :].rearrange("p s n -> p (s n)").bitcast(F32)[:, 0:3072].rearrange("p (c i) -> p c i", i=128)
        BSB = SCR[:, 0:1024].rearrange("p (g i) -> p g i", i=128)
        WST = SCRb[:, 11264:12288].rearrange("p (g i) -> p g i", i=128)
        LNG = SMALL[:, 0:24]
        LNB = SMALL[:, 24:48]
        mk.dma("POOL", WST, D["gws"][j])
        mk.dma("SP", BSB, D["gbs"][j].rearrange("p (g i) -> p g i", i=128))
        mk.dma("SP", LNG, D["glng"][j])
        mk.dma("SP", LNB, D["glnb"][j])
        for hh in range(2):
            p = ps()
            for g4 in range(4):
                g = hh * 4 + g4
                mk.mm(p[:, g4 * 128:(g4 + 1) * 128], ONES[:], WST[:, g, :], start=True, stop=True)
            for g4 in range(4):
                g = hh * 4 + g4
                for c3 in range(3):
                    fc = g * 3 + c3
                    mk.stt("DVE", T1[:, fc, :], p[:, g4 * 128:(g4 + 1) * 128], LNB[:, fc:fc + 1], BSB[:, g, :], ALU.mult, ALU.add)
        if gstop <= 2:
            return
        VT = SCR[:, 0:3072]
        VN = SCRb[:, 6144:9216]
        SVT = SCR[:, 4608:5120].rearrange("p (k i) -> p k i", i=128)
        SVT_B = SCR[:, 5120:5632].rearrange("p (k i) -> p k i", i=128)
        ST = SMALL[:, 64:80]
        def v_mm(tc, vp, bank):
            tcs = slice(tc * 128, (tc + 1) * 128)
            for kc in range(8):
                mk.mm(bank[:], H[:, kc, tcs], WV[:, vp, kc, :], start=(kc == 0), stop=(kc == 7))

        def v_evac(vp, bank):
            mk.act(VT[:, vp * 512:(vp + 1) * 512], bank[:], AF.Gelu, accum_out=ST[:, 8 + vp:9 + vp])

        _rot = [0]

        def rot3():
            b = mk.psum_banks[4 + _rot[0] % 3]
            _rot[0] += 1
            return b

        SE = cfg.get("stat_eng", "POOL")

        def v_chunk_early(tc):
            for vp in range(4):
                v_mm(tc, vp, mk.psum_banks[vp])

        def v_chunk_late(tc):
            for vp in range(4):
                v_evac(vp, mk.psum_banks[vp])
            for vp in (4, 5):
                v_mm(tc, vp, mk.psum_banks[vp - 4])
                v_evac(vp, mk.psum_banks[vp - 4])

        def rot2():
            return rot3()

        v_chunk_early(0)
        v_chunk_late(0)
        for tc in range(8):
            tcs = slice(tc * 128, (tc + 1) * 128)
            STX = SMALL[:, 320:328]
            mk.act(STX[:, 0:6], ST[:, 8:14], AF.Copy, accum_out=ST[:, 0:1])
            mk.act(VN, VT, AF.Square, accum_out=ST[:, 2:3])
            mk.act(ST[:, 5:6], ST[:, 0:1], AF.Square, scale=1.0 / 3072.0)
            mk.act(ST[:, 5:6], ST[:, 5:6], AF.Copy, scale=-1.0)
            mk.act(ST[:, 6:7], ST[:, 2:3], AF.Identity, bias=ST[:, 5:6], scale=1.0 / 3072.0)
            mk.act(ST[:, 3:4], ST[:, 6:7], AF.Ln, bias=EPS[:], scale=1.0)
            mk.act(ST[:, 4:5], ST[:, 3:4], AF.Exp, scale=-0.5)
            mk.act(ST[:, 7:8], ST[:, 0:1], AF.Copy, scale=ST[:, 4:5])
            mk.act(ST[:, 7:8], ST[:, 7:8], AF.Copy, scale=-1.0 / 3072.0)
            mk.act(VN, VT, AF.Identity, bias=ST[:, 7:8], scale=ST[:, 4:5])
            if tc + 1 < 8:
                v_chunk_early(tc + 1)
            for f4 in range(6):
                p = rot2()
                for k in range(4):
                    fc = f4 * 4 + k
                    mk.mm(p[:, k * 128:(k + 1) * 128], VN[:, fc * 128:(fc + 1) * 128], WST[:, fc // 3, :], start=True, stop=True)
                svt = SVT if f4 % 2 == 0 else SVT_B
                for k in range(4):
                    fc = f4 * 4 + k
                    mk.stt("DVE", svt[:, k, :], p[:, k * 128:(k + 1) * 128], LNG[:, fc:fc + 1], T1[:, fc, :], ALU.mult, ALU.add)
                u = ACTR[:, f4 * 4:(f4 + 1) * 4, tcs]
                mk.tt("POOL", u, u, svt, ALU.mult)
            if tc + 1 < 8:
                v_chunk_late(tc + 1)
        if gstop <= 6:
            return
        for dc in range(8):
            W = ring.load(D["gout"][j, dc])
            for th in range(2):
                ths = slice(th * 512, (th + 1) * 512)
                p = ps()
                for kc in range(24):
                    mk.mm(p[:], W[:, kc, :], ACTR[:, kc, ths], start=(kc == 0), stop=(kc == 23))
                evac_f(F, dc, th, p)
            bg_step()
        postnorm(L, 1)

    def attn(L):
        adaln(L, 1)
        QT = ACTR[:, 0:8, :]
        KT = ACTR[:, 8:10, :]
        VTK = ACTR[:, 10:12, :].rearrange("p a (t n) -> p (a t) n", n=256)
        OT = FSb[0:64, 0:16384].rearrange("p (h n) -> p h n", n=1024)
        ROPEC = SCR[:, 0:1024]
        ROPES = SCR[:, 1024:2048]
        AMASK = SCRb[:, 4096:6144].rearrange("p (m n) -> p m n", n=128)
        CKT = SCRb[:, 6144:6656].rearrange("p (c n) -> p c n", n=256)
        CVT = SCRb[:, 6656:7168].rearrange("p (c n) -> p c n", n=256)
        QB = [SCRb[:, 7168:7680], SCRb[:, 7680:8192]]
        RT0 = [SCR[:, 4096:4608], SCR[:, 4608:5120]]
        RT1 = [SCR[:, 5120:5632], SCR[:, 5632:6144]]
        PERM = SMALL[:, 128:192].bitcast(BF16)
        ESB = SMALL[:, 192:208]
        mk.dma("SP", ROPEC, D["ropeC"])
        mk.dma("SP", ROPES, D["ropeS"])
        mk.dma("POOL", AMASK, D["amask"])
        mk.dma("POOL", CKT, D["ckT"])
        mk.dma("POOL", CVT, D["cvT"])
        mk.dma("POOL", PERM, D["perm"])
        mk.dma("SP", ESB, D["asink"])
        mk.act(ESB, ESB, AF.Exp)
        W2 = None

        _rp = []

        def rope_flush():
            while _rp:
                _rp.pop(0)()

        def rope_chunk(dst3, ci, W, c):
            for th in range(2):
                ths = slice(th * 512, (th + 1) * 512)
                p = ps()
                for kc in range(8):
                    mk.mm(p[:], W[:, kc, c * 128:(c + 1) * 128], H[:, kc, ths], start=(kc == 0), stop=(kc == 7))
                rope_flush()
                mk.act(QB[th], p[:], AF.Copy)
                mk.tt("DVE", RT0[th], p[:], ROPEC[:, ths], ALU.mult)

                def later(th=th, ths=ths, ci=ci, dst3=dst3):
                    pr = ps()
                    mk.mm(pr[:], PERM, QB[th], start=True, stop=True)
                    mk.tt("DVE", RT1[th], pr[:], ROPES[:, ths], ALU.mult)
                    mk.tt("DVE", dst3[:, ci, ths], RT0[th], RT1[th], ALU.add)
                _rp.append(later)

        for qp in range(2):
            W = ring.load(D["aqkv"][qp])
            for c in range(4):
                rope_chunk(QT, qp * 4 + c, W, c)
        W2 = ring.load(D["aqkv"][2])
        for c in range(2):
            rope_chunk(KT, c, W2, c)
        rope_flush()
        astop = cfg.get("astop", 9)
        if astop <= 1:
            return
        KVS = [FS[:, 8192:8704], FS[:, 8704:9216]]
        for tc in range(8):
            tcs = slice(tc * 128, (tc + 1) * 128)
            p = ps()
            for kc in range(8):
                mk.mm(p[:], H[:, kc, tcs], W2[:, kc, :], start=(kc == 0), stop=(kc == 7))
            kv = KVS[tc % 2]
            mk.act(kv, p[:], AF.Copy)
            mk.copy("DVE", VTK[:, tc, :], p[:, 256:512])
            mk.dma("SP", O["kout"][tc * 128:(tc + 1) * 128, :], kv[:, 0:256], is_output=True)
            mk.dma("SP", O["vout"][tc * 128:(tc + 1) * 128, :], kv[:, 256:512], is_output=True)
        if astop <= 2:
            return
        EST = FS[0:64, 9216:11264].rearrange("p (h n) -> p h n", n=128)
        mk.copy("DVE", EST, ESB[0:64, :].unsqueeze(2).to_broadcast([64, 16, 128]))
        PTS = [[SCRb[:, 8192 + s * 512:8192 + (s + 1) * 512] for s in range(5)],
               [SCRb[:, s * 512:(s + 1) * 512] for s in range(5)]]
        DEN = FS[0:64, 11264:11776]
        its = [(i, hk) for i in range(8) for hk in range(4)]
        vls = {}

        def stage_s(n):
            i, hk = its[n]
            PT = PTS[n % 2]
            half = hk % 2
            hs = slice(half * 64, half * 64 + 64)
            qc0 = 4 * (hk // 2)
            qmov = QT[hs, qc0:qc0 + 4, i * 128:(i + 1) * 128]
            vlist = []
            for s_ in range(5):
                if s_ < 3:
                    kb = min(max(i - 1 + s_, 0), 7)
                    keysT = KT[hs, hk // 2, kb * 128:(kb + 1) * 128]
                    vlist.append(VTK[:, kb, hk * 64:(hk + 1) * 64])
                else:
                    keysT = CKT[hs, hk // 2, (s_ - 3) * 128:(s_ - 2) * 128]
                    vlist.append(CVT[:, s_ - 3, hk * 64:(hk + 1) * 64])
                p = mk.psum_banks[s_]
                mk.mm(p[:].rearrange("p (h n) -> p h n", n=128), keysT, qmov, start=True, stop=True)
                if s_ < 3:
                    mk.act(PT[s_], p[:], AF.Exp, scale=0.125)
                else:
                    mk.act(PT[s_], p[:], AF.Exp, bias=FLAGS[:, 0:1], scale=0.125)
                if s_ in (0, 2):
                    m = AMASK[:, i * 2 + (0 if s_ == 0 else 1), :]
                    pv = PT[s_].rearrange("p (h n) -> p h n", n=128)
                    mk.tt("POOL", pv, pv, m.unsqueeze(1).to_broadcast([128, 4, 128]), ALU.mult)
            vls[n] = vlist

        def stage_o(n):
            i, hk = its[n]
            PT = PTS[n % 2]
            vlist = vls.pop(n)
            po, pd = mk.psum_banks[5], mk.psum_banks[6]
            for s_ in range(5):
                mk.mm(po[0:64, :], vlist[s_], PT[s_], start=(s_ == 0), stop=(s_ == 4))
            for s_ in range(5):
                mk.mm(pd[0:64, :], ONES[:, 0:64], PT[s_], start=(s_ == 0), stop=(s_ == 4))
            mk.tt("DVE", DEN, pd[0:64, :], EST[:, hk * 4:(hk + 1) * 4, :], ALU.add)
            mk.recip(DEN, DEN)
            mk.tt("DVE", OT[:, hk * 4:(hk + 1) * 4, i * 128:(i + 1) * 128], po[0:64, :].rearrange("p (h n) -> p h n", n=128),
                  DEN.rearrange("p (h n) -> p h n", n=128), ALU.mult)

        stage_s(0)
        for n in range(len(its)):
            if n + 1 < len(its):
                stage_s(n + 1)
            stage_o(n)
        FA = ACTR[:].rearrange("p c n -> p (c n)").bitcast(F32)[:, 0:8192].rearrange("p (c n) -> p c n", n=1024)
        for dc in range(8):
            W = ring.load(D["aout"][dc])
            for th in range(2):
                ths = slice(th * 512, (th + 1) * 512)
                p = ps()
                for h in range(16):
                    mk.mm(p[:], W[:, h, :], OT[:, h, ths], start=(h == 0), stop=(h == 15))
                evac_f(FA, dc, th, p)
            bg_step()
        postnorm(L, 1, FA)

    def ssm(L):
        adaln(L, 1)
        XCT = ACTR
        HB = FSb[:, 0:16384].rearrange("p (t n) -> p t n", n=2048)
        BTK = FSb[:, 16384:20480].rearrange("p (t n) -> p t n", n=512)
        HST = FS[:, 10240:12288]
        CW = SMALL[:, 0:72].rearrange("p (c k) -> p c k", k=3)
        CBI = SMALL[:, 72:96]
        W0P = SMALL[:, 96:120]
        W2P = SMALL[:, 120:144]
        DTB = SMALL[:, 144:208]
        ABC = SMALL[:, 208:272]
        SDB = SMALL[:, 272:304]
        SNORM = SMALL[:, 304:320]
        SSQ = SMALL[:, 320:328]
        mk.dma("SP", CW, D["sconvw"])
        mk.dma("SP", CBI, D["sconvb"])
        mk.dma("SP", DTB, D["sdtb"])
        mk.dma("SP", ABC, D["salog"])
        mk.dma("SP", SDB, D["sd"])
        mk.dma("SP", SNORM, D["snorm"])
        mk.act(ABC, ABC, AF.Exp)
        mk.ts("DVE", ABC, ABC, -1.0, None, ALU.mult)
        mk.ts("DVE", W0P, CW[:, :, 0], FLAGS[:, 3:4], None, ALU.mult)
        mk.ts("DVE", W2P, CW[:, :, 2], FLAGS[:, 3:4], None, ALU.mult)
        RAW = [SCR[:, 0:1026], SCR[:, 1026:2052]]
        ACC = [SCR[:, 2052:3076], SCR[:, 3076:4100]]
        for r in RAW:
            mk.memset("DVE", r[:, 0:1], 0.0)
            mk.memset("DVE", r[:, 1025:1026], 0.0)
        conv_pending = []
        for xp in range(6):
            W = ring.load(D["sin_x"][xp])
            for c in range(4):
                ch = xp * 4 + c
                raw = RAW[ch % 2]
                acc = ACC[ch % 2]
                if len(conv_pending) > 1:
                    pch, pacc = conv_pending.pop(0)
                    mk.act(XCT[:, pch, :], pacc, AF.Silu)
                for th in range(2):
                    ths = slice(th * 512, (th + 1) * 512)
                    p = ps()
                    for kc in range(8):
                        mk.mm(p[:], W[:, kc, c * 128:(c + 1) * 128], H[:, kc, ths], start=(kc == 0), stop=(kc == 7))
                    mk.act(raw[:, 1 + th * 512:1 + (th + 1) * 512], p[:], AF.Copy)
                mk.act(acc, raw[:, 1:1025], AF.Identity, bias=CBI[:, ch:ch + 1], scale=CW[:, ch, 1:2])
                mk.stt("DVE", acc, raw[:, 0:1024], CW[:, ch, 0:1], acc, ALU.mult, ALU.add)
                mk.stt("DVE", acc, raw[:, 2:1026], CW[:, ch, 2:3], acc, ALU.mult, ALU.add)
                a0 = acc[:, 256:1024:256]
                mk.stt("DVE", a0, raw[:, 256:1024:256], W0P[:, ch:ch + 1], a0, ALU.mult, ALU.add)
                a1 = acc[:, 255:1023:256]
                mk.stt("DVE", a1, raw[:, 257:1025:256], W2P[:, ch:ch + 1], a1, ALU.mult, ALU.add)
                conv_pending.append((ch, acc))
            bg_step()
        while conv_pending:
            pch, pacc = conv_pending.pop(0)
            mk.act(XCT[:, pch, :], pacc, AF.Silu)
        Wdt = ring.load(D["sin_dt"])
        DT = SCR[:, 0:512].rearrange("p (t n) -> p t n", n=64)
        DTR = SCR[:, 4100:4612].rearrange("p (t n) -> p t n", n=64)
        ABt = SCR[:, 4612:5124].rearrange("p (t n) -> p t n", n=64)
        DAH = SCRb[:, 10248:10760].rearrange("p (t n) -> p t n", n=64)
        DAL = SCRb[:, 10760:11272].rearrange("p (t n) -> p t n", n=64)
        pdt = ps()
        for tc in range(8):
            tcs = slice(tc * 128, (tc + 1) * 128)
            for kc in range(8):
                mk.mm(pdt[:, tc * 64:(tc + 1) * 64], H[:, kc, tcs], Wdt[:, kc, :], start=(kc == 0), stop=(kc == 7))
        mk.tt("DVE", DTR, pdt[:].rearrange("p (t n) -> p t n", n=64), DTB.unsqueeze(1).to_broadcast([128, 8, 64]), ALU.add)
        mk.act(ABt, DTR, AF.Abs)
        mk.act(ABt, ABt, AF.Exp, scale=-1.0)
        mk.act(ABt, ABt, AF.Ln, bias=ONE1[:], scale=1.0)
        mk.act(DTR, DTR, AF.Relu)
        mk.tt("DVE", DT, DTR, ABt, ALU.add)
        DA = DTR
        mk.tt("DVE", DA, DT, ABC.unsqueeze(1).to_broadcast([128, 8, 64]), ALU.mult)
        mk.copy("DVE", DAH, DA)
        mk.copy("DVE", DAB[:], DA)
        mk.tt("DVE", ABt, DA, DAH, ALU.subtract)
        mk.copy("DVE", DAL, ABt)
        pac, ptot = ps(), ps()
        for tc in range(8):
            for d_ in range(2):
                o = pac[:, tc * 64 + d_ * 32:tc * 64 + (d_ + 1) * 32]
                mk.mm(o, TRI[:, d_, :], DAH[:, tc, d_ * 32:(d_ + 1) * 32], start=True, stop=False)
                mk.mm(o, TRI[:, d_, :], DAL[:, tc, d_ * 32:(d_ + 1) * 32], start=False, stop=True)
            o = ptot[:, tc * 64:(tc + 1) * 64]
            mk.mm(o, ONES[:], DAH[:, tc, :], start=True, stop=False)
            mk.mm(o, ONES[:], DAL[:, tc, :], start=False, stop=True)
        EA = SCR[:, 512:1024].rearrange("p (t n) -> p t n", n=64)
        CD = SCR[:, 1024:1536].rearrange("p (t n) -> p t n", n=64)
        CO = SCR[:, 1536:2048].rearrange("p (t n) -> p t n", n=64)
        mk.copy("DVE", EA, pac[:].rearrange("p (t n) -> p t n", n=64))
        mk.copy("DVE", CD, ptot[:].rearrange("p (t n) -> p t n", n=64))
        mk.tt("DVE", CO, CD, EA, ALU.subtract)
        mk.act(CO, CO, AF.Exp)
        mk.tt("DVE", CO, CO, DT, ALU.mult)
        mk.act(EA, EA, AF.Exp)
        mk.act(CD, CD, AF.Exp)
        XTK = SCRb[:, 4096:6144]
        XDW = SCRb[:, 6144:8192]
        CBM = SCRb[:, 8192:9216].rearrange("p (d g i) -> p d g i", d=2, g=4)
        YA = SCR[:, 4608:5120]
        YB = SCR[:, 5120:5632]
        WTS = [SCRb[:, 11264:11776].rearrange("p (k i) -> p k i", i=128), LW2[:, 512:1024].rearrange("p (k i) -> p k i", i=128)]
        LBS = [SCRb[:, 11776:12288].rearrange("p (k i) -> p k i", i=128), LW2[:, 0:512].rearrange("p (k i) -> p k i", i=128)]
        Wz = [ring.load(D["sin_z"][g]) for g in range(4)]

        def bc_hp(ap2, n=8):
            return ap2.unsqueeze(2).to_broadcast([128, n, 64])

        def make_tok(tc, with_b):
            tcs = slice(tc * 128, (tc + 1) * 128)
            chans = list(range(16)) + (list(range(16, 20)) if with_b else [])
            for q in range(0, len(chans), 4):
                pb = ps()[:].bitcast(BF16)
                for k in range(4):
                    mk.transpose(pb[:, k * 128:(k + 1) * 128], XCT[:, chans[q + k], tcs], IDB[:])
                if q < 16:
                    mk.copy("ACT", XTK[:, q * 128:(q + 4) * 128], pb[:, 0:512])
                else:
                    mk.copy("ACT", BTK[:, tc, :], pb[:, 0:512])

        def state_update(tc, d_):
            mk.tt("DVE", XDW.rearrange("p (h q) -> p h q", q=64), XTK.rearrange("p (h q) -> p h q", q=64),
                  bc_hp(CO[:, tc, d_ * 32:(d_ + 1) * 32], 32), ALU.mult)
            for g in range(4):
                p = ps()
                mk.mm(p[:], BTK[:, tc, g * 128:(g + 1) * 128], XDW[:, g * 512:(g + 1) * 512], start=True, stop=True)
                hs = HST[:, g * 512:(g + 1) * 512].rearrange("p (h q) -> p h q", q=64)
                mk.tt("DVE", hs, hs, bc_hp(CD[:, tc, d_ * 32 + g * 8:d_ * 32 + (g + 1) * 8]), ALU.mult)
                mk.tt("DVE", HST[:, g * 512:(g + 1) * 512], HST[:, g * 512:(g + 1) * 512], p[:], ALU.add)

        mk.dma("SP", HST, D["h0T"][:, 1, :])
        for tc in range(7, -1, -1):
            if tc % 2 == 1 and tc < 7:
                mk.ts("DVE", HST, HST, FLAGS[:, 1:2], None, ALU.mult)
            mk.copy("ACT", HB[:, tc, :], HST)
            make_tok(tc, True)
            state_update(tc, 1)
            if tc % 2 == 0:
                mk.dma("SP", O["stT"][:, (tc // 2) * 2 + 1, :], HST, is_output=True)
        mk.dma("SP", HST, D["h0T"][:, 0, :])
        HFB = XDW
        GY = XDW
        mk.copy("ACT", HFB, HST)
        for tc in range(8):
            tcs = slice(tc * 128, (tc + 1) * 128)
            make_tok(tc, False)
            pcb = ps()
            for g in range(4):
                mk.mm(pcb[:, g * 128:(g + 1) * 128], XCT[:, 16 + g, tcs], XCT[:, 20 + g, tcs], start=True, stop=True)
            for d_ in range(2):
                mk.tt("DVE", CBM[:, d_], pcb[:].rearrange("p (g i) -> p g i", i=128),
                      TRI[:, d_, :].unsqueeze(1).to_broadcast([128, 4, 128]), ALU.mult)
            hf_cur = HFB if tc == 0 else HB[:, tc - 1, :]
            units = [(g, d_, h4) for g in range(4) for d_ in range(2) for h4 in range(2)]
            segs = {}
            pyb = {}

            def stage_a(ui):
                g, d_, h4 = units[ui]
                lbs = LBS[ui % 2]
                pseg = mk.psum_banks[2 + ui % 2]
                segs[ui] = pseg
                dh0 = d_ * 32 + g * 8 + h4 * 4
                mk.tt(LB_ENG, lbs, TRI[:, 2 + d_, :].unsqueeze(1).to_broadcast([128, 4, 128]),
                      DAB[:, tc, dh0:dh0 + 4].unsqueeze(2).to_broadcast([128, 4, 128]), ALU.mult)
                for k in range(4):
                    mk.mm(pseg[:, k * 128:(k + 1) * 128], lbs[:, k, :], TRI[:, d_, :], start=True, stop=True)

            def stage_b(ui):
                pseg = segs[ui]
                mk.act(pseg[:], pseg[:], AF.Exp)

            def stage_c(ui):
                g, d_, h4 = units[ui]
                wts = WTS[ui % 2]
                pseg = segs[ui]
                if g not in pyb:
                    pyb[g] = mk.psum_banks[g % 2]
                py = pyb[g]
                dh0 = d_ * 32 + g * 8 + h4 * 4
                p3 = pseg[:].rearrange("p (k i) -> p k i", i=128)
                mk.tt("DVE", p3, p3, CBM[:, d_, g, :].unsqueeze(1).to_broadcast([128, 4, 128]), ALU.mult)
                mk.tt("DVE", wts, p3, DT[:, tc, dh0:dh0 + 4].unsqueeze(2).to_broadcast([128, 4, 128]), ALU.mult)
                for k in range(4):
                    hh = h4 * 4 + k
                    h = g * 8 + hh
                    mk.mm(py[:, hh * 64:(hh + 1) * 64], wts[:, k, :], XTK[:, h * 64:(h + 1) * 64],
                          start=(d_ == 0 and hh == 0), stop=(d_ == 1 and hh == 7))

            pzb = {}

            pofb, pobb = {}, {}

            def z_early(g):
                pz = mk.psum_banks[5 + g % 2]
                pzb[g] = pz
                for kc in range(8):
                    mk.mm(pz[:], H[:, kc, tcs], Wz[g][:, kc, :], start=(kc == 0), stop=(kc == 7))
                mk.act(pz[:], pz[:], AF.Silu)
                pofb[g] = mk.psum_banks[4]
                pobb[g] = mk.psum_banks[6 - g % 2]
                mk.mm(pofb[g][:], XCT[:, 20 + g, tcs], hf_cur[:, g * 512:(g + 1) * 512], start=True, stop=True)
                mk.mm(pobb[g][:], XCT[:, 20 + g, tcs], HB[:, tc, g * 512:(g + 1) * 512], start=True, stop=True)

            def ycomb(g):
                py = pyb[g]
                ya3 = YA.rearrange("p (h q) -> p h q", q=64)
                yb3 = YB.rearrange("p (h q) -> p h q", q=64)
                pof, pob = pofb[g], pobb[g]
                mk.tt("DVE", ya3, pof[:].rearrange("p (h q) -> p h q", q=64), bc_hp(EA[:, tc, g * 8:(g + 1) * 8]), ALU.mult)
                mk.tt("DVE", yb3, pob[:].rearrange("p (h q) -> p h q", q=64), bc_hp(EA[:, tc, 32 + g * 8:32 + (g + 1) * 8]), ALU.mult)
                mk.tt("DVE", YA, YA, YB, ALU.add)
                mk.tt("DVE", YA, YA, py[:], ALU.add)
                mk.tt("DVE", yb3, XTK[:, g * 512:(g + 1) * 512].rearrange("p (h q) -> p h q", q=64), bc_hp(SDB[:, g * 8:(g + 1) * 8]), ALU.mult)
                mk.tt("DVE", YA, YA, YB, ALU.add)
                mk.tt("DVE", YA, YA, pzb[g][:], ALU.mult)

                def act_part(g=g):
                    mk.act(YB, YA, AF.Square, accum_out=SSQ[:, g:g + 1])
                    mk.copy("ACT", GY[:, g * 512:(g + 1) * 512], YA)
                yc_pending.append(act_part)

            _ps_pool[0] = [5, 6]
            yc_pending = []
            stage_a(0)
            stage_b(0)
            for ui in range(16):
                if ui % 4 == 0:
                    z_early(ui // 4)
                if ui + 1 < 16:
                    stage_a(ui + 1)
                    stage_b(ui + 1)
                while yc_pending:
                    yc_pending.pop(0)()
                stage_c(ui)
                if ui % 4 == 3:
                    ycomb(ui // 4)
            while yc_pending:
                yc_pending.pop(0)()
            _ps_pool[0] = list(range(5))
            mk.op("DVE", lambda e: e.reduce_sum(SSQ[:, 4:5], SSQ[:, 0:4], AX.X), [SSQ[:, 0:4]], [SSQ[:, 4:5]])
            mk.act(SSQ[:, 5:6], SSQ[:, 4:5], AF.Sqrt, bias=EPS[:], scale=1.0 / 2048.0)
            mk.recip(SSQ[:, 6:7], SSQ[:, 5:6])
            mk.ts("DVE", GY, GY, SSQ[:, 6:7], None, ALU.mult)
            for q in range(0, 16, 4):
                pb = ps()[:].bitcast(BF16)
                for k in range(4):
                    mk.transpose(pb[:, k * 128:(k + 1) * 128], GY[:, (q + k) * 128:(q + k + 1) * 128], IDB[:])
                for k in range(4):
                    mk.act(XCT[:, q + k, tcs], pb[:, k * 128:(k + 1) * 128], AF.Identity, scale=SNORM[:, q + k:q + k + 1])
            state_update(tc, 0)
            if tc % 2 == 1:
                mk.dma("SP", O["stT"][:, (tc // 2) * 2, :], HST, is_output=True)
                if tc < 7:
                    mk.ts("DVE", HST, HST, FLAGS[:, 1:2], None, ALU.mult)
            if tc < 7:
                mk.copy("ACT", HB[:, tc, :], HST)
        for dc in range(8):
            W = ring.load(D["sout"][dc])
            for th in range(2):
                ths = slice(th * 512, (th + 1) * 512)
                p = ps()
                for kc in range(16):
                    mk.mm(p[:], W[:, kc, :], XCT[:, kc, ths], start=(kc == 0), stop=(kc == 15))
                evac_f(F, dc, th, p)
            bg_step()
        postnorm(L, 1)

    bg.extend(mod_steps(plan[0][0]))
    for k, (L, which) in enumerate(plan):
        nxt = plan[k + 1][0] if k + 1 < len(plan) else None
        if nxt is not None and nxt != L and not bg and which == (0 if cfg.get("plan") else 0):
            pass
        if which == 0 or (k == 0) or plan[k - 1][0] != L:
            nl = None
            for (L2, _) in plan[k:]:
                if L2 != L:
                    nl = L2
                    break
            if nl is not None:
                bg.extend(mod_steps(nl))
        if which == 0:
            ffn(L, 0)
        elif which == 2:
            ffn(L, 1)
        else:
            kind = L % 3
            if kind == 0:
                gmlp(L, L // 3)
            elif kind == 1:
                attn(L)
            else:
                ssm(L)
        if nxt is None or nxt != L:
            while bg:
                bg_step()
    for dc in range(8):
        mk.dma("SP", O["yT"][:, dc, :], X[:, dc, :], is_output=True)
    mk.finish()
    mk.input_names = list(D.keys())
    return nc, mk

def _panelize(W, NW):
    K, N = W.shape
    return np.ascontiguousarray(W.reshape(K // 128, 128, N // NW, NW).transpose(2, 1, 0, 3))


def _fm(v):
    v = np.asarray(v)
    lead = v.shape[:-1]
    n = v.shape[-1] // 128
    r = v.reshape(lead + (n, 128))
    r = np.moveaxis(r, -1, 0)
    return np.ascontiguousarray(r.reshape(128, -1))


def _rep(v):
    v = np.asarray(v, np.float32).reshape(1, -1)
    return np.ascontiguousarray(np.broadcast_to(v, (128, v.shape[1])))


def prep_weights(inp):
    f32 = np.float32
    W = {}
    W["wmod"] = np.stack([_panelize(np.asarray(inp["w_mod"][i], f32), 512) for i in range(4)])
    fi = np.asarray(inp["ffn_in"], f32)
    perm = []
    for pj in range(11):
        for c in range(2):
            perm.extend(range((2 * pj + c) * 128, (2 * pj + c + 1) * 128))
        for c in range(2):
            perm.extend(range(D_FF + (2 * pj + c) * 128, D_FF + (2 * pj + c + 1) * 128))
    perm = np.array(perm)
    W["ffin"] = np.stack([np.stack([_panelize(fi[i, s][:, perm], 512) for s in range(2)]) for i in range(4)])
    fo = np.asarray(inp["ffn_out"], f32)
    W["ffout"] = np.stack([np.stack([_panelize(fo[i, s], 128) for s in range(2)]) for i in range(4)])
    gi = np.asarray(inp["gmlp_in"], f32)
    W["gin_u"] = np.stack([_panelize(gi[j][:, :3072], 512) for j in range(2)])
    W["gin_v"] = np.stack([_panelize(gi[j][:, 3072:], 512) for j in range(2)])
    ws = np.asarray(inp["gmlp_ws"], f32)
    W["gws"] = np.ascontiguousarray(ws.transpose(0, 3, 1, 2))
    W["gbs"] = np.stack([_rep(np.asarray(inp["gmlp_bs"], f32)[j].reshape(-1)) for j in range(2)])
    W["glng"] = np.stack([_fm(np.asarray(inp["gmlp_ln_g"], f32)[j]) for j in range(2)])
    W["glnb"] = np.stack([_fm(np.asarray(inp["gmlp_ln_b"], f32)[j]) for j in range(2)])
    W["gout"] = np.stack([_panelize(np.asarray(inp["gmlp_out"], f32)[j], 128) for j in range(2)])
    qkv = np.asarray(inp["attn_qkv"], f32)[0]
    qperm = []
    for c in range(8):
        a = 8 * (c // 4) + c % 4
        qperm.extend(range(a * 64, a * 64 + 64))
        qperm.extend(range((a + 4) * 64, (a + 4) * 64 + 64))
    qcols = qkv[:, :1024][:, np.array(qperm)]
    W["aqkv"] = np.concatenate([_panelize(qcols, 512), _panelize(qkv[:, 1024:1536], 512)], axis=0)
    W["asink"] = _rep(np.asarray(inp["attn_sink"], f32)[0])
    ao = np.asarray(inp["attn_out"], f32)[0]
    W["aout"] = np.ascontiguousarray(ao.reshape(16, 64, 8, 128).transpose(2, 1, 0, 3))
    si = np.asarray(inp["ssm_in"], f32)[0]
    W["sin_z"] = _panelize(si[:, :2048], 512)
    W["sin_x"] = _panelize(si[:, 2048:5120], 512)
    W["sin_dt"] = _panelize(si[:, 5120:5184], 64)[0]
    cw = np.asarray(inp["ssm_conv_w"], f32)[0]
    W["sconvw"] = np.ascontiguousarray(cw.reshape(3, 24, 128).transpose(2, 1, 0))
    W["sconvb"] = _fm(np.asarray(inp["ssm_conv_b"], f32)[0])
    W["sdtb"] = _rep(np.asarray(inp["ssm_dt_bias"], f32)[0].reshape(-1))
    W["salog"] = _rep(np.asarray(inp["ssm_a_log"], f32)[0].reshape(-1))
    W["sd"] = _rep(np.asarray(inp["ssm_d"], f32)[0])
    W["snorm"] = _fm(np.asarray(inp["ssm_norm"], f32)[0])
    W["sout"] = _panelize(np.asarray(inp["ssm_out"], f32)[0], 128)
    W["bmod"] = _fm(np.asarray(inp["b_mod"], f32))
    W["normg"] = _fm(np.asarray(inp["norm_g"], f32))
    W["ident"] = np.eye(128, dtype=f32)
    pm = np.zeros((128, 128), f32)
    for m in range(128):
        sub = (m % 64) % 32
        partner = m + 16 if sub < 16 else m - 16
        pm[partner, m] = 1.0
    W["perm"] = pm
    tri = np.zeros((128, 4, 128), f32)
    jj, ii = np.meshgrid(np.arange(128), np.arange(128), indexing="ij")
    tri[:, 0, :] = (jj <= ii)
    tri[:, 1, :] = (jj >= ii)
    tri[:, 2, :] = (jj > ii)
    tri[:, 3, :] = (jj < ii)
    W["tri"] = tri
    return W


def rope_tables():
    t = np.arange(1024)
    pos_r = (t // 64).astype(np.float64)
    pos_c = (t % 64).astype(np.float64)
    inv = 10000.0 ** (-np.arange(16, dtype=np.float64) / 16)
    C = np.zeros((128, 1024), np.float32)
    S = np.zeros((128, 1024), np.float32)
    for p in range(128):
        d = p % 64
        sub = d % 32
        i = sub % 16
        pos = pos_r if d < 32 else pos_c
        ang = (pos.astype(np.float32) * np.float32(inv[i])).astype(np.float32)
        C[p] = np.cos(ang)
        S[p] = (-np.sin(ang)) if sub < 16 else np.sin(ang)
    return C, S


def prep_core(inp, core):
    f32 = np.float32
    d = {}
    is_s = core >= 4
    if not is_s:
        x = np.asarray(inp["x_prompt"], f32)[4 * core:4 * core + 4].reshape(1024, 1024)
        cond = np.asarray(inp["c_ctx"], f32)
    else:
        b = core - 4
        x = np.asarray(inp["x_sample"], f32)[b]
        cond = np.asarray(inp["c"], f32)[b]
    d["xT"] = np.ascontiguousarray(x.T.reshape(8, 128, 1024).transpose(1, 0, 2))
    d["condT"] = _fm(cond)
    jj, ii = np.meshgrid(np.arange(128), np.arange(128), indexing="ij")
    am = np.zeros((128, 16, 128), f32)
    if is_s:
        b = core - 4
        ck = np.asarray(inp["cache_k"], f32)[b, 0]
        cv = np.asarray(inp["cache_v"], f32)[b, 0]
        d["ckT"] = np.ascontiguousarray(ck.reshape(256, 2, 128).transpose(2, 1, 0))
        d["cvT"] = np.ascontiguousarray(cv.reshape(2, 128, 256).transpose(1, 0, 2))
        st = np.asarray(inp["state_ssm"], f32)[b, 0]
        d["h0T"] = np.ascontiguousarray(st.reshape(2, 2048, 128).transpose(2, 0, 1))
        C, S = rope_tables()
        d["ropeC"], d["ropeS"] = C, S
        for i in range(8):
            if i >= 1:
                am[:, 2 * i, :] = (jj >= ii)
            if i <= 6:
                am[:, 2 * i + 1, :] = (jj <= ii)
        fl = np.zeros((128, 8), f32)
        fl[:, 0] = 0.0
        fl[:, 1] = 1.0
        fl[:, 2] = 0.0
        fl[:, 3] = 0.0
    else:
        d["ckT"] = np.zeros((128, 2, 256), f32)
        d["cvT"] = np.zeros((128, 2, 256), f32)
        d["h0T"] = np.zeros((128, 2, 2048), f32)
        d["ropeC"] = np.ones((128, 1024), f32)
        d["ropeS"] = np.zeros((128, 1024), f32)
        for i in range(8):
            if i % 2 == 1:
                am[:, 2 * i, :] = 1.0
            else:
                am[:, 2 * i + 1, :] = 1.0
        fl = np.zeros((128, 8), f32)
        fl[:, 0] = -30000.0
        fl[:, 1] = 0.0
        fl[:, 2] = 1.0
        fl[:, 3] = -1.0
    d["amask"] = am
    d["flags"] = fl
    return d


_CACHE = {}


def run_cores(inp, cfg=None, trace=False):
    key = repr(sorted((cfg or {}).items()))
    if key not in _CACHE:
        _CACHE[key] = build_program(cfg)
    nc, mk = _CACHE[key]
    W = prep_weights(inp)
    plan = (cfg or {}).get("plan")
    if plan:
        ul = sorted(set(L for L, _ in plan))
        for nm in ("wmod", "ffin", "ffout"):
            W[nm] = np.ascontiguousarray(W[nm][ul])
    names = set(mk.input_names)
    in_maps = []
    for core in range(8):
        m = dict(W)
        m.update(prep_core(inp, core))
        in_maps.append({k: v for k, v in m.items() if k in names})
    res = run_bass_kernel_spmd(nc, in_maps, core_ids=list(range(8)), trace=trace)
    return res


def assemble(res):
    f32 = np.float32
    yp = np.zeros((16, 256, 1024), f32)
    ys = np.zeros((4, 1024, 1024), f32)
    nk = np.zeros((16, 1, 256, 4, 64), f32)
    nv = np.zeros((16, 1, 256, 4, 64), f32)
    ns = np.zeros((16, 1, 2, 32, 64, 128), f32)
    for core in range(8):
        r = res.results[core]
        y = np.asarray(r["yT"]).transpose(1, 0, 2).reshape(1024, 1024).T
        if core < 4:
            yp[4 * core:4 * core + 4] = y.reshape(4, 256, 1024)
            nk[4 * core:4 * core + 4, 0] = np.asarray(r["kout"]).reshape(4, 256, 4, 64)
            nv[4 * core:4 * core + 4, 0] = np.asarray(r["vout"]).reshape(4, 256, 4, 64)
            st = np.asarray(r["stT"]).reshape(128, 4, 2, 2048)
            ns[4 * core:4 * core + 4, 0] = st.transpose(1, 2, 3, 0).reshape(4, 2, 32, 64, 128)
        else:
            ys[core - 4] = y
    return yp, ys, nk, nv, ns


def kernel(**inputs):
    res = run_cores(inputs, None)
    return assemble(res)
```
